# Optimizing a Trainium2 kernel written in Bass

```python
import math
import jax, jax.numpy as jnp
from jax import lax
import numpy as np

D_MODEL = 1024
BATCH = 2
SEQ = 16384
DEPTH = 1
DEC_BATCH = 4
DEC_SEQ = 8192
PAST_LEN = 128

EPS = 1e-6
MLA_HEADS = 4
QK_NOPE_DIM = 128
QK_ROPE_DIM = 64
V_HEAD_DIM = 128
Q_LORA_RANK = 384
KV_LORA_RANK = 256
ROPE_THETA = 10000.0
Q_BLOCK = 128
GDN_HEADS = 4
GDN_DK = 128
GDN_DV = 128
CONV_WIDTH = 5
CHUNK = 64
D_FF = 4 * D_MODEL
N_MOD = 6

MLA_WIDTH = MLA_HEADS * V_HEAD_DIM
GDN_WIDTH = GDN_HEADS * GDN_DV
MIX_WIDTH = MLA_WIDTH + GDN_WIDTH
GDN_CONV_CH = 2 * GDN_HEADS * GDN_DK + GDN_HEADS * GDN_DV
IN_SPLITS = (Q_LORA_RANK, KV_LORA_RANK, QK_ROPE_DIM, GDN_CONV_CH, GDN_WIDTH,
             GDN_HEADS, GDN_HEADS, GDN_HEADS, GDN_HEADS)
IN_COLS = sum(IN_SPLITS)

kernel_name = "hymba_mla_gdn_adaln_encoder"


def _split_cols(t, sizes):
    idx = np.cumsum(np.array(sizes))[:-1].tolist()
    return jnp.split(t, idx, axis=-1)


def rms_norm(x, g):
    xf = x.astype(jnp.float32)
    y = xf * lax.rsqrt(jnp.mean(xf * xf, axis=-1, keepdims=True) + EPS)
    return (y * g.astype(jnp.float32)).astype(x.dtype)


def l2_norm(x):
    xf = x.astype(jnp.float32)
    return xf * lax.rsqrt(jnp.sum(xf * xf, axis=-1, keepdims=True) + EPS)


def rotary_tables(S):
    inv = 1.0 / (ROPE_THETA ** (jnp.arange(0, QK_ROPE_DIM, 2, dtype=jnp.float32) / QK_ROPE_DIM))
    ang = jnp.arange(S, dtype=jnp.float32)[:, None] * inv[None, :]
    return jnp.cos(ang), jnp.sin(ang)


def apply_rope(x, cos, sin):
    xf = x.astype(jnp.float32)
    x1, x2 = jnp.split(xf, 2, axis=-1)
    return jnp.concatenate([x1 * cos - x2 * sin, x2 * cos + x1 * sin], axis=-1).astype(x.dtype)


def mla_mixer(cq, ckv, kr, g_q, w_uq, g_kv, w_ukv):
    B, S, _ = cq.shape
    cos, sin = rotary_tables(S)
    scale = (QK_NOPE_DIM + QK_ROPE_DIM) ** -0.5
    q = (rms_norm(cq, g_q) @ w_uq).reshape(B, S, MLA_HEADS, QK_NOPE_DIM + QK_ROPE_DIM)
    q_nope = q[..., :QK_NOPE_DIM] * scale
    q_rope = apply_rope(q[..., QK_NOPE_DIM:], cos[:, None, :], sin[:, None, :]) * scale
    kv = (rms_norm(ckv, g_kv) @ w_ukv).reshape(B, S, MLA_HEADS, QK_NOPE_DIM + V_HEAD_DIM)
    k_nope, v = kv[..., :QK_NOPE_DIM], kv[..., QK_NOPE_DIM:]
    k_rope = apply_rope(kr, cos, sin)
    nb = S // Q_BLOCK

    def blocks(t):
        return jnp.moveaxis(t.reshape(B, nb, Q_BLOCK, *t.shape[2:]), 1, 0)

    def attend(qb):
        qn, qr = qb
        s = (jnp.einsum('bqhd,bkhd->bhqk', qn, k_nope)
             + jnp.einsum('bqhr,bkr->bhqk', qr, k_rope)).astype(jnp.float32)
        p = jax.nn.softmax(s, axis=-1).astype(v.dtype)
        return jnp.einsum('bhqk,bkhd->bqhd', p, v)

    o = lax.map(attend, (blocks(q_nope), blocks(q_rope)))
    return jnp.moveaxis(o, 0, 1).reshape(B, S, MLA_WIDTH)


def gated_delta_chunked(q, k, v, g, beta):
    B, S, H, Dk = q.shape
    Dv = v.shape[-1]
    C = CHUNK
    N = S // C
    f32 = jnp.float32
    q = q.astype(f32) * (Dk ** -0.5)
    k = k.astype(f32)
    v = v.astype(f32)

    def to_chunks(t):
        return t.reshape(B, N, C, H, t.shape[-1]).transpose(0, 3, 1, 2, 4)

    q, k, v = to_chunks(q), to_chunks(k), to_chunks(v)
    g = g.astype(f32).reshape(B, N, C, H).transpose(0, 3, 1, 2)
    beta = beta.astype(f32).reshape(B, N, C, H).transpose(0, 3, 1, 2)
    g = jnp.cumsum(g, axis=-1)
    k_beta = k * beta[..., None]
    v_beta = v * beta[..., None]
    lower = jnp.tril(jnp.ones((C, C), dtype=bool))
    strict = jnp.tril(jnp.ones((C, C), dtype=bool), -1)
    diff = g[..., :, None] - g[..., None, :]
    decay = jnp.where(lower, jnp.exp(jnp.where(lower, diff, 0.0)), 0.0)
    a_mat = jnp.where(strict, jnp.einsum('bhncd,bhnmd->bhncm', k_beta, k) * decay, 0.0)
    t_mat = a_mat + jnp.eye(C, dtype=f32)
    u = lax.linalg.triangular_solve(t_mat, v_beta, left_side=True, lower=True, unit_diagonal=True)
    w = lax.linalg.triangular_solve(t_mat, k_beta * jnp.exp(g)[..., None], left_side=True,
                                    lower=True, unit_diagonal=True)
    attn = jnp.where(lower, jnp.einsum('bhncd,bhnmd->bhncm', q, k) * decay, 0.0)
    g_last = g[..., -1]
    k_tail = k * jnp.exp(g_last[..., None] - g)[..., None]
    q_dec = q * jnp.exp(g)[..., None]

    def step(state, inp):
        q_i, kt_i, u_i, w_i, at_i, gl_i = inp
        v_new = u_i - jnp.einsum('bhcd,bhde->bhce', w_i, state)
        o_i = jnp.einsum('bhcd,bhde->bhce', q_i, state) + jnp.einsum('bhcm,bhme->bhce', at_i, v_new)
        state = state * jnp.exp(gl_i)[..., None, None] + jnp.einsum('bhcd,bhce->bhde', kt_i, v_new)
        return state, o_i

    xs = tuple(jnp.moveaxis(t, 2, 0) for t in (q_dec, k_tail, u, w, attn, g_last))
    s0 = jnp.zeros((B, H, Dk, Dv), f32)
    _, o = lax.scan(step, s0, xs)
    return o.transpose(1, 0, 3, 2, 4).reshape(B, S, H, Dv)


def gdn_mixer(qkv, z, a_f, a_b, b_f, b_b, conv_w, a_log_f, a_log_b, dt_f, dt_b, g_gdn):
    B, S, _ = qkv.shape
    pad = CONV_WIDTH // 2
    qkv = lax.conv_general_dilated(qkv, conv_w[:, None, :].astype(qkv.dtype), window_strides=(1,),
                                   padding=[(pad, pad)], dimension_numbers=('NWC', 'WIO', 'NWC'),
                                   feature_group_count=GDN_CONV_CH)
    qkv = jax.nn.silu(qkv)
    q, k, v = _split_cols(qkv, (GDN_HEADS * GDN_DK, GDN_HEADS * GDN_DK, GDN_WIDTH))
    q = l2_norm(q.reshape(B, S, GDN_HEADS, GDN_DK))
    k = l2_norm(k.reshape(B, S, GDN_HEADS, GDN_DK))
    v = v.reshape(B, S, GDN_HEADS, GDN_DV)

    def log_decay(a, a_log, dt):
        return -jnp.exp(a_log.astype(jnp.float32)) * jax.nn.softplus(a.astype(jnp.float32) + dt.astype(jnp.float32))

    g_fwd, g_bwd = log_decay(a_f, a_log_f, dt_f), log_decay(a_b, a_log_b, dt_b)
    beta_fwd, beta_bwd = jax.nn.sigmoid(b_f.astype(jnp.float32)), jax.nn.sigmoid(b_b.astype(jnp.float32))
    flip = lambda t: jnp.flip(t, axis=1)
    o_fwd = gated_delta_chunked(q, k, v, g_fwd, beta_fwd)
    o_bwd = flip(gated_delta_chunked(flip(q), flip(k), flip(v), flip(g_bwd), flip(beta_bwd)))
    o = rms_norm(o_fwd + o_bwd, g_gdn) * jax.nn.silu(z.reshape(B, S, GDN_HEADS, GDN_DV).astype(jnp.float32))
    return o.reshape(B, S, GDN_WIDTH).astype(z.dtype)


def encoder_layer(x, mod, g_mix, w_in, g_q, w_uq, g_kv, w_ukv, conv_w, a_log_f, a_log_b, dt_f, dt_b,
                  g_gdn, w_out, g_mlp, w_mlp_in, w_mlp_out):
    shift_a, scale_a, gate_a, shift_m, scale_m, gate_m = jnp.split(mod[:, None, :], N_MOD, axis=-1)
    h = rms_norm(x, g_mix) * (1.0 + scale_a) + shift_a
    proj = h @ w_in
    cq, ckv, kr, qkv, z, a_f, a_b, b_f, b_b = _split_cols(proj, IN_SPLITS)
    o_mla = mla_mixer(cq, ckv, kr, g_q, w_uq, g_kv, w_ukv)
    o_gdn = gdn_mixer(qkv, z, a_f, a_b, b_f, b_b, conv_w, a_log_f, a_log_b, dt_f, dt_b, g_gdn)
    mixed = jnp.concatenate([o_mla, o_gdn], axis=-1) @ w_out
    x = x + gate_a * mixed
    h = rms_norm(x, g_mlp) * (1.0 + scale_m) + shift_m
    x = x + gate_m * (jnp.square(jax.nn.relu(h @ w_mlp_in)) @ w_mlp_out)
    return x


def run_trunk(x, c, w_ada, b_ada, g_mix, w_in, g_q, w_uq, g_kv, w_ukv, conv_w, a_log_f, a_log_b,
              dt_f, dt_b, g_gdn, w_out, g_mlp, w_mlp_in, w_mlp_out, w_ada_f, b_ada_f, g_final):
    sc = jax.nn.silu(c)
    for l in range(DEPTH):
        mod = sc @ w_ada[l] + b_ada[l]
        x = encoder_layer(x, mod, g_mix[l], w_in[l], g_q[l], w_uq[l], g_kv[l], w_ukv[l], conv_w[l],
                          a_log_f[l], a_log_b[l], dt_f[l], dt_b[l], g_gdn[l], w_out[l], g_mlp[l],
                          w_mlp_in[l], w_mlp_out[l])
    shift_f, scale_f = jnp.split((sc @ w_ada_f + b_ada_f)[:, None, :], 2, axis=-1)
    return rms_norm(x, g_final) * (1.0 + scale_f) + shift_f


def setup_inputs(seed: int = 0) -> dict:
    key = jax.random.key(seed)
    ks = jax.random.split(key, 26)
    f32 = jnp.float32
    L, D = DEPTH, D_MODEL

    def nrm(k, shape, fan_in):
        return jax.random.normal(k, shape, f32) * (fan_in ** -0.5)

    def gain(k, shape):
        return 1.0 + 0.02 * jax.random.normal(k, shape, f32)

    def dt_bias(k):
        dt = jnp.exp(jax.random.uniform(k, (L, GDN_HEADS), f32) * (math.log(0.1) - math.log(0.001)) + math.log(0.001))
        return dt + jnp.log(-jnp.expm1(-dt))

    return {
        "x_prompt": jax.random.normal(ks[0], (BATCH, SEQ, D), f32),
        "x_sample": jax.random.normal(ks[1], (DEC_BATCH, DEC_SEQ, D), f32),
        "c_prompt": jax.random.normal(ks[2], (BATCH, D), f32),
        "c_sample": jax.random.normal(ks[3], (DEC_BATCH, D), f32),
        "w_ada": nrm(ks[4], (L, D, N_MOD * D), D),
        "b_ada": 0.02 * jax.random.normal(ks[5], (L, N_MOD * D), f32),
        "g_mix": gain(ks[6], (L, D)),
        "w_in": nrm(ks[7], (L, D, IN_COLS), D),
        "g_q": gain(ks[8], (L, Q_LORA_RANK)),
        "w_uq": nrm(ks[9], (L, Q_LORA_RANK, MLA_HEADS * (QK_NOPE_DIM + QK_ROPE_DIM)), Q_LORA_RANK),
        "g_kv": gain(ks[10], (L, KV_LORA_RANK)),
        "w_ukv": nrm(ks[11], (L, KV_LORA_RANK, MLA_HEADS * (QK_NOPE_DIM + V_HEAD_DIM)), KV_LORA_RANK),
        "conv_w": nrm(ks[12], (L, CONV_WIDTH, GDN_CONV_CH), CONV_WIDTH),
        "a_log_f": jnp.log(jax.random.uniform(ks[13], (L, GDN_HEADS), f32, 1.0, 16.0)),
        "a_log_b": jnp.log(jax.random.uniform(ks[14], (L, GDN_HEADS), f32, 1.0, 16.0)),
        "dt_f": dt_bias(ks[15]),
        "dt_b": dt_bias(ks[16]),
        "g_gdn": gain(ks[17], (L, GDN_DV)),
        "w_out": nrm(ks[18], (L, MIX_WIDTH, D), MIX_WIDTH),
        "g_mlp": gain(ks[19], (L, D)),
        "w_mlp_in": nrm(ks[20], (L, D, D_FF), D),
        "w_mlp_out": nrm(ks[21], (L, D_FF, D), D_FF),
        "w_ada_f": nrm(ks[22], (D, 2 * D), D),
        "b_ada_f": 0.02 * jax.random.normal(ks[23], (2 * D,), f32),
        "g_final": gain(ks[24], (D,)),
    }


def reference(x_prompt, x_sample, c_prompt, c_sample, w_ada, b_ada, g_mix, w_in, g_q, w_uq, g_kv, w_ukv,
              conv_w, a_log_f, a_log_b, dt_f, dt_b, g_gdn, w_out, g_mlp, w_mlp_in, w_mlp_out,
              w_ada_f, b_ada_f, g_final):
    y_prompt = run_trunk(x_prompt, c_prompt, w_ada, b_ada, g_mix, w_in, g_q, w_uq, g_kv, w_ukv, conv_w,
                         a_log_f, a_log_b, dt_f, dt_b, g_gdn, w_out, g_mlp, w_mlp_in, w_mlp_out,
                         w_ada_f, b_ada_f, g_final)
    y_sample = run_trunk(x_sample, c_sample, w_ada, b_ada, g_mix, w_in, g_q, w_uq, g_kv, w_ukv, conv_w,
                         a_log_f, a_log_b, dt_f, dt_b, g_gdn, w_out, g_mlp, w_mlp_in, w_mlp_out,
                         w_ada_f, b_ada_f, g_final)
    return (y_prompt, y_sample)
```

```python
import numpy as np
import concourse.bass as bass
import concourse.mybir as mybir
from concourse.bass_utils import run_bass_kernel_spmd

F32 = mybir.dt.float32
BF16 = mybir.dt.bfloat16
AF = mybir.ActivationFunctionType
ALU = mybir.AluOpType
P = 128
TS = 512
EPS = 1e-6
D = 1024
NCOL = 2960
DFF = 4096


class V:
    __slots__ = ("ap", "key")

    def __init__(self, ap, key):
        self.ap = ap
        self.key = key

    def __getitem__(self, idx):
        return V(self.ap[idx], self.key)

    def k(self, key):
        return V(self.ap, key)


class Sched:
    ENGS = ("pe", "act", "dve", "pool", "sp")
    RING = 24

    def __init__(self, same_sync=True):
        self.ops = []
        self.lastw = {}
        self.readers = {}
        self.same_sync = same_sync
        self.last_eng = {}
        self.dma_hist = []
        self.marks = []

    def op(self, eng, fn, r=(), w=(), dma=False):
        i = len(self.ops)
        deps = set()
        for k in r:
            if k is None:
                continue
            lw = self.lastw.get(k)
            if lw is not None:
                deps.add(lw)
        for k in w:
            if k is None:
                continue
            lw = self.lastw.get(k)
            if lw is not None:
                deps.add(lw)
            for j in self.readers.get(k, {}).values():
                deps.add(j)
        import sys as _s
        f = _s._getframe(2)
        self.ops.append(dict(eng=eng, fn=fn, deps=deps, dma=dma, line=(f.f_lineno, f.f_back.f_lineno)))
        for k in r:
            if k is None:
                continue
            self.readers.setdefault(k, {})[("dma", i) if dma else eng] = i
        for k in w:
            if k is None:
                continue
            self.lastw[k] = i
            self.readers[k] = {}
        if dma:
            self.dma_hist.append(i)
        else:
            self.last_eng[eng] = i
        return i

    def barrier(self):
        self.marks.append(len(self.ops) + len(self.ENGS))
        ids = set(self.last_eng.values()) | set(self.dma_hist[-self.RING:])
        for e in self.ENGS:
            i = len(self.ops)
            self.ops.append(dict(eng=e, fn=None, deps=set(ids), dma=False))
            self.last_eng[e] = i
        self.lastw = {}
        self.readers = {}

    def emit(self, nc, stack, limit=None):
        ops = self.ops if limit is None else self.ops[:limit]
        self.ops = ops
        n = len(ops)
        needed = [False] * n
        for i, o in enumerate(ops):
            keep = set()
            for d in o["deps"]:
                od = ops[d]
                if od["fn"] is None:
                    if od["eng"] == o["eng"]:
                        continue
                    keep |= od["deps"]
                    continue
                if (not od["dma"]) and od["eng"] == o["eng"]:
                    if o["dma"]:
                        pass
                    if od["eng"] == "pe" or not self.same_sync or o["dma"] or o["fn"] is None:
                        continue
                keep.add(d)
            o["deps"] = keep
        for o in ops:
            for d in o["deps"]:
                needed[d] = True
        sems = {e: stack.enter_context(nc.semaphore("s_" + e)) for e in self.ENGS}
        ring = [stack.enter_context(nc.semaphore("r%d" % i)) for i in range(self.RING)]
        token = [None] * n
        cnt = {e: 0 for e in self.ENGS}
        ndma = 0
        ring_prev = [None] * self.RING
        for i, o in enumerate(ops):
            if o["fn"] is None:
                continue
            if o["dma"]:
                slot = ndma % self.RING
                val = 16 * (ndma // self.RING + 1)
                token[i] = (ring[slot], val, "r%d" % slot)
                if ring_prev[slot] is not None:
                    o["deps"].add(ring_prev[slot])
                ring_prev[slot] = i
                ndma += 1
            elif needed[i]:
                cnt[o["eng"]] += 1
                token[i] = (sems[o["eng"]], cnt[o["eng"]], "s_" + o["eng"])
        per_eng = {e: [] for e in self.ENGS}
        for i, o in enumerate(ops):
            per_eng[o["eng"]].append(i)
        self.stats = {e: len(per_eng[e]) for e in self.ENGS}
        self.stats["ndma"] = ndma
        block = stack.enter_context(nc.Block())

        def run(eng_name, h):
            waited = {}
            nw = 0
            for i in per_eng[eng_name]:
                o = ops[i]
                for d in sorted(o["deps"]):
                    sem, val, sname = token[d]
                    if waited.get(sname, 0) < val:
                        h.wait_ge(sem, val)
                        waited[sname] = val
                        nw += 1
                if o["fn"] is None:
                    continue
                inst = o["fn"](h)
                if o["dma"]:
                    inst.then_inc(token[i][0], 16)
                elif needed[i]:
                    inst.then_inc(token[i][0], 1)
            self.stats["w_" + eng_name] = nw

        @block.tensor
        def _(h):
            run("pe", h)

        @block.scalar
        def _(h):
            run("act", h)

        @block.vector
        def _(h):
            run("dve", h)

        @block.gpsimd
        def _(h):
            run("pool", h)

        @block.sync
        def _(h):
            run("sp", h)


class Arena:
    def __init__(self, big, nwords, base=0):
        self.big = big
        self.n = nwords
        self.off = base
        self.uid = 0
        self.log = []

    def mark(self):
        return self.off

    def reset(self, m):
        self.off = m

    def new(self, dt, shape, key=None):
        n = int(np.prod(shape))
        words = (n + 1) // 2 if dt == BF16 else n
        words = (words + 1) // 2 * 2
        a = self.off
        self.off += words
        assert self.off <= self.n, ("arena overflow", self.off, self.n)
        v = self.big[:, a:a + words]
        if dt == BF16:
            v = v.bitcast(BF16)
        v = v[:, 0:n]
        if len(shape) == 2:
            v = v.rearrange("p (a b) -> p a b", a=shape[0])
        elif len(shape) == 3:
            v = v.rearrange("p (a b c) -> p a b c", a=shape[0], b=shape[1])
        self.uid += 1
        import sys as _s
        self.log.append((a, words, dt == BF16, tuple(shape), _s._getframe(1).f_lineno, _s._getframe(2).f_lineno))
        return V(v, key if key is not None else ("t", self.uid))


SB_WORDS = 49152 - 2048


def build(NO, NW, stop_after=None):
    NT = NO + NW
    TO, TW = NO // TS, NW // TS
    TT = TO + TW
    NKT = NT // P
    CO, CW = NO // P, NW // P
    nc = bass.Bass("TRN2", target_bir_lowering=False)

    def din(name, shape, dt=F32):
        return nc.dram_tensor(name, list(shape), dt, kind="ExternalInput").ap()

    def dscr(name, shape, dt):
        return nc.dram_tensor(name, list(shape), dt, kind="Internal").ap()

    xo = din("xo", [NO, D])
    xw = din("xw", [NW, D])
    vmask = din("vmask", [P, NO])
    kbias_d = din("kbias", [P, NKT])
    cos_d = din("cos2", [P, NT])
    sin_d = din("sinS", [P, NT])
    cvec = din("cvec", [P, 8])
    w_ada = din("w_ada", [D, 6 * D])
    w_adaf = din("w_ada_f", [D, 2 * D])
    bada = din("bada", [P, 64])
    gains = din("gains", [P, 24])
    w_in = din("w_in", [D, NCOL])
    w_uq = din("w_uq", [384, 1024])
    w_ukv = din("w_ukv", [256, 1024])
    small = din("small", [P, 134])
    consts = din("consts", [P, 1024])
    w_out = din("w_out", [D, D])
    w_mi = din("w_mlp_in", [D, DFF])
    w_mo = din("w_mlp_out", [DFF, D])
    y = nc.dram_tensor("y", [NW, D], F32, kind="ExternalOutput").ap()

    QN = dscr("QN", [4, P, NW], BF16)
    QR = dscr("QR", [2, P, NW], BF16)
    KN = dscr("KN", [4, P, NT], BF16)
    KR = dscr("KR", [P, NT], BF16)
    VV = dscr("VV", [NT, 512], BF16)
    PC = dscr("PC", [12, P, NT + 4], F32)
    ZS = dscr("ZS", [4, P, NW], F32)
    ABt = dscr("ABt", [NT, 16], F32)
    GQ = dscr("GQ", [4, P, NW], BF16)
    GK = dscr("GK", [4, P, NT], BF16)
    GKt = dscr("GKt", [NT, 512], BF16)
    GVt = dscr("GVt", [NT, 512], BF16)
    OX = dscr("OX", [4, P, NW], F32)
    OM = dscr("OM", [8, P, NW], BF16)
    MODS = dscr("MODS", [4, D], F32)
    X1 = dscr("X1", [NW, D], F32)
    ACTS = dscr("ACTS", [32, P, NW], BF16)

    import contextlib
    stack = contextlib.ExitStack()
    big = stack.enter_context(nc.sbuf_tensor("big", [P, SB_WORDS], F32))
    psum = stack.enter_context(nc.psum_tensor("psum", [P, 4096], F32))
    S = Sched(same_sync=True)
    A = Arena(big, SB_WORDS)

    def bank(b, key=None):
        return V(psum[:, b * 512:(b + 1) * 512], key if key is not None else ("bank", b))

    def keys(*vs):
        return [v.key for v in vs if v is not None]

    def mm(out, lhsT, rhs, start=True, stop=True):
        S.op("pe", lambda e: e.matmul(out.ap, lhsT=lhsT.ap, rhs=rhs.ap, start=start, stop=stop),
             r=keys(lhsT, rhs), w=keys(out))

    def tr(out, in_, ident):
        S.op("pe", lambda e: e.transpose(out.ap, in_.ap, ident.ap), r=keys(in_, ident), w=keys(out))

    def act(out, in_, func, bias=None, scale=None, accum=None, extra_r=()):
        kw = {}
        rr = [in_]
        if bias is not None:
            if isinstance(bias, V):
                kw["bias"] = bias.ap
                rr.append(bias)
            else:
                kw["bias"] = float(bias)
        if scale is not None:
            if isinstance(scale, V):
                kw["scale"] = scale.ap
                rr.append(scale)
            else:
                kw["scale"] = float(scale)
        ww = [out]
        if accum is not None:
            kw["accum_out"] = accum.ap
            ww.append(accum)
        S.op("act", lambda e: e.activation(out.ap, in_.ap, func, **kw), r=keys(*rr) + list(extra_r), w=keys(*ww))

    def tt(eng, out, a, b, op):
        S.op(eng, lambda e: e.tensor_tensor(out.ap, a.ap, b.ap, op), r=keys(a, b), w=keys(out))

    def ts(eng, out, a, s1, s2, op0, op1=None):
        rr = [a]
        a1 = s1.ap if isinstance(s1, V) else float(s1)
        if isinstance(s1, V):
            rr.append(s1)
        if s2 is None:
            S.op(eng, lambda e: e.tensor_scalar(out.ap, a.ap, a1, None, op0), r=keys(*rr), w=keys(out))
        else:
            a2 = s2.ap if isinstance(s2, V) else float(s2)
            if isinstance(s2, V):
                rr.append(s2)
            S.op(eng, lambda e: e.tensor_scalar(out.ap, a.ap, a1, a2, op0, op1), r=keys(*rr), w=keys(out))

    def stt(out, a, s, b, op0, op1):
        rr = [a, b]
        a1 = s.ap if isinstance(s, V) else float(s)
        if isinstance(s, V):
            rr.append(s)
        S.op("dve", lambda e: e.scalar_tensor_tensor(out.ap, a.ap, a1, b.ap, op0, op1), r=keys(*rr), w=keys(out))

    def cp(eng, out, in_):
        if eng == "act":
            act(out, in_, AF.Copy)
        else:
            S.op(eng, lambda e: e.tensor_copy(out.ap, in_.ap), r=keys(in_), w=keys(out))

    def recip(out, in_):
        S.op("dve", lambda e: e.reciprocal(out.ap, in_.ap), r=keys(in_), w=keys(out))

    def memset(eng, out, val):
        S.op(eng, lambda e: e.memset(out.ap, val), w=keys(out))

    def dma(out, in_, slow=False):
        if slow:
            S.op("sp", lambda e: e.dma_start(out=out.ap, in_=in_.ap, allow_slow_non_contiguous=True),
                 r=keys(in_), w=keys(out), dma=True)
        else:
            S.op("sp", lambda e: e.dma_start(out=out.ap, in_=in_.ap), r=keys(in_), w=keys(out), dma=True)

    def DV(ap, key=None):
        return V(ap, key)

    cst = A.new(F32, [1024])
    dma(cst, DV(consts))
    ident32 = cst[:, 0:128]
    ones32 = cst[:, 128:256]
    LC = {"X": cst[:, 256:384], "Y": cst[:, 512:640]}
    UM = {"X": cst[:, 384:512], "Y": cst[:, 640:768]}
    BDm = cst[:, 768:896]
    NBDm = cst[:, 896:1024]
    identb = A.new(BF16, [128])
    onesb = A.new(BF16, [128])
    cp("dve", identb, ident32)
    cp("dve", onesb, ones32)
    sm = A.new(F32, [134])
    dma(sm, DV(small))
    gq_t = sm[:, 0:3]
    gkv_t = sm[:, 3:5]
    ggdn_t = sm[:, 5:6]
    dtrow4 = sm[:, 6:38].k(sm.key)
    alrow4 = sm[:, 38:70]
    convw = V(sm.ap[:, 70:130].rearrange("p (c j) -> p c j", c=12), sm.key)
    negA4 = A.new(F32, [32])
    act(negA4, alrow4, AF.Exp)
    ts("dve", negA4, negA4, -1.0, None, ALU.mult)
    gn = A.new(F32, [24])
    dma(gn, DV(gains))
    bd = A.new(F32, [64])
    dma(bd, DV(bada))
    cv = A.new(F32, [8])
    dma(cv, DV(cvec))
    sc = A.new(F32, [8])
    act(sc, cv, AF.Silu)
    modT = A.new(F32, [64])
    A1 = A.new(F32, [8])
    A2 = A.new(F32, [8])
    A3 = A.new(F32, [8])
    m0 = A.mark()
    wst = [A.new(F32, [8, 512]) for _ in range(2)]
    pmod = bank(0)
    for cg in range(16):
        src = w_ada if cg < 12 else w_adaf
        c0 = (cg if cg < 12 else cg - 12) * 512
        wt = wst[cg % 2]
        dma(wt, DV(src[:, c0:c0 + 512].rearrange("(k p) c -> p k c", p=P)))
        for jj in range(4):
            j = cg * 4 + jj
            for k in range(8):
                mm(pmod[:, j:j + 1], wt[:, k, jj * 128:(jj + 1) * 128], sc[:, k:k + 1], start=(k == 0), stop=(k == 7))
    tt("dve", modT, pmod[:, 0:64], bd, ALU.add)
    stt(A1, modT[:, 8:16], 1.0, gn[:, 0:8], ALU.add, ALU.mult)
    stt(A2, modT[:, 32:40], 1.0, gn[:, 8:16], ALU.add, ALU.mult)
    stt(A3, modT[:, 56:64], 1.0, gn[:, 16:24], ALU.add, ALU.mult)
    B1 = modT[:, 0:8]
    B2 = modT[:, 24:32]
    dma(DV(MODS[0].rearrange("(k p) -> p k", p=P), "MODS"), modT[:, 16:24], slow=True)
    dma(DV(MODS[1].rearrange("(k p) -> p k", p=P), "MODS"), modT[:, 40:48], slow=True)
    dma(DV(MODS[2].rearrange("(k p) -> p k", p=P), "MODS"), A3, slow=True)
    dma(DV(MODS[3].rearrange("(k p) -> p k", p=P), "MODS"), modT[:, 48:56], slow=True)
    A.reset(m0)
    S.barrier()
    mP = A.mark()

    win_b = A.new(BF16, [8, NCOL])
    wuq_b = A.new(BF16, [3, 1024])
    wukv_b = A.new(BF16, [2, 1024])
    m1 = A.mark()
    stg = [A.new(F32, [NCOL]) for _ in range(2)]
    for k in range(8):
        st_ = stg[k % 2]
        dma(st_, DV(w_in[k * P:(k + 1) * P, :]))
        cp("act" if k % 2 == 0 else "dve", win_b[:, k, :], st_)
    for k in range(3):
        st_ = stg[k % 2]
        dma(st_[:, 0:1024], DV(w_uq[k * P:(k + 1) * P, :]))
        ts("dve", wuq_b[:, k, :], st_[:, 0:1024], gq_t[:, k:k + 1], 192.0 ** -0.5, ALU.mult, ALU.mult)
    for k in range(2):
        st_ = stg[(k + 1) % 2]
        dma(st_[:, 0:1024], DV(w_ukv[k * P:(k + 1) * P, :]))
        ts("dve", wukv_b[:, k, :], st_[:, 0:1024], gkv_t[:, k:k + 1], None, ALU.mult)
    zt = A.new(F32, [12, 2])
    memset("dve", zt, 0.0)
    dma(DV(PC[:, :, 0:2].rearrange("c p t -> p c t")), zt)
    dma(DV(PC[:, :, NT + 2:NT + 4].rearrange("c p t -> p c t")), zt)
    S.barrier()
    A.reset(m1)

    xt = [A.new(F32, [4, D]) for _ in range(2)]
    cosT = [A.new(F32, [TS]) for _ in range(2)]
    sinT = [A.new(F32, [TS]) for _ in range(2)]
    vm = [A.new(F32, [TS]) for _ in range(2)]
    junk = A.new(BF16, [D])
    ss = [A.new(F32, [4]) for _ in range(2)]
    sd = [A.new(F32, [4]) for _ in range(2)]
    rstd = [A.new(F32, [4]) for _ in range(2)]
    xn = A.new(BF16, [4, D])
    hT = [[A.new(BF16, [TS]) for _ in range(8)] for _ in range(2)]
    pTb = [V(psum[:, 0:256].bitcast(BF16), ("bank", 0)), V(psum[:, 512:768].bitcast(BF16), ("bank", 1))]
    pj_banks = [bank(b) for b in range(2, 8)]
    pj_i = [0]

    def next_bank():
        b = pj_banks[pj_i[0] % len(pj_banks)]
        pj_i[0] += 1
        return b

    cqs = [A.new(F32, [TS]) for _ in range(3)]
    sqb = [A.new(BF16, [TS]) for _ in range(3)]
    cqn = [A.new(BF16, [TS]) for _ in range(3)]
    lnt = A.new(F32, [TS])
    rb = A.new(F32, [TS])
    outb = [A.new(BF16, [TS]) for _ in range(4)]
    outf = [A.new(F32, [TS]) for _ in range(4)]
    t1 = [A.new(F32, [TS]) for _ in range(2)]
    t2 = [A.new(F32, [TS]) for _ in range(2)]
    abt = [A.new(F32, [4, 16]) for _ in range(2)]
    abx = A.new(F32, [4, 8])
    ob_i = [0]
    of_i = [0]

    def nob():
        v = outb[ob_i[0] % 4]
        ob_i[0] += 1
        return v

    def nof():
        v = outf[of_i[0] % 4]
        of_i[0] += 1
        return v

    ev_i = [0]

    def evac_copy(out, in_):
        e = "act" if ev_i[0] % 2 == 0 else "dve"
        ev_i[0] += 1
        cp(e, out, in_)

    def load_tile(t):
        par = t % 2
        own = t >= TO
        src = xw if own else xo
        r0 = (t - TO if own else t) * TS
        dma(xt[par], DV(src[r0:r0 + TS, :].rearrange("(s p) d -> p s d", p=P)))
        dma(cosT[par], DV(cos_d[:, t * TS:(t + 1) * TS]))
        dma(sinT[par], DV(sin_d[:, t * TS:(t + 1) * TS]))
        if not own:
            dma(vm[par], DV(vmask[:, t * TS:(t + 1) * TS]))

    def norm_to_hT(xtile, A_, B_, hT_par, ss_, sd_, rstd_):
        for s in range(4):
            act(junk, xtile[:, s, :], AF.Square, accum=ss_[:, s:s + 1])
        act(sd_, ss_, AF.Sqrt, bias=eps_t, scale=1.0 / D)
        recip(rstd_, sd_)
        for s in range(4):
            ts("dve" if s % 2 == 0 else "pool", xn[:, s, :], xtile[:, s, :], rstd_[:, s:s + 1], None, ALU.mult)
        for k in range(8):
            pt = pTb[k % 2]
            for s in range(4):
                tr(pt[:, s * P:(s + 1) * P], xn[:, s, k * P:(k + 1) * P], identb)
            if k % 2 == 0:
                act(hT_par[k], pt, AF.Identity, bias=B_[:, k:k + 1], scale=A_[:, k:k + 1])
            else:
                ts("dve", hT_par[k], pt, A_[:, k:k + 1], B_[:, k:k + 1], ALU.mult, ALU.add)

    eps_t = A.new(F32, [1])
    memset("dve", eps_t, EPS)
    lnq_t = A.new(F32, [1])
    memset("dve", lnq_t, float(np.log(128.0 ** -0.5)))

    def proj(c, par, width=P):
        pb = next_bank()
        for k in range(8):
            mm(pb[0:width, :], win_b[:, k, c * P:c * P + width], hT[par][k], start=(k == 0), stop=(k == 7))
        return pb

    def rms_bc(chunks, nfeat):
        n = len(chunks)
        for k in range(n):
            tt("pool", sqb[k], chunks[k], chunks[k], ALU.mult)
        pb = next_bank()
        for k in range(n):
            mm(pb, onesb, sqb[k], start=(k == 0), stop=(k == n - 1))
        act(lnt, pb, AF.Ln, bias=eps_t, scale=1.0 / nfeat)
        act(rb, lnt, AF.Exp, scale=-0.5)

    load_tile(0)
    for t in range(TT):
        par = t % 2
        own = t >= TO
        tok0 = t * TS
        ot0 = (t - TO) * TS
        if t + 1 < TT:
            load_tile(t + 1)
        norm_to_hT(xt[par], A1, B1, hT[par], ss[par], sd[par], rstd[par])
        if own:
            for k in range(3):
                pb = proj(k, par)
                evac_copy(cqs[k], pb)
            rms_bc(cqs, 384)
            for k in range(3):
                tt("dve", cqn[k], cqs[k], rb, ALU.mult)
            for g in range(4):
                pb = next_bank()
                for k in range(3):
                    mm(pb, wuq_b[:, k, g * P:(g + 1) * P], cqn[k], start=(k == 0), stop=(k == 2))
                o = nob()
                evac_copy(o, pb)
                dma(DV(QN[g, :, ot0:ot0 + TS]), o)
            for pr in range(2):
                pb = next_bank()
                for k in range(3):
                    mm(pb, wuq_b[:, k, (4 + pr) * P:(5 + pr) * P], cqn[k], start=(k == 0), stop=(k == 2))
                tt("dve", t1[pr], pb, cosT[par], ALU.mult)
                pb2 = next_bank()
                for k in range(3):
                    mm(pb2, wuq_b[:, k, (6 + pr) * P:(7 + pr) * P], cqn[k], start=(k == 0), stop=(k == 2))
                tt("dve", t2[pr], pb2, sinT[par], ALU.mult)
                o = nob()
                tt("pool", o, t1[pr], t2[pr], ALU.add)
                dma(DV(QR[pr, :, ot0:ot0 + TS]), o)
        for k in range(2):
            pb = proj(3 + k, par)
            evac_copy(cqs[k], pb)
        rms_bc(cqs[0:2], 256)
        for k in range(2):
            tt("dve", cqn[k], cqs[k], rb, ALU.mult)
        for h in range(4):
            pb = next_bank()
            for k in range(2):
                mm(pb, wukv_b[:, k, h * P:(h + 1) * P], cqn[k], start=(k == 0), stop=(k == 1))
            o = nob()
            evac_copy(o, pb)
            dma(DV(KN[h, :, tok0:tok0 + TS]), o)
        for s in range(4):
            pb = next_bank()
            for k in range(2):
                mm(pb, cqn[k][:, s * P:(s + 1) * P], wukv_b[:, k, 512:1024], start=(k == 0), stop=(k == 1))
            o = nob()
            evac_copy(o, pb)
            dma(DV(VV[tok0 + s * P:tok0 + (s + 1) * P, :]), o)
        pb = proj(5, par)
        tt("dve", t1[0], pb, cosT[par], ALU.mult)
        pb2 = proj(6, par)
        tt("dve", t2[0], pb2, sinT[par], ALU.mult)
        o = nob()
        tt("pool", o, t1[0], t2[0], ALU.add)
        dma(DV(KR[:, tok0:tok0 + TS]), o)
        if own:
            cts = list(range(12))
        elif t == TO - 1:
            cts = list(range(12))
        else:
            cts = list(range(4, 12))
        for ct in cts:
            pb = proj(7 + ct, par)
            o = nof()
            if own:
                evac_copy(o, pb)
            else:
                tt("dve", o, pb, vm[par], ALU.mult)
            dma(DV(PC[ct, :, 2 + tok0:2 + tok0 + TS]), o)
        if own:
            for zc in range(4):
                pb = proj(19 + zc, par)
                o = nof()
                act(o, pb, AF.Silu)
                dma(DV(ZS[zc, :, ot0:ot0 + TS]), o)
        pb = next_bank()
        for s in range(4):
            for k in range(8):
                mm(pb[:, s * 16:(s + 1) * 16], hT[par][k][:, s * P:(s + 1) * P], win_b[:, k, 2944:2960],
                   start=(k == 0), stop=(k == 7))
        pv = V(pb.ap[:, 0:64].rearrange("p (s c) -> p s c", s=4), pb.key)
        ab_ = abt[par]
        dt4 = V(dtrow4.ap.rearrange("p (s c) -> p s c", s=4), dtrow4.key)
        na4 = V(negA4.ap.rearrange("p (s c) -> p s c", s=4), negA4.key)
        tt("dve", abx, pv[:, :, 0:8], dt4, ALU.add)
        act(abx, abx, AF.Exp)
        act(abx, abx, AF.Ln, bias=1.0)
        tt("dve", ab_[:, :, 0:8], abx, na4, ALU.mult)
        act(ab_[:, :, 8:16], pv[:, :, 8:16], AF.Sigmoid)
        dma(DV(ABt[tok0:tok0 + TS, :].rearrange("(s p) c -> p s c", p=P)), ab_)
    S.barrier()
    A.reset(mP)

    pcx = [A.new(F32, [TS + 4]) for _ in range(2)]
    acc = [A.new(F32, [TS]) for _ in range(2)]
    sl = [A.new(F32, [TS]) for _ in range(2)]
    slb = [A.new(BF16, [TS]) for _ in range(2)]
    sq1 = A.new(BF16, [TS])
    ln1 = A.new(F32, [TS])
    rs1 = A.new(F32, [TS])
    nb = [A.new(BF16, [TS]) for _ in range(2)]
    tok = [A.new(BF16, [4, P]) for _ in range(2)]
    vml = A.new(F32, [TS])
    if TO > 0:
        dma(vml, DV(vmask[:, (TO - 1) * TS:TO * TS]))
    eps_t = A.new(F32, [1])
    memset("dve", eps_t, EPS)
    lnq_t = A.new(F32, [1])
    memset("dve", lnq_t, float(np.log(128.0 ** -0.5)))
    zero_t = A.new(F32, [1])
    memset("dve", zero_t, 0.0)
    items = []
    for t in range(TT):
        own = t >= TO
        for ct in (range(12) if own else range(4, 12)):
            items.append((t, ct))

    def load_pc(i):
        t, ct = items[i]
        dma(pcx[i % 2], DV(PC[ct, :, t * TS:t * TS + TS + 4]))

    if items:
        load_pc(0)
    pj_i[0] = 0
    for i, (t, ct) in enumerate(items):
        par = i % 2
        if i + 1 < len(items):
            load_pc(i + 1)
        tok0 = t * TS
        ot0 = (t - TO) * TS
        px = pcx[par]
        ac = acc[par]
        ts("dve", ac, px[:, 0:TS], convw[:, ct, 0:1], None, ALU.mult)
        for j in range(1, 5):
            stt(ac, px[:, j:j + TS], convw[:, ct, j:j + 1], ac, ALU.mult, ALU.add)
        h = ct % 4
        if ct < 8:
            s_ = sl[par]
            act(s_, ac, AF.Silu)
            if t == TO - 1:
                tt("pool", s_, s_, vml, ALU.mult)
            tt("pool", sq1, s_, s_, ALU.mult)
            pb = next_bank()
            mm(pb, onesb, sq1)
            act(ln1, pb, AF.Ln, bias=eps_t, scale=1.0)
            act(rs1, ln1, AF.Exp, scale=-0.5, bias=(lnq_t if ct < 4 else zero_t))
            n_ = nb[par]
            tt("dve", n_, s_, rs1, ALU.mult)
            if ct < 4:
                dma(DV(GQ[h, :, ot0:ot0 + TS]), n_)
                continue
            dma(DV(GK[h, :, tok0:tok0 + TS]), n_)
            srcb = n_
            dst = GKt
        else:
            s_ = slb[par]
            act(s_, ac, AF.Silu)
            if t == TO - 1:
                tt("pool", s_, s_, vml, ALU.mult)
            srcb = s_
            dst = GVt
        pt = pTb[par]
        for s in range(4):
            tr(pt[:, s * P:(s + 1) * P], srcb[:, s * P:(s + 1) * P], identb)
        tk = tok[par]
        cp("act", tk, V(pt.ap.rearrange("p (s c) -> p s c", s=4), pt.key))
        dma(DV(dst[tok0:tok0 + TS, h * P:(h + 1) * P].rearrange("(s p) c -> p s c", p=P)), tk)
    S.barrier()
    A.reset(mP)

    kb_t = A.new(F32, [NKT])
    dma(kb_t, DV(kbias_d))
    kr_t = A.new(BF16, [NT])
    dma(kr_t, DV(KR))
    kn_t = A.new(BF16, [NT])
    v_t = A.new(BF16, [NKT, P])
    qn_t = [A.new(BF16, [TS]) for _ in range(2)]
    qr_t = [A.new(BF16, [TS]) for _ in range(2)]
    pT_t = [A.new(BF16, [TS]) for _ in range(3)]
    racc = A.new(F32, [TS])
    rec = A.new(F32, [TS])
    oo = [A.new(BF16, [TS]) for _ in range(2)]
    sbanks = [bank(0), bank(1), bank(2)]
    obanks = [bank(3), bank(4)]
    dbank = bank(5)
    it = 0
    for h in range(4):
        dma(kn_t, DV(KN[h]))
        dma(v_t, DV(VV[:, h * P:(h + 1) * P].rearrange("(j p) c -> p j c", p=P)))
        b0 = (h % 2) * 64
        for qt in range(TW):
            qp = it % 2
            it += 1
            dma(qn_t[qp], DV(QN[h, :, qt * TS:(qt + 1) * TS]))
            dma(qr_t[qp], DV(QR[h // 2, :, qt * TS:(qt + 1) * TS]))
            ob = obanks[qp]
            for j in range(NKT):
                sb_ = sbanks[j % 3]
                mm(sb_, kn_t[:, j * P:(j + 1) * P], qn_t[qp], start=True, stop=False)
                mm(sb_, kr_t[b0:b0 + 64, j * P:(j + 1) * P], qr_t[qp][b0:b0 + 64, :], start=False, stop=True)
                pt = pT_t[j % 3]
                act(pt, sb_, AF.Exp, bias=kb_t[:, j:j + 1])
                mm(ob, v_t[:, j, :], pt, start=(j == 0), stop=(j == NKT - 1))
                if j == 0:
                    cp("dve", racc, pt)
                else:
                    tt("dve", racc, racc, pt, ALU.add)
            mm(dbank, ones32, racc)
            recip(rec, dbank)
            o = oo[qp]
            tt("dve", o, ob, rec, ALU.mult)
            dma(DV(OM[h, :, qt * TS:(qt + 1) * TS]), o)
    S.barrier()
    A.reset(mP)

    def slot(h, i):
        b = 2 * h + i // 4
        o = (i % 4) * P
        return V(psum[:, b * 512 + o:b * 512 + o + P], ("bank", b))

    S32 = {(h, d): A.new(F32, [P]) for h in range(4) for d in "XY"}
    Sb = {(h, d): A.new(BF16, [P]) for h in range(4) for d in "XY"}
    for kk in S32:
        memset("dve", S32[kk], 0.0)
        memset("pool", Sb[kk], 0.0)
    eps_t = A.new(F32, [1])
    memset("dve", eps_t, EPS)
    NPAR = 2
    kT4 = [A.new(BF16, [4, P]) for _ in range(NPAR)]
    qT4 = [A.new(BF16, [4, P]) for _ in range(NPAR)]
    ktok = [A.new(BF16, [512]) for _ in range(NPAR)]
    vtok = [A.new(BF16, [512]) for _ in range(NPAR)]
    ab = [A.new(F32, [16]) for _ in range(NPAR)]
    oxl = [A.new(F32, [4, P]) for _ in range(NPAR)]
    zsl = [A.new(F32, [4, P]) for _ in range(NPAR)]
    egc = [A.new(F32, [4]) for _ in range(NPAR)]
    bege = [A.new(F32, [4]) for _ in range(NPAR)]
    etail = [A.new(F32, [4]) for _ in range(NPAR)]
    negb = [A.new(F32, [4]) for _ in range(NPAR)]

    def per_head(dt, n=NPAR):
        return [[A.new(dt, [P]) for _ in range(4)] for _ in range(n)]

    gUM = per_head(F32)
    gOn = per_head(F32)
    dec = per_head(F32)
    dm = per_head(F32)
    P0f = per_head(F32)
    decT = per_head(F32)
    dTm = per_head(F32)
    Er = per_head(F32)
    XT32 = per_head(F32)
    u32 = per_head(F32)
    Pf = [per_head(F32), per_head(F32)]
    PTf = [per_head(F32), per_head(F32)]
    Lm = per_head(F32)
    Dm = per_head(F32)
    Nf = per_head(F32)
    NTf = per_head(F32)
    N2f = per_head(F32)
    Y1 = per_head(F32)
    tA = per_head(F32)
    tB = per_head(F32)
    XTb = per_head(BF16)
    attnT = per_head(BF16)
    qd = per_head(BF16)
    kbg = per_head(BF16)
    vb = per_head(BF16)
    ktl = per_head(BF16)
    wTb = per_head(BF16)
    vnew = per_head(BF16)
    osum = [A.new(F32, [4, P]) for _ in range(NPAR)]
    osq = [A.new(BF16, [4, P]) for _ in range(NPAR)]
    oln = A.new(F32, [4, P])
    ors = A.new(F32, [4, P])
    on_ = A.new(F32, [4, P])
    oout = [A.new(BF16, [4, P]) for _ in range(NPAR)]

    def load_chunk(ci, par, full):
        c0 = ci * P
        dma(kT4[par], DV(GK[:, :, c0:c0 + P].rearrange("h p t -> p h t")))
        dma(ktok[par], DV(GKt[c0:c0 + P, :]))
        dma(vtok[par], DV(GVt[c0:c0 + P, :]))
        dma(ab[par], DV(ABt[c0:c0 + P, :]))
        if full:
            o0 = c0 - NO
            dma(qT4[par], DV(GQ[:, :, o0:o0 + P].rearrange("h p t -> p h t")))

    def gdn_step(ci, d, par, full, last_dir):
        goff = 0 if d == "X" else 4
        boff = 8 if d == "X" else 12
        lc, um = LC[d], UM[d]
        lastcol = P - 1 if d == "X" else 0
        o0 = ci * P - NO
        a_ = ab[par]
        g4 = a_[:, goff:goff + 4]
        b4 = a_[:, boff:boff + 4]
        gc_ps = slot(0, 6)
        gt_ps = slot(0, 7)
        mm(gc_ps[:, 0:4], lc, g4)
        mm(gt_ps[:, 0:4], um, g4)
        act(egc[par], gc_ps[:, 0:4], AF.Exp)
        act(etail[par], gt_ps[:, 0:4], AF.Exp)
        tt("dve", bege[par], b4, egc[par], ALU.mult)
        ts("pool", negb[par], b4, -1.0, None, ALU.mult)
        H = range(4)
        for h in H:
            ts("pool", gUM[par][h], um, g4[:, h:h + 1], None, ALU.mult)
            ts("pool", gOn[par][h], ones32, g4[:, h:h + 1], None, ALU.mult)
        for h in H:
            mm(slot(h, 0), lc, gUM[par][h])
            mm(slot(h, 1), gUM[par][h], lc)
            mm(slot(h, 2), gOn[par][h], lc)
            mm(slot(h, 3), kT4[par][:, h, :], kT4[par][:, h, :])
            if full:
                mm(slot(h, 4), kT4[par][:, h, :], qT4[par][:, h, :])
        for h in H:
            act(dec[par][h], slot(h, 0), AF.Exp)
            tt("pool", dm[par][h], dec[par][h], um, ALU.mult)
            stt(P0f[par][h], slot(h, 3), negb[par][:, h:h + 1], dm[par][h], ALU.mult, ALU.mult)
            tt("pool", Pf[0][par][h], P0f[par][h], BDm, ALU.mult)
            tt("pool", Lm[par][h], Pf[0][par][h], P0f[par][h], ALU.subtract)
            mm(slot(h, 5), Pf[0][par][h], ident32)
            cp("act", PTf[0][par][h], slot(h, 5))
            tt("pool", XT32[par][h], PTf[0][par][h], ident32, ALU.add)
            act(Er[par][h], slot(h, 2), AF.Exp)
            if full:
                act(decT[par][h], slot(h, 1), AF.Exp)
                tt("pool", dTm[par][h], decT[par][h], lc, ALU.mult)
                cp("act", tB[par][h], slot(h, 4))
                tt("pool", attnT[par][h], tB[par][h], dTm[par][h], ALU.mult)
                tt("pool", qd[par][h], qT4[par][:, h, :], Er[par][h], ALU.mult)
            act(kbg[par][h], ktok[par][:, h * P:(h + 1) * P], AF.Identity, scale=bege[par][:, h:h + 1])
            act(vb[par][h], vtok[par][:, h * P:(h + 1) * P], AF.Identity, scale=b4[:, h:h + 1])
            act(ktl[par][h], ktok[par][:, h * P:(h + 1) * P], AF.Identity, scale=etail[par][:, h:h + 1])
        for j in range(1, 5):
            cur, prv = j % 2, (j - 1) % 2
            for h in H:
                mm(slot(h, 0), PTf[prv][par][h], Pf[prv][par][h])
                if j < 4:
                    mm(slot(h, 1), Pf[prv][par][h], PTf[prv][par][h])
            for h in H:
                cp("act", Pf[cur][par][h], slot(h, 0))
                if j < 4:
                    cp("act", PTf[cur][par][h], slot(h, 1))
            for h in H:
                mm(slot(h, 2), Pf[cur][par][h], XT32[par][h])
            for h in H:
                cp("act", tA[par][h], slot(h, 2))
                tt("pool", XT32[par][h], XT32[par][h], tA[par][h], ALU.add)
        for h in H:
            mm(slot(h, 5), XT32[par][h], ident32)
            mm(slot(h, 0), XT32[par][h], Lm[par][h])
            mm(slot(h, 1), Lm[par][h], XT32[par][h])
        for h in H:
            cp("act", Dm[par][h], slot(h, 5))
            cp("act", Nf[par][h], slot(h, 0))
            cp("act", NTf[par][h], slot(h, 1))
            tt("pool", Y1[par][h], ident32, NTf[par][h], ALU.subtract)
        for h in H:
            mm(slot(h, 2), NTf[par][h], Nf[par][h])
        for h in H:
            cp("act", N2f[par][h], slot(h, 2))
        for h in H:
            mm(slot(h, 3), N2f[par][h], Y1[par][h])
        for h in H:
            cp("act", tA[par][h], slot(h, 3))
            tt("pool", Y1[par][h], Y1[par][h], tA[par][h], ALU.add)
        for h in H:
            mm(slot(h, 4), Dm[par][h], Y1[par][h])
        for h in H:
            cp("act", XTb[par][h], slot(h, 4))
        for h in H:
            mm(slot(h, 3), kbg[par][h], XTb[par][h])
            mm(slot(h, 4), XTb[par][h], vb[par][h])
        for h in H:
            cp("act", wTb[par][h], slot(h, 3))
            cp("act", u32[par][h], slot(h, 4))
        for h in H:
            mm(slot(h, 5), wTb[par][h], Sb[(h, d)])
        for h in H:
            cp("act", tA[par][h], slot(h, 5))
            tt("pool", vnew[par][h], u32[par][h], tA[par][h], ALU.subtract)
        for h in H:
            if full:
                mm(slot(h, 6), Sb[(h, d)], qd[par][h], start=True, stop=False)
                mm(slot(h, 6), vnew[par][h], attnT[par][h], start=False, stop=True)
            mm(slot(h, 7), ktl[par][h], vnew[par][h])
        for h in H:
            cp("act", tB[par][h], slot(h, 7))
            stt(S32[(h, d)], S32[(h, d)], Er[par][h][:, lastcol:lastcol + 1], tB[par][h], ALU.mult, ALU.add)
            cp("act", Sb[(h, d)], S32[(h, d)])
        if full:
            if not last_dir:
                for h in H:
                    cp("act", osum[par][:, h, :], slot(h, 6))
                dma(DV(OX[:, :, o0:o0 + P].rearrange("h p t -> p h t"), ("OX", ci)), osum[par])
            else:
                dma(oxl[par], DV(OX[:, :, o0:o0 + P].rearrange("h p t -> p h t"), ("OX", ci)))
                dma(zsl[par], DV(ZS[:, :, o0:o0 + P].rearrange("h p t -> p h t")))
                for h in H:
                    cp("act", osum[par][:, h, :], slot(h, 6))
                    tt("pool", osum[par][:, h, :], osum[par][:, h, :], oxl[par][:, h, :], ALU.add)
                tt("pool", osq[par], osum[par], osum[par], ALU.mult)
                nb_ = V(psum[:, 2 * 512:3 * 512], None)
                S.op("pe", lambda e, o=nb_.ap, l=onesb.ap, r_=osq[par].ap.rearrange("p h t -> p (h t)"):
                     e.matmul(o, lhsT=l, rhs=r_, start=True, stop=True),
                     r=[onesb.key, osq[par].key], w=[("bank", 2)])
                olf = V(oln.ap.rearrange("p h t -> p (h t)"), oln.key)
                S.op("act", lambda e, o=olf.ap, i_=nb_.ap, b_=eps_t.ap: e.activation(o, i_, AF.Ln, bias=b_, scale=1.0 / P),
                     r=[("bank", 2), eps_t.key], w=[oln.key])
                act(ors, oln, AF.Exp, scale=-0.5)
                tt("dve", on_, osum[par], ors, ALU.mult)
                stt(oout[par], on_, ggdn_t[:, 0:1], zsl[par], ALU.mult, ALU.mult)
                dma(DV(OM[4:8, :, o0:o0 + P].rearrange("h p t -> p h t")), oout[par])

    seq = [(ci, "X", ci >= CO, False) for ci in range(CO + CW)]
    seq += [(ci, "Y", True, True) for ci in range(CO + CW - 1, CO - 1, -1)]
    if seq:
        load_chunk(seq[0][0], 0, seq[0][2])
    for i, (ci, d, full, last_dir) in enumerate(seq):
        par = i % 2
        if i + 1 < len(seq):
            load_chunk(seq[i + 1][0], (i + 1) % 2, seq[i + 1][2])
        gdn_step(ci, d, par, full, last_dir)
    S.barrier()
    A.reset(mP)

    mP4 = A.mark()
    wo_b = A.new(BF16, [8, D])
    wmi_b = A.new(BF16, [8, DFF])
    m4 = A.mark()
    gaR = A.new(F32, [D])
    dma(gaR, DV(MODS[0].partition_broadcast(P), "MODS"))
    stg4 = [A.new(F32, [DFF]) for _ in range(2)]
    si = 0
    for k in range(8):
        st_ = stg4[si % 2]
        si += 1
        dma(st_[:, 0:D], DV(w_out[k * P:(k + 1) * P, :]))
        tt("dve", wo_b[:, k, :], st_[:, 0:D], gaR, ALU.mult)
    for k in range(8):
        st_ = stg4[si % 2]
        si += 1
        dma(st_, DV(w_mi[k * P:(k + 1) * P, :]))
        cp("act" if k % 2 == 0 else "dve", wmi_b[:, k, :], st_)
    S.barrier()
    A.reset(m4)
    eps_t = A.new(F32, [1])
    memset("dve", eps_t, EPS)
    xt4 = [A.new(F32, [4, D]) for _ in range(2)]
    om_t = [A.new(BF16, [8, TS]) for _ in range(2)]
    xn = A.new(BF16, [4, D])
    junk = A.new(BF16, [D])
    ss4 = A.new(F32, [4])
    sd4 = A.new(F32, [4])
    rstd4 = A.new(F32, [4])
    h2T = [A.new(BF16, [TS]) for _ in range(8)]
    rl = [A.new(F32, [TS]) for _ in range(2)]
    aring = [A.new(BF16, [TS]) for _ in range(4)]
    pj_i[0] = 0

    def load4(t):
        par = t % 2
        dma(xt4[par], DV(xw[t * TS:(t + 1) * TS, :].rearrange("(s p) d -> p s d", p=P)))
        dma(om_t[par], DV(OM[:, :, t * TS:(t + 1) * TS].rearrange("k p t -> p k t")))

    if TW:
        load4(0)
    for t in range(TW):
        par = t % 2
        if t + 1 < TW:
            load4(t + 1)
        x1 = xt4[par]
        for s in range(4):
            for hf in range(2):
                pb = next_bank()
                for k in range(8):
                    mm(pb, om_t[par][:, k, s * P:(s + 1) * P], wo_b[:, k, hf * 512:(hf + 1) * 512],
                       start=(k == 0), stop=(k == 7))
                tt("dve", x1[:, s, hf * 512:(hf + 1) * 512], pb, x1[:, s, hf * 512:(hf + 1) * 512], ALU.add)
        dma(DV(X1[t * TS:(t + 1) * TS, :].rearrange("(s p) d -> p s d", p=P)), x1)
        norm_to_hT(x1, A2, B2, h2T, ss4, sd4, rstd4)
        for f in range(32):
            pb = next_bank()
            for k in range(8):
                mm(pb, wmi_b[:, k, f * P:(f + 1) * P], h2T[k], start=(k == 0), stop=(k == 7))
            r_ = rl[f % 2]
            act(r_, pb, AF.Relu)
            ao = aring[f % 4]
            tt("pool" if f % 2 == 0 else "dve", ao, r_, r_, ALU.mult)
            dma(DV(ACTS[f, :, t * TS:(t + 1) * TS]), ao)
    S.barrier()
    A.reset(mP4)

    wmo_b = A.new(BF16, [32, D])
    A3R = A.new(F32, [D])
    B3R = A.new(F32, [D])
    dma(A3R, DV(MODS[2].partition_broadcast(P), "MODS"))
    dma(B3R, DV(MODS[3].partition_broadcast(P), "MODS"))
    m4 = A.mark()
    gmR = A.new(F32, [D])
    dma(gmR, DV(MODS[1].partition_broadcast(P), "MODS"))
    stg4 = [A.new(F32, [DFF]) for _ in range(2)]
    for k in range(0, 32, 4):
        st_ = stg4[(k // 4) % 2]
        st3 = V(st_.ap.rearrange("p (a c) -> p a c", a=4), st_.key)
        dma(st3, DV(w_mo[k * P:(k + 4) * P, :].rearrange("(a p) c -> p a c", p=P)))
        for a_ in range(4):
            tt("dve" if a_ % 2 == 0 else "pool", wmo_b[:, k + a_, :], st3[:, a_, :], gmR, ALU.mult)
    S.barrier()
    A.reset(m4)
    eps_t = A.new(F32, [1])
    memset("dve", eps_t, EPS)
    xb4 = [A.new(F32, [4, D]) for _ in range(2)]
    at_t = [A.new(BF16, [32, TS]) for _ in range(2)]
    junk = A.new(BF16, [D])
    ss4 = A.new(F32, [4])
    sd4 = A.new(F32, [4])
    rstd4 = A.new(F32, [4])
    pj_i[0] = 0

    def load4b(t):
        par = t % 2
        dma(xb4[par], DV(X1[t * TS:(t + 1) * TS, :].rearrange("(s p) d -> p s d", p=P)))
        dma(at_t[par], DV(ACTS[:, :, t * TS:(t + 1) * TS].rearrange("f p t -> p f t")))

    if TW:
        load4b(0)
    for t in range(TW):
        par = t % 2
        if t + 1 < TW:
            load4b(t + 1)
        x2 = xb4[par]
        for s in range(4):
            for hf in range(2):
                pb = next_bank()
                for f in range(32):
                    mm(pb, at_t[par][:, f, s * P:(s + 1) * P], wmo_b[:, f, hf * 512:(hf + 1) * 512],
                       start=(f == 0), stop=(f == 31))
                tt("dve", x2[:, s, hf * 512:(hf + 1) * 512], pb, x2[:, s, hf * 512:(hf + 1) * 512], ALU.add)
        for s in range(4):
            act(junk, x2[:, s, :], AF.Square, accum=ss4[:, s:s + 1])
        act(sd4, ss4, AF.Sqrt, bias=eps_t, scale=1.0 / D)
        recip(rstd4, sd4)
        for s in range(4):
            stt(x2[:, s, :], x2[:, s, :], rstd4[:, s:s + 1], A3R, ALU.mult, ALU.mult)
            tt("pool", x2[:, s, :], x2[:, s, :], B3R, ALU.add)
        dma(DV(y[t * TS:(t + 1) * TS, :].rearrange("(s p) d -> p s d", p=P)), x2)
    S.barrier()
    S.emit(nc, stack, None if stop_after is None else (S.marks[stop_after] if stop_after < 100 else stop_after))
    stack.close()
    S.arena_log = A.log
    return nc, S


def _arr_pk(v):
    return np.ascontiguousarray(v.reshape(-1, P).T)


def make_consts():
    i = np.arange(P)
    ident = np.eye(P, dtype=np.float32)
    ones = np.ones((P, P), np.float32)
    lcx = (i[:, None] <= i[None, :]).astype(np.float32)
    umx = (i[:, None] > i[None, :]).astype(np.float32)
    lcy = (i[:, None] >= i[None, :]).astype(np.float32)
    umy = (i[:, None] < i[None, :]).astype(np.float32)
    bd = ((i[:, None] // 32) == (i[None, :] // 32)).astype(np.float32)
    return np.ascontiguousarray(np.concatenate([ident, ones, lcx, umx, lcy, umy, bd, 1.0 - bd], axis=1))


def rope_tables(pos):
    inv = (1.0 / (np.float32(10000.0) ** (np.arange(0, 64, 2, dtype=np.float32) / np.float32(64)))).astype(np.float32)
    ang = (pos.astype(np.float32)[None, :] * inv[:, None]).astype(np.float32)
    c = np.cos(ang).astype(np.float32)
    s = np.sin(ang).astype(np.float32)
    cos2 = np.concatenate([c, c, c, c], axis=0)
    sinS = np.concatenate([-s, s, -s, s], axis=0)
    return np.ascontiguousarray(cos2), np.ascontiguousarray(sinS)


def prep_core(W, x_other, x_own, valid_other, pos, cvec, flipped, xdir):
    NO = x_other.shape[0]
    NT = NO + x_own.shape[0]
    m = {}
    m["xo"] = np.ascontiguousarray(x_other, dtype=np.float32)
    m["xw"] = np.ascontiguousarray(x_own, dtype=np.float32)
    m["vmask"] = np.ascontiguousarray(np.broadcast_to(valid_other.astype(np.float32)[None, :], (P, NO)))
    kvalid = np.concatenate([valid_other.astype(np.float32), np.ones(NT - NO, np.float32)])
    kb = np.where(kvalid > 0, 0.0, -30000.0).astype(np.float32)
    m["kbias"] = _arr_pk(kb)
    m["cos2"], m["sinS"] = rope_tables(pos)
    m["cvec"] = _arr_pk(cvec.astype(np.float32))
    m["w_ada"] = W["w_ada"]
    m["w_ada_f"] = W["w_ada_f"]
    m["bada"] = np.ascontiguousarray(np.concatenate([_arr_pk(W["b_ada"]), _arr_pk(W["b_ada_f"])], axis=1))
    m["gains"] = np.ascontiguousarray(np.concatenate([_arr_pk(W["g_mix"]), _arr_pk(W["g_mlp"]), _arr_pk(W["g_final"])], axis=1))
    wi = W["w_in"]
    cq, ckv, kr = wi[:, 0:384], wi[:, 384:640], wi[:, 640:704]
    q, k, v = wi[:, 704:1216], wi[:, 1216:1728], wi[:, 1728:2240]
    z = wi[:, 2240:2752]
    a_f, a_b, b_f, b_b = wi[:, 2752:2756], wi[:, 2756:2760], wi[:, 2760:2764], wi[:, 2764:2768]
    krs = np.concatenate([kr[:, 32:64], kr[:, 0:32]], axis=1)
    if xdir == "f":
        aX, aY, bX, bY = a_f, a_b, b_f, b_b
        alX, alY, dtX, dtY = W["a_log_f"], W["a_log_b"], W["dt_f"], W["dt_b"]
    else:
        aX, aY, bX, bY = a_b, a_f, b_b, b_f
        alX, alY, dtX, dtY = W["a_log_b"], W["a_log_f"], W["dt_b"], W["dt_f"]
    m["w_in"] = np.ascontiguousarray(np.concatenate([cq, ckv, kr, kr, krs, krs, q, k, v, z, aX, aY, bX, bY], axis=1))
    assert m["w_in"].shape[1] == NCOL
    wq = W["w_uq"]
    nope = [wq[:, h * 192:h * 192 + 128] for h in range(4)]
    rope = [wq[:, h * 192 + 128:h * 192 + 192] for h in range(4)]
    ropes = [np.concatenate([r[:, 32:64], r[:, 0:32]], axis=1) for r in rope]
    m["w_uq"] = np.ascontiguousarray(np.concatenate(nope + rope + ropes, axis=1))
    wk = W["w_ukv"]
    kn = [wk[:, h * 256:h * 256 + 128] for h in range(4)]
    vv = [wk[:, h * 256 + 128:h * 256 + 256] for h in range(4)]
    m["w_ukv"] = np.ascontiguousarray(np.concatenate(kn + vv, axis=1))
    cw = W["conv_w"]
    if flipped:
        cw = cw[::-1]
    cwa = np.transpose(cw.reshape(5, 12, P), (2, 1, 0)).reshape(P, 60)
    dtrow = np.concatenate([dtX, dtY])
    alrow = np.concatenate([alX, alY])
    sm = np.zeros((P, 134), np.float32)
    sm[:, 0:3] = _arr_pk(W["g_q"])
    sm[:, 3:5] = _arr_pk(W["g_kv"])
    sm[:, 5] = W["g_gdn"]
    sm[:, 6:38] = np.tile(dtrow, 4)[None, :]
    sm[:, 38:70] = np.tile(alrow, 4)[None, :]
    sm[:, 70:130] = cwa
    m["small"] = sm
    m["consts"] = make_consts()
    m["w_out"] = W["w_out"]
    m["w_mlp_in"] = W["w_mlp_in"]
    m["w_mlp_out"] = W["w_mlp_out"]
    return m


_WKEYS = ["w_ada", "b_ada", "g_mix", "w_in", "g_q", "w_uq", "g_kv", "w_ukv", "conv_w", "a_log_f", "a_log_b",
          "dt_f", "dt_b", "g_gdn", "w_out", "g_mlp", "w_mlp_in", "w_mlp_out"]


def _weights(inputs):
    W = {k: np.ascontiguousarray(np.asarray(inputs[k], dtype=np.float32)[0]) for k in _WKEYS}
    for k in ("w_ada_f", "b_ada_f", "g_final"):
        W[k] = np.ascontiguousarray(np.asarray(inputs[k], dtype=np.float32))
    return W


_CACHE = {}


def kernel(**inputs):
    W = _weights(inputs)
    xp = np.asarray(inputs["x_prompt"], dtype=np.float32)
    xs = np.asarray(inputs["x_sample"], dtype=np.float32)
    cp_ = np.asarray(inputs["c_prompt"], dtype=np.float32)
    cs_ = np.asarray(inputs["c_sample"], dtype=np.float32)
    B, SP_, _ = xp.shape
    BS, SS_, _ = xs.shape
    H = SP_ // 2
    assert SS_ == H and 2 * B + BS == 8
    NO = NW = H
    maps = []
    for b in range(B):
        xf = xp[b, ::-1]
        pos = np.arange(SP_ - 1, -1, -1)
        maps.append(prep_core(W, xf[:H], xf[H:], np.ones(H), pos, cp_[b], True, "b"))
        pos = np.arange(SP_)
        maps.append(prep_core(W, xp[b, :H], xp[b, H:], np.ones(H), pos, cp_[b], False, "f"))
    for b in range(BS):
        pos = np.concatenate([np.zeros(H, np.int64), np.arange(H)])
        maps.append(prep_core(W, np.zeros((H, D), np.float32), xs[b], np.zeros(H), pos, cs_[b], False, "f"))
    key = (NO, NW)
    if key not in _CACHE:
        _CACHE[key] = build(NO, NW)[0]
    nc = _CACHE[key]
    res = run_bass_kernel_spmd(nc, maps, core_ids=list(range(8)))
    yp = np.zeros_like(xp)
    ys = np.zeros_like(xs)
    for b in range(B):
        yp[b, :H] = res.results[2 * b]["y"][::-1]
        yp[b, H:] = res.results[2 * b + 1]["y"]
    for b in range(BS):
        ys[b] = res.results[2 * B + b]["y"]
    return yp, ys
```

```python
import numpy as np
import concourse.bass as bass
import concourse.mybir as mybir
from concourse.bass_utils import run_bass_kernel_spmd

F32 = mybir.dt.float32
BF16 = mybir.dt.bfloat16
AF = mybir.ActivationFunctionType
ALU = mybir.AluOpType
P = 128
TS = 512
EPS = 1e-6
D = 1024
NCOL = 2960
DFF = 4096


class V:
    __slots__ = ("ap", "key")

    def __init__(self, ap, key):
        self.ap = ap
        self.key = key

    def __getitem__(self, idx):
        return V(self.ap[idx], self.key)

    def k(self, key):
        return V(self.ap, key)


class Sched:
    ENGS = ("pe", "act", "dve", "pool", "sp")
    RING = 24

    def __init__(self, same_sync=True):
        self.ops = []
        self.lastw = {}
        self.readers = {}
        self.same_sync = same_sync
        self.last_eng = {}
        self.dma_hist = []
        self.marks = []

    def op(self, eng, fn, r=(), w=(), dma=False):
        i = len(self.ops)
        deps = set()
        for k in r:
            if k is None:
                continue
            lw = self.lastw.get(k)
            if lw is not None:
                deps.add(lw)
        for k in w:
            if k is None:
                continue
            lw = self.lastw.get(k)
            if lw is not None:
                deps.add(lw)
            for j in self.readers.get(k, {}).values():
                deps.add(j)
        import sys as _s
        f = _s._getframe(2)
        self.ops.append(dict(eng=eng, fn=fn, deps=deps, dma=dma, line=(f.f_lineno, f.f_back.f_lineno)))
        for k in r:
            if k is None:
                continue
            self.readers.setdefault(k, {})[("dma", i) if dma else eng] = i
        for k in w:
            if k is None:
                continue
            self.lastw[k] = i
            self.readers[k] = {}
        if dma:
            self.dma_hist.append(i)
        else:
            self.last_eng[eng] = i
        return i

    def barrier(self):
        self.marks.append(len(self.ops) + len(self.ENGS))
        ids = set(self.last_eng.values()) | set(self.dma_hist[-self.RING:])
        for e in self.ENGS:
            i = len(self.ops)
            self.ops.append(dict(eng=e, fn=None, deps=set(ids), dma=False))
            self.last_eng[e] = i
        self.lastw = {}
        self.readers = {}

    def emit(self, nc, stack, limit=None):
        ops = self.ops if limit is None else self.ops[:limit]
        self.ops = ops
        n = len(ops)
        needed = [False] * n
        for i, o in enumerate(ops):
            keep = set()
            for d in o["deps"]:
                od = ops[d]
                if od["fn"] is None:
                    if od["eng"] == o["eng"]:
                        continue
                    keep |= od["deps"]
                    continue
                if (not od["dma"]) and od["eng"] == o["eng"]:
                    if o["dma"]:
                        pass
                    if od["eng"] == "pe" or not self.same_sync or o["dma"] or o["fn"] is None:
                        continue
                keep.add(d)
            o["deps"] = keep
        for o in ops:
            for d in o["deps"]:
                needed[d] = True
        sems = {e: stack.enter_context(nc.semaphore("s_" + e)) for e in self.ENGS}
        ring = [stack.enter_context(nc.semaphore("r%d" % i)) for i in range(self.RING)]
        token = [None] * n
        cnt = {e: 0 for e in self.ENGS}
        ndma = 0
        ring_prev = [None] * self.RING
        for i, o in enumerate(ops):
            if o["fn"] is None:
                continue
            if o["dma"]:
                slot = ndma % self.RING
                val = 16 * (ndma // self.RING + 1)
                token[i] = (ring[slot], val, "r%d" % slot)
                if ring_prev[slot] is not None:
                    o["deps"].add(ring_prev[slot])
                ring_prev[slot] = i
                ndma += 1
            elif needed[i]:
                cnt[o["eng"]] += 1
                token[i] = (sems[o["eng"]], cnt[o["eng"]], "s_" + o["eng"])
        per_eng = {e: [] for e in self.ENGS}
        for i, o in enumerate(ops):
            per_eng[o["eng"]].append(i)
        self.stats = {e: len(per_eng[e]) for e in self.ENGS}
        self.stats["ndma"] = ndma
        block = stack.enter_context(nc.Block())

        def run(eng_name, h):
            waited = {}
            nw = 0
            for i in per_eng[eng_name]:
                o = ops[i]
                for d in sorted(o["deps"]):
                    sem, val, sname = token[d]
                    if waited.get(sname, 0) < val:
                        h.wait_ge(sem, val)
                        waited[sname] = val
                        nw += 1
                if o["fn"] is None:
                    continue
                inst = o["fn"](h)
                if o["dma"]:
                    inst.then_inc(token[i][0], 16)
                elif needed[i]:
                    inst.then_inc(token[i][0], 1)
            self.stats["w_" + eng_name] = nw

        @block.tensor
        def _(h):
            run("pe", h)

        @block.scalar
        def _(h):
            run("act", h)

        @block.vector
        def _(h):
            run("dve", h)

        @block.gpsimd
        def _(h):
            run("pool", h)

        @block.sync
        def _(h):
            run("sp", h)


class Arena:
    def __init__(self, big, nwords, base=0):
        self.big = big
        self.n = nwords
        self.off = base
        self.uid = 0
        self.log = []

    def mark(self):
        return self.off

    def reset(self, m):
        self.off = m

    def new(self, dt, shape, key=None):
        n = int(np.prod(shape))
        words = (n + 1) // 2 if dt == BF16 else n
        words = (words + 1) // 2 * 2
        a = self.off
        self.off += words
        assert self.off <= self.n, ("arena overflow", self.off, self.n)
        v = self.big[:, a:a + words]
        if dt == BF16:
            v = v.bitcast(BF16)
        v = v[:, 0:n]
        if len(shape) == 2:
            v = v.rearrange("p (a b) -> p a b", a=shape[0])
        elif len(shape) == 3:
            v = v.rearrange("p (a b c) -> p a b c", a=shape[0], b=shape[1])
        self.uid += 1
        import sys as _s
        self.log.append((a, words, dt == BF16, tuple(shape), _s._getframe(1).f_lineno, _s._getframe(2).f_lineno))
        return V(v, key if key is not None else ("t", self.uid))


SB_WORDS = 49152 - 2048


def build(NO, NW, stop_after=None):
    NT = NO + NW
    TO, TW = NO // TS, NW // TS
    TT = TO + TW
    NKT = NT // P
    CO, CW = NO // P, NW // P
    nc = bass.Bass("TRN2", target_bir_lowering=False)

    def din(name, shape, dt=F32):
        return nc.dram_tensor(name, list(shape), dt, kind="ExternalInput").ap()

    def dscr(name, shape, dt):
        return nc.dram_tensor(name, list(shape), dt, kind="Internal").ap()

    xo = din("xo", [NO, D])
    xw = din("xw", [NW, D])
    vmask = din("vmask", [P, NO])
    kbias_d = din("kbias", [P, NKT])
    cos_d = din("cos2", [P, NT])
    sin_d = din("sinS", [P, NT])
    cvec = din("cvec", [P, 8])
    w_ada = din("w_ada", [D, 6 * D])
    w_adaf = din("w_ada_f", [D, 2 * D])
    bada = din("bada", [P, 64])
    gains = din("gains", [P, 24])
    w_in = din("w_in", [D, NCOL])
    w_uq = din("w_uq", [384, 1024])
    w_ukv = din("w_ukv", [256, 1024])
    small = din("small", [P, 134])
    consts = din("consts", [P, 1024])
    w_out = din("w_out", [D, D])
    w_mi = din("w_mlp_in", [D, DFF])
    w_mo = din("w_mlp_out", [DFF, D])
    y = nc.dram_tensor("y", [NW, D], F32, kind="ExternalOutput").ap()

    QN = dscr("QN", [4, P, NW], BF16)
    QR = dscr("QR", [2, P, NW], BF16)
    KN = dscr("KN", [4, P, NT], BF16)
    KR = dscr("KR", [P, NT], BF16)
    VV = dscr("VV", [NT, 512], BF16)
    PC = dscr("PC", [12, P, NT + 4], F32)
    ZS = dscr("ZS", [4, P, NW], F32)
    ABt = dscr("ABt", [NT, 16], F32)
    GQ = dscr("GQ", [4, P, NW], BF16)
    GK = dscr("GK", [4, P, NT], BF16)
    GKt = dscr("GKt", [NT, 512], BF16)
    GVt = dscr("GVt", [NT, 512], BF16)
    OX = dscr("OX", [4, P, NW], F32)
    OM = dscr("OM", [8, P, NW], BF16)
    MODS = dscr("MODS", [4, D], F32)
    X1 = dscr("X1", [NW, D], F32)
    ACTS = dscr("ACTS", [32, P, NW], BF16)

    import contextlib
    stack = contextlib.ExitStack()
    big = stack.enter_context(nc.sbuf_tensor("big", [P, SB_WORDS], F32))
    psum = stack.enter_context(nc.psum_tensor("psum", [P, 4096], F32))
    S = Sched(same_sync=True)
    A = Arena(big, SB_WORDS)

    def bank(b, key=None):
        return V(psum[:, b * 512:(b + 1) * 512], key if key is not None else ("bank", b))

    def keys(*vs):
        return [v.key for v in vs if v is not None]

    def mm(out, lhsT, rhs, start=True, stop=True):
        S.op("pe", lambda e: e.matmul(out.ap, lhsT=lhsT.ap, rhs=rhs.ap, start=start, stop=stop),
             r=keys(lhsT, rhs), w=keys(out))

    def tr(out, in_, ident):
        S.op("pe", lambda e: e.transpose(out.ap, in_.ap, ident.ap), r=keys(in_, ident), w=keys(out))

    def act(out, in_, func, bias=None, scale=None, accum=None, extra_r=()):
        kw = {}
        rr = [in_]
        if bias is not None:
            if isinstance(bias, V):
                kw["bias"] = bias.ap
                rr.append(bias)
            else:
                kw["bias"] = float(bias)
        if scale is not None:
            if isinstance(scale, V):
                kw["scale"] = scale.ap
                rr.append(scale)
            else:
                kw["scale"] = float(scale)
        ww = [out]
        if accum is not None:
            kw["accum_out"] = accum.ap
            ww.append(accum)
        S.op("act", lambda e: e.activation(out.ap, in_.ap, func, **kw), r=keys(*rr) + list(extra_r), w=keys(*ww))

    def tt(eng, out, a, b, op):
        S.op(eng, lambda e: e.tensor_tensor(out.ap, a.ap, b.ap, op), r=keys(a, b), w=keys(out))

    def ts(eng, out, a, s1, s2, op0, op1=None):
        rr = [a]
        a1 = s1.ap if isinstance(s1, V) else float(s1)
        if isinstance(s1, V):
            rr.append(s1)
        if s2 is None:
            S.op(eng, lambda e: e.tensor_scalar(out.ap, a.ap, a1, None, op0), r=keys(*rr), w=keys(out))
        else:
            a2 = s2.ap if isinstance(s2, V) else float(s2)
            if isinstance(s2, V):
                rr.append(s2)
            S.op(eng, lambda e: e.tensor_scalar(out.ap, a.ap, a1, a2, op0, op1), r=keys(*rr), w=keys(out))

    def stt(out, a, s, b, op0, op1):
        rr = [a, b]
        a1 = s.ap if isinstance(s, V) else float(s)
        if isinstance(s, V):
            rr.append(s)
        S.op("dve", lambda e: e.scalar_tensor_tensor(out.ap, a.ap, a1, b.ap, op0, op1), r=keys(*rr), w=keys(out))

    def cp(eng, out, in_):
        if eng == "act":
            act(out, in_, AF.Copy)
        else:
            S.op(eng, lambda e: e.tensor_copy(out.ap, in_.ap), r=keys(in_), w=keys(out))

    def recip(out, in_):
        S.op("dve", lambda e: e.reciprocal(out.ap, in_.ap), r=keys(in_), w=keys(out))

    def memset(eng, out, val):
        S.op(eng, lambda e: e.memset(out.ap, val), w=keys(out))

    def dma(out, in_, slow=False):
        if slow:
            S.op("sp", lambda e: e.dma_start(out=out.ap, in_=in_.ap, allow_slow_non_contiguous=True),
                 r=keys(in_), w=keys(out), dma=True)
        else:
            S.op("sp", lambda e: e.dma_start(out=out.ap, in_=in_.ap), r=keys(in_), w=keys(out), dma=True)

    def DV(ap, key=None):
        return V(ap, key)

    cst = A.new(F32, [1024])
    dma(cst, DV(consts))
    ident32 = cst[:, 0:128]
    ones32 = cst[:, 128:256]
    LC = {"X": cst[:, 256:384], "Y": cst[:, 512:640]}
    UM = {"X": cst[:, 384:512], "Y": cst[:, 640:768]}
    BDm = cst[:, 768:896]
    NBDm = cst[:, 896:1024]
    identb = A.new(BF16, [128])
    onesb = A.new(BF16, [128])
    cp("dve", identb, ident32)
    cp("dve", onesb, ones32)
    sm = A.new(F32, [134])
    dma(sm, DV(small))
    gq_t = sm[:, 0:3]
    gkv_t = sm[:, 3:5]
    ggdn_t = sm[:, 5:6]
    dtrow4 = sm[:, 6:38].k(sm.key)
    alrow4 = sm[:, 38:70]
    convw = V(sm.ap[:, 70:130].rearrange("p (c j) -> p c j", c=12), sm.key)
    negA4 = A.new(F32, [32])
    act(negA4, alrow4, AF.Exp)
    ts("dve", negA4, negA4, -1.0, None, ALU.mult)
    gn = A.new(F32, [24])
    dma(gn, DV(gains))
    bd = A.new(F32, [64])
    dma(bd, DV(bada))
    cv = A.new(F32, [8])
    dma(cv, DV(cvec))
    sc = A.new(F32, [8])
    act(sc, cv, AF.Silu)
    modT = A.new(F32, [64])
    A1 = A.new(F32, [8])
    A2 = A.new(F32, [8])
    A3 = A.new(F32, [8])
    m0 = A.mark()
    wst = [A.new(F32, [8, 512]) for _ in range(2)]
    pmod = bank(0)
    for cg in range(16):
        src = w_ada if cg < 12 else w_adaf
        c0 = (cg if cg < 12 else cg - 12) * 512
        wt = wst[cg % 2]
        dma(wt, DV(src[:, c0:c0 + 512].rearrange("(k p) c -> p k c", p=P)))
        for jj in range(4):
            j = cg * 4 + jj
            for k in range(8):
                mm(pmod[:, j:j + 1], wt[:, k, jj * 128:(jj + 1) * 128], sc[:, k:k + 1], start=(k == 0), stop=(k == 7))
    tt("dve", modT, pmod[:, 0:64], bd, ALU.add)
    stt(A1, modT[:, 8:16], 1.0, gn[:, 0:8], ALU.add, ALU.mult)
    stt(A2, modT[:, 32:40], 1.0, gn[:, 8:16], ALU.add, ALU.mult)
    stt(A3, modT[:, 56:64], 1.0, gn[:, 16:24], ALU.add, ALU.mult)
    B1 = modT[:, 0:8]
    B2 = modT[:, 24:32]
    dma(DV(MODS[0].rearrange("(k p) -> p k", p=P), "MODS"), modT[:, 16:24], slow=True)
    dma(DV(MODS[1].rearrange("(k p) -> p k", p=P), "MODS"), modT[:, 40:48], slow=True)
    dma(DV(MODS[2].rearrange("(k p) -> p k", p=P), "MODS"), A3, slow=True)
    dma(DV(MODS[3].rearrange("(k p) -> p k", p=P), "MODS"), modT[:, 48:56], slow=True)
    A.reset(m0)
    S.barrier()
    mP = A.mark()

    win_b = A.new(BF16, [8, NCOL])
    wuq_b = A.new(BF16, [3, 1024])
    wukv_b = A.new(BF16, [2, 1024])
    m1 = A.mark()
    stg = [A.new(F32, [NCOL]) for _ in range(2)]
    for k in range(8):
        st_ = stg[k % 2]
        dma(st_, DV(w_in[k * P:(k + 1) * P, :]))
        cp("act" if k % 2 == 0 else "dve", win_b[:, k, :], st_)
    for k in range(3):
        st_ = stg[k % 2]
        dma(st_[:, 0:1024], DV(w_uq[k * P:(k + 1) * P, :]))
        ts("dve", wuq_b[:, k, :], st_[:, 0:1024], gq_t[:, k:k + 1], 192.0 ** -0.5, ALU.mult, ALU.mult)
    for k in range(2):
        st_ = stg[(k + 1) % 2]
        dma(st_[:, 0:1024], DV(w_ukv[k * P:(k + 1) * P, :]))
        ts("dve", wukv_b[:, k, :], st_[:, 0:1024], gkv_t[:, k:k + 1], None, ALU.mult)
    zt = A.new(F32, [12, 2])
    memset("dve", zt, 0.0)
    dma(DV(PC[:, :, 0:2].rearrange("c p t -> p c t")), zt)
    dma(DV(PC[:, :, NT + 2:NT + 4].rearrange("c p t -> p c t")), zt)
    S.barrier()
    A.reset(m1)

    xt = [A.new(F32, [4, D]) for _ in range(2)]
    cosT = [A.new(F32, [TS]) for _ in range(2)]
    sinT = [A.new(F32, [TS]) for _ in range(2)]
    vm = [A.new(F32, [TS]) for _ in range(2)]
    junk = A.new(BF16, [D])
    ss = [A.new(F32, [4]) for _ in range(2)]
    sd = [A.new(F32, [4]) for _ in range(2)]
    rstd = [A.new(F32, [4]) for _ in range(2)]
    xn = A.new(BF16, [4, D])
    hT = [[A.new(BF16, [TS]) for _ in range(8)] for _ in range(2)]
    pTb = [V(psum[:, 0:256].bitcast(BF16), ("bank", 0)), V(psum[:, 512:768].bitcast(BF16), ("bank", 1))]
    pj_banks = [bank(b) for b in range(2, 8)]
    pj_i = [0]

    def next_bank():
        b = pj_banks[pj_i[0] % len(pj_banks)]
        pj_i[0] += 1
        return b

    cqs = [A.new(F32, [TS]) for _ in range(3)]
    sqb = [A.new(BF16, [TS]) for _ in range(3)]
    cqn = [A.new(BF16, [TS]) for _ in range(3)]
    lnt = A.new(F32, [TS])
    rb = A.new(F32, [TS])
    outb = [A.new(BF16, [TS]) for _ in range(4)]
    outf = [A.new(F32, [TS]) for _ in range(4)]
    t1 = [A.new(F32, [TS]) for _ in range(2)]
    t2 = [A.new(F32, [TS]) for _ in range(2)]
    abt = [A.new(F32, [4, 16]) for _ in range(2)]
    abx = A.new(F32, [4, 8])
    ob_i = [0]
    of_i = [0]

    def nob():
        v = outb[ob_i[0] % 4]
        ob_i[0] += 1
        return v

    def nof():
        v = outf[of_i[0] % 4]
        of_i[0] += 1
        return v

    ev_i = [0]

    def evac_copy(out, in_):
        e = "act" if ev_i[0] % 2 == 0 else "dve"
        ev_i[0] += 1
        cp(e, out, in_)

    def load_tile(t):
        par = t % 2
        own = t >= TO
        src = xw if own else xo
        r0 = (t - TO if own else t) * TS
        dma(xt[par], DV(src[r0:r0 + TS, :].rearrange("(s p) d -> p s d", p=P)))
        dma(cosT[par], DV(cos_d[:, t * TS:(t + 1) * TS]))
        dma(sinT[par], DV(sin_d[:, t * TS:(t + 1) * TS]))
        if not own:
            dma(vm[par], DV(vmask[:, t * TS:(t + 1) * TS]))

    def norm_to_hT(xtile, A_, B_, hT_par, ss_, sd_, rstd_):
        for s in range(4):
            act(junk, xtile[:, s, :], AF.Square, accum=ss_[:, s:s + 1])
        act(sd_, ss_, AF.Sqrt, bias=eps_t, scale=1.0 / D)
        recip(rstd_, sd_)
        for s in range(4):
            ts("dve" if s % 2 == 0 else "pool", xn[:, s, :], xtile[:, s, :], rstd_[:, s:s + 1], None, ALU.mult)
        for k in range(8):
            pt = pTb[k % 2]
            for s in range(4):
                tr(pt[:, s * P:(s + 1) * P], xn[:, s, k * P:(k + 1) * P], identb)
            if k % 2 == 0:
                act(hT_par[k], pt, AF.Identity, bias=B_[:, k:k + 1], scale=A_[:, k:k + 1])
            else:
                ts("dve", hT_par[k], pt, A_[:, k:k + 1], B_[:, k:k + 1], ALU.mult, ALU.add)

    eps_t = A.new(F32, [1])
    memset("dve", eps_t, EPS)
    lnq_t = A.new(F32, [1])
    memset("dve", lnq_t, float(np.log(128.0 ** -0.5)))

    def proj(c, par, width=P):
        pb = next_bank()
        for k in range(8):
            mm(pb[0:width, :], win_b[:, k, c * P:c * P + width], hT[par][k], start=(k == 0), stop=(k == 7))
        return pb

    def rms_bc(chunks, nfeat):
        n = len(chunks)
        for k in range(n):
            tt("pool", sqb[k], chunks[k], chunks[k], ALU.mult)
        pb = next_bank()
        for k in range(n):
            mm(pb, onesb, sqb[k], start=(k == 0), stop=(k == n - 1))
        act(lnt, pb, AF.Ln, bias=eps_t, scale=1.0 / nfeat)
        act(rb, lnt, AF.Exp, scale=-0.5)

    load_tile(0)
    for t in range(TT):
        par = t % 2
        own = t >= TO
        tok0 = t * TS
        ot0 = (t - TO) * TS
        if t + 1 < TT:
            load_tile(t + 1)
        norm_to_hT(xt[par], A1, B1, hT[par], ss[par], sd[par], rstd[par])
        if own:
            for k in range(3):
                pb = proj(k, par)
                evac_copy(cqs[k], pb)
            rms_bc(cqs, 384)
            for k in range(3):
                tt("dve", cqn[k], cqs[k], rb, ALU.mult)
            for g in range(4):
                pb = next_bank()
                for k in range(3):
                    mm(pb, wuq_b[:, k, g * P:(g + 1) * P], cqn[k], start=(k == 0), stop=(k == 2))
                o = nob()
                evac_copy(o, pb)
                dma(DV(QN[g, :, ot0:ot0 + TS]), o)
            for pr in range(2):
                pb = next_bank()
                for k in range(3):
                    mm(pb, wuq_b[:, k, (4 + pr) * P:(5 + pr) * P], cqn[k], start=(k == 0), stop=(k == 2))
                tt("dve", t1[pr], pb, cosT[par], ALU.mult)
                pb2 = next_bank()
                for k in range(3):
                    mm(pb2, wuq_b[:, k, (6 + pr) * P:(7 + pr) * P], cqn[k], start=(k == 0), stop=(k == 2))
                tt("dve", t2[pr], pb2, sinT[par], ALU.mult)
                o = nob()
                tt("pool", o, t1[pr], t2[pr], ALU.add)
                dma(DV(QR[pr, :, ot0:ot0 + TS]), o)
        for k in range(2):
            pb = proj(3 + k, par)
            evac_copy(cqs[k], pb)
        rms_bc(cqs[0:2], 256)
        for k in range(2):
            tt("dve", cqn[k], cqs[k], rb, ALU.mult)
        for h in range(4):
            pb = next_bank()
            for k in range(2):
                mm(pb, wukv_b[:, k, h * P:(h + 1) * P], cqn[k], start=(k == 0), stop=(k == 1))
            o = nob()
            evac_copy(o, pb)
            dma(DV(KN[h, :, tok0:tok0 + TS]), o)
        for s in range(4):
            pb = next_bank()
            for k in range(2):
                mm(pb, cqn[k][:, s * P:(s + 1) * P], wukv_b[:, k, 512:1024], start=(k == 0), stop=(k == 1))
            o = nob()
            evac_copy(o, pb)
            dma(DV(VV[tok0 + s * P:tok0 + (s + 1) * P, :]), o)
        pb = proj(5, par)
        tt("dve", t1[0], pb, cosT[par], ALU.mult)
        pb2 = proj(6, par)
        tt("dve", t2[0], pb2, sinT[par], ALU.mult)
        o = nob()
        tt("pool", o, t1[0], t2[0], ALU.add)
        dma(DV(KR[:, tok0:tok0 + TS]), o)
        if own:
            cts = list(range(12))
        elif t == TO - 1:
            cts = list(range(12))
        else:
            cts = list(range(4, 12))
        for ct in cts:
            pb = proj(7 + ct, par)
            o = nof()
            if own:
                evac_copy(o, pb)
            else:
                tt("dve", o, pb, vm[par], ALU.mult)
            dma(DV(PC[ct, :, 2 + tok0:2 + tok0 + TS]), o)
        if own:
            for zc in range(4):
                pb = proj(19 + zc, par)
                o = nof()
                act(o, pb, AF.Silu)
                dma(DV(ZS[zc, :, ot0:ot0 + TS]), o)
        pb = next_bank()
        for s in range(4):
            for k in range(8):
                mm(pb[:, s * 16:(s + 1) * 16], hT[par][k][:, s * P:(s + 1) * P], win_b[:, k, 2944:2960],
                   start=(k == 0), stop=(k == 7))
        pv = V(pb.ap[:, 0:64].rearrange("p (s c) -> p s c", s=4), pb.key)
        ab_ = abt[par]
        dt4 = V(dtrow4.ap.rearrange("p (s c) -> p s c", s=4), dtrow4.key)
        na4 = V(negA4.ap.rearrange("p (s c) -> p s c", s=4), negA4.key)
        tt("dve", abx, pv[:, :, 0:8], dt4, ALU.add)
        act(abx, abx, AF.Exp)
        act(abx, abx, AF.Ln, bias=1.0)
        tt("dve", ab_[:, :, 0:8], abx, na4, ALU.mult)
        act(ab_[:, :, 8:16], pv[:, :, 8:16], AF.Sigmoid)
        dma(DV(ABt[tok0:tok0 + TS, :].rearrange("(s p) c -> p s c", p=P)), ab_)
    S.barrier()
    A.reset(mP)

    pcx = [A.new(F32, [TS + 4]) for _ in range(2)]
    acc = [A.new(F32, [TS]) for _ in range(2)]
    sl = [A.new(F32, [TS]) for _ in range(2)]
    slb = [A.new(BF16, [TS]) for _ in range(2)]
    sq1 = A.new(BF16, [TS])
    ln1 = A.new(F32, [TS])
    rs1 = A.new(F32, [TS])
    nb = [A.new(BF16, [TS]) for _ in range(2)]
    tok = [A.new(BF16, [4, P]) for _ in range(2)]
    vml = A.new(F32, [TS])
    if TO > 0:
        dma(vml, DV(vmask[:, (TO - 1) * TS:TO * TS]))
    eps_t = A.new(F32, [1])
    memset("dve", eps_t, EPS)
    lnq_t = A.new(F32, [1])
    memset("dve", lnq_t, float(np.log(128.0 ** -0.5)))
    zero_t = A.new(F32, [1])
    memset("dve", zero_t, 0.0)
    items = []
    for t in range(TT):
        own = t >= TO
        for ct in (range(12) if own else range(4, 12)):
            items.append((t, ct))

    def load_pc(i):
        t, ct = items[i]
        dma(pcx[i % 2], DV(PC[ct, :, t * TS:t * TS + TS + 4]))

    if items:
        load_pc(0)
    pj_i[0] = 0
    for i, (t, ct) in enumerate(items):
        par = i % 2
        if i + 1 < len(items):
            load_pc(i + 1)
        tok0 = t * TS
        ot0 = (t - TO) * TS
        px = pcx[par]
        ac = acc[par]
        ts("dve", ac, px[:, 0:TS], convw[:, ct, 0:1], None, ALU.mult)
        for j in range(1, 5):
            stt(ac, px[:, j:j + TS], convw[:, ct, j:j + 1], ac, ALU.mult, ALU.add)
        h = ct % 4
        if ct < 8:
            s_ = sl[par]
            act(s_, ac, AF.Silu)
            if t == TO - 1:
                tt("pool", s_, s_, vml, ALU.mult)
            tt("pool", sq1, s_, s_, ALU.mult)
            pb = next_bank()
            mm(pb, onesb, sq1)
            act(ln1, pb, AF.Ln, bias=eps_t, scale=1.0)
            act(rs1, ln1, AF.Exp, scale=-0.5, bias=(lnq_t if ct < 4 else zero_t))
            n_ = nb[par]
            tt("dve", n_, s_, rs1, ALU.mult)
            if ct < 4:
                dma(DV(GQ[h, :, ot0:ot0 + TS]), n_)
                continue
            dma(DV(GK[h, :, tok0:tok0 + TS]), n_)
            srcb = n_
            dst = GKt
        else:
            s_ = slb[par]
            act(s_, ac, AF.Silu)
            if t == TO - 1:
                tt("pool", s_, s_, vml, ALU.mult)
            srcb = s_
            dst = GVt
        pt = pTb[par]
        for s in range(4):
            tr(pt[:, s * P:(s + 1) * P], srcb[:, s * P:(s + 1) * P], identb)
        tk = tok[par]
        cp("act", tk, V(pt.ap.rearrange("p (s c) -> p s c", s=4), pt.key))
        dma(DV(dst[tok0:tok0 + TS, h * P:(h + 1) * P].rearrange("(s p) c -> p s c", p=P)), tk)
    S.barrier()
    A.reset(mP)

    kb_t = A.new(F32, [NKT])
    dma(kb_t, DV(kbias_d))
    kr_t = A.new(BF16, [NT])
    dma(kr_t, DV(KR))
    kn_t = A.new(BF16, [NT])
    v_t = A.new(BF16, [NKT, P])
    qn_t = [A.new(BF16, [TS]) for _ in range(2)]
    qr_t = [A.new(BF16, [TS]) for _ in range(2)]
    pT_t = [A.new(BF16, [TS]) for _ in range(3)]
    racc = A.new(F32, [TS])
    rec = A.new(F32, [TS])
    oo = [A.new(BF16, [TS]) for _ in range(2)]
    sbanks = [bank(0), bank(1), bank(2)]
    obanks = [bank(3), bank(4)]
    dbank = bank(5)
    it = 0
    for h in range(4):
        dma(kn_t, DV(KN[h]))
        dma(v_t, DV(VV[:, h * P:(h + 1) * P].rearrange("(j p) c -> p j c", p=P)))
        b0 = (h % 2) * 64
        for qt in range(TW):
            qp = it % 2
            it += 1
            dma(qn_t[qp], DV(QN[h, :, qt * TS:(qt + 1) * TS]))
            dma(qr_t[qp], DV(QR[h // 2, :, qt * TS:(qt + 1) * TS]))
            ob = obanks[qp]
            def qk(j, qp=qp, b0=b0):
                sb_ = sbanks[j % 3]
                mm(sb_, kn_t[:, j * P:(j + 1) * P], qn_t[qp], start=True, stop=False)
                mm(sb_, kr_t[b0:b0 + 64, j * P:(j + 1) * P], qr_t[qp][b0:b0 + 64, :], start=False, stop=True)

            qk(0)
            if NKT > 1:
                qk(1)
            for j in range(NKT):
                sb_ = sbanks[j % 3]
                pt = pT_t[j % 3]
                act(pt, sb_, AF.Exp, bias=kb_t[:, j:j + 1])
                if j + 2 < NKT:
                    qk(j + 2)
                mm(ob, v_t[:, j, :], pt, start=(j == 0), stop=(j == NKT - 1))
                if j == 0:
                    cp("dve", racc, pt)
                else:
                    tt("dve", racc, racc, pt, ALU.add)
            mm(dbank, ones32, racc)
            recip(rec, dbank)
            o = oo[qp]
            tt("dve", o, ob, rec, ALU.mult)
            dma(DV(OM[h, :, qt * TS:(qt + 1) * TS]), o)
    S.barrier()
    A.reset(mP)

    def slot(h, i):
        b = 2 * h + i // 4
        o = (i % 4) * P
        return V(psum[:, b * 512 + o:b * 512 + o + P], ("bank", b))

    S32 = {(h, d): A.new(F32, [P]) for h in range(4) for d in "XY"}
    Sb = {(h, d): A.new(BF16, [P]) for h in range(4) for d in "XY"}
    for kk in S32:
        memset("dve", S32[kk], 0.0)
        memset("pool", Sb[kk], 0.0)
    eps_t = A.new(F32, [1])
    memset("dve", eps_t, EPS)
    NPAR = 2
    kT4 = [A.new(BF16, [4, P]) for _ in range(NPAR)]
    qT4 = [A.new(BF16, [4, P]) for _ in range(NPAR)]
    ktok = [A.new(BF16, [512]) for _ in range(NPAR)]
    vtok = [A.new(BF16, [512]) for _ in range(NPAR)]
    ab = [A.new(F32, [16]) for _ in range(NPAR)]
    oxl = [A.new(F32, [4, P]) for _ in range(NPAR)]
    zsl = [A.new(F32, [4, P]) for _ in range(NPAR)]
    egc = [A.new(F32, [4]) for _ in range(NPAR)]
    bege = [A.new(F32, [4]) for _ in range(NPAR)]
    etail = [A.new(F32, [4]) for _ in range(NPAR)]
    negb = [A.new(F32, [4]) for _ in range(NPAR)]

    def per_head(dt, n=NPAR):
        return [[A.new(dt, [P]) for _ in range(4)] for _ in range(n)]

    gUM = per_head(F32)
    gOn = per_head(F32)
    dec = per_head(F32)
    dm = per_head(F32)
    P0f = per_head(F32)
    decT = per_head(F32)
    dTm = per_head(F32)
    Er = per_head(F32)
    XT32 = per_head(F32)
    u32 = per_head(F32)
    Pf = [per_head(F32), per_head(F32)]
    PTf = [per_head(F32), per_head(F32)]
    Lm = per_head(BF16)
    Dm = per_head(BF16)
    Nf = per_head(BF16)
    NTf = per_head(BF16)
    N2f = per_head(BF16)
    Y1 = per_head(BF16)
    XTh = per_head(BF16)
    tA = per_head(F32)
    tB = per_head(F32)
    XTb = per_head(BF16)
    attnT = per_head(BF16)
    qd = per_head(BF16)
    kbg = per_head(BF16)
    vb = per_head(BF16)
    ktl = per_head(BF16)
    wTb = per_head(BF16)
    vnew = per_head(BF16)
    osum = [A.new(F32, [4, P]) for _ in range(NPAR)]
    osq = [A.new(BF16, [4, P]) for _ in range(NPAR)]
    oln = A.new(F32, [4, P])
    ors = A.new(F32, [4, P])
    on_ = A.new(F32, [4, P])
    oout = [A.new(BF16, [4, P]) for _ in range(NPAR)]

    def load_chunk(ci, par, full):
        c0 = ci * P
        dma(kT4[par], DV(GK[:, :, c0:c0 + P].rearrange("h p t -> p h t")))
        dma(ktok[par], DV(GKt[c0:c0 + P, :]))
        dma(vtok[par], DV(GVt[c0:c0 + P, :]))
        dma(ab[par], DV(ABt[c0:c0 + P, :]))
        if full:
            o0 = c0 - NO
            dma(qT4[par], DV(GQ[:, :, o0:o0 + P].rearrange("h p t -> p h t")))

    def gdn_step(ci, d, par, full, last_dir):
        goff = 0 if d == "X" else 4
        boff = 8 if d == "X" else 12
        lc, um = LC[d], UM[d]
        lastcol = P - 1 if d == "X" else 0
        o0 = ci * P - NO
        a_ = ab[par]
        g4 = a_[:, goff:goff + 4]
        b4 = a_[:, boff:boff + 4]
        gc_ps = slot(0, 6)
        gt_ps = slot(0, 7)
        mm(gc_ps[:, 0:4], lc, g4)
        mm(gt_ps[:, 0:4], um, g4)
        act(egc[par], gc_ps[:, 0:4], AF.Exp)
        act(etail[par], gt_ps[:, 0:4], AF.Exp)
        tt("dve", bege[par], b4, egc[par], ALU.mult)
        ts("pool", negb[par], b4, -1.0, None, ALU.mult)
        H = range(4)
        for h in H:
            ts("pool", gUM[par][h], um, g4[:, h:h + 1], None, ALU.mult)
            ts("pool", gOn[par][h], ones32, g4[:, h:h + 1], None, ALU.mult)
        for h in H:
            mm(slot(h, 0), lc, gUM[par][h])
            if full:
                mm(slot(h, 1), gUM[par][h], lc)
            mm(slot(h, 2), gOn[par][h], lc)
            mm(slot(h, 3), kT4[par][:, h, :], kT4[par][:, h, :])
            if full:
                mm(slot(h, 4), kT4[par][:, h, :], qT4[par][:, h, :])
        for h in H:
            act(dec[par][h], slot(h, 0), AF.Exp)
            tt("pool", dm[par][h], dec[par][h], um, ALU.mult)
            stt(P0f[par][h], slot(h, 3), negb[par][:, h:h + 1], dm[par][h], ALU.mult, ALU.mult)
            tt("pool", Pf[0][par][h], P0f[par][h], BDm, ALU.mult)
            tt("pool", Lm[par][h], Pf[0][par][h], P0f[par][h], ALU.subtract)
            mm(slot(h, 5), Pf[0][par][h], ident32)
            cp("act", PTf[0][par][h], slot(h, 5))
            tt("pool", XT32[par][h], PTf[0][par][h], ident32, ALU.add)
            act(Er[par][h], slot(h, 2), AF.Exp)
            if full:
                act(decT[par][h], slot(h, 1), AF.Exp)
                tt("pool", dTm[par][h], decT[par][h], lc, ALU.mult)
                cp("act", tB[par][h], slot(h, 4))
                tt("pool", attnT[par][h], tB[par][h], dTm[par][h], ALU.mult)
                tt("pool", qd[par][h], qT4[par][:, h, :], Er[par][h], ALU.mult)
            act(kbg[par][h], ktok[par][:, h * P:(h + 1) * P], AF.Identity, scale=bege[par][:, h:h + 1])
            act(vb[par][h], vtok[par][:, h * P:(h + 1) * P], AF.Identity, scale=b4[:, h:h + 1])
            act(ktl[par][h], ktok[par][:, h * P:(h + 1) * P], AF.Identity, scale=etail[par][:, h:h + 1])
        for j in range(1, 5):
            cur, prv = j % 2, (j - 1) % 2
            for h in H:
                mm(slot(h, 0), PTf[prv][par][h], Pf[prv][par][h])
                if j < 4:
                    mm(slot(h, 1), Pf[prv][par][h], PTf[prv][par][h])
            for h in H:
                cp("act", Pf[cur][par][h], slot(h, 0))
                if j < 4:
                    cp("act", PTf[cur][par][h], slot(h, 1))
            for h in H:
                mm(slot(h, 2), Pf[cur][par][h], XT32[par][h])
            for h in H:
                cp("act", tA[par][h], slot(h, 2))
                tt("pool", XT32[par][h], XT32[par][h], tA[par][h], ALU.add)
        for h in H:
            cp("pool", XTh[par][h], XT32[par][h])
        for h in H:
            mm(slot(h, 5), XTh[par][h], identb)
            mm(slot(h, 0), XTh[par][h], Lm[par][h])
            mm(slot(h, 1), Lm[par][h], XTh[par][h])
        for h in H:
            cp("act", Dm[par][h], slot(h, 5))
            cp("act", Nf[par][h], slot(h, 0))
            cp("act", NTf[par][h], slot(h, 1))
            tt("pool", Y1[par][h], ident32, NTf[par][h], ALU.subtract)
        for h in H:
            mm(slot(h, 2), NTf[par][h], Nf[par][h])
        for h in H:
            cp("act", N2f[par][h], slot(h, 2))
        for h in H:
            mm(slot(h, 3), N2f[par][h], Y1[par][h])
        for h in H:
            cp("act", tA[par][h], slot(h, 3))
            tt("pool", Y1[par][h], Y1[par][h], tA[par][h], ALU.add)
        for h in H:
            mm(slot(h, 4), Dm[par][h], Y1[par][h])
        for h in H:
            cp("act", XTb[par][h], slot(h, 4))
        for h in H:
            mm(slot(h, 3), kbg[par][h], XTb[par][h])
            mm(slot(h, 4), XTb[par][h], vb[par][h])
        for h in H:
            cp("act", wTb[par][h], slot(h, 3))
            cp("act", u32[par][h], slot(h, 4))
        for h in H:
            mm(slot(h, 5), wTb[par][h], Sb[(h, d)])
        for h in H:
            cp("act", tA[par][h], slot(h, 5))
            tt("pool", vnew[par][h], u32[par][h], tA[par][h], ALU.subtract)
        for h in H:
            if full:
                mm(slot(h, 6), Sb[(h, d)], qd[par][h], start=True, stop=False)
                mm(slot(h, 6), vnew[par][h], attnT[par][h], start=False, stop=True)
            mm(slot(h, 7), ktl[par][h], vnew[par][h])
        for h in H:
            cp("act", tB[par][h], slot(h, 7))
            stt(S32[(h, d)], S32[(h, d)], Er[par][h][:, lastcol:lastcol + 1], tB[par][h], ALU.mult, ALU.add)
            cp("act", Sb[(h, d)], S32[(h, d)])
        if full:
            if not last_dir:
                for h in H:
                    cp("act", osum[par][:, h, :], slot(h, 6))
                dma(DV(OX[:, :, o0:o0 + P].rearrange("h p t -> p h t"), ("OX", ci)), osum[par])
            else:
                dma(oxl[par], DV(OX[:, :, o0:o0 + P].rearrange("h p t -> p h t"), ("OX", ci)))
                dma(zsl[par], DV(ZS[:, :, o0:o0 + P].rearrange("h p t -> p h t")))
                for h in H:
                    cp("act", osum[par][:, h, :], slot(h, 6))
                    tt("pool", osum[par][:, h, :], osum[par][:, h, :], oxl[par][:, h, :], ALU.add)
                tt("pool", osq[par], osum[par], osum[par], ALU.mult)
                nb_ = V(psum[:, 2 * 512:3 * 512], None)
                S.op("pe", lambda e, o=nb_.ap, l=onesb.ap, r_=osq[par].ap.rearrange("p h t -> p (h t)"):
                     e.matmul(o, lhsT=l, rhs=r_, start=True, stop=True),
                     r=[onesb.key, osq[par].key], w=[("bank", 2)])
                olf = V(oln.ap.rearrange("p h t -> p (h t)"), oln.key)
                S.op("act", lambda e, o=olf.ap, i_=nb_.ap, b_=eps_t.ap: e.activation(o, i_, AF.Ln, bias=b_, scale=1.0 / P),
                     r=[("bank", 2), eps_t.key], w=[oln.key])
                act(ors, oln, AF.Exp, scale=-0.5)
                tt("dve", on_, osum[par], ors, ALU.mult)
                stt(oout[par], on_, ggdn_t[:, 0:1], zsl[par], ALU.mult, ALU.mult)
                dma(DV(OM[4:8, :, o0:o0 + P].rearrange("h p t -> p h t")), oout[par])

    seq = [(ci, "X", ci >= CO, False) for ci in range(CO + CW)]
    seq += [(ci, "Y", True, True) for ci in range(CO + CW - 1, CO - 1, -1)]
    if seq:
        load_chunk(seq[0][0], 0, seq[0][2])
    for i, (ci, d, full, last_dir) in enumerate(seq):
        par = i % 2
        if i + 1 < len(seq):
            load_chunk(seq[i + 1][0], (i + 1) % 2, seq[i + 1][2])
        gdn_step(ci, d, par, full, last_dir)
    S.barrier()
    A.reset(mP)

    mP4 = A.mark()
    wo_b = A.new(BF16, [8, D])
    wmi_b = A.new(BF16, [8, DFF])
    m4 = A.mark()
    gaR = A.new(F32, [D])
    dma(gaR, DV(MODS[0].partition_broadcast(P), "MODS"))
    stg4 = [A.new(F32, [DFF]) for _ in range(2)]
    si = 0
    for k in range(8):
        st_ = stg4[si % 2]
        si += 1
        dma(st_[:, 0:D], DV(w_out[k * P:(k + 1) * P, :]))
        tt("dve", wo_b[:, k, :], st_[:, 0:D], gaR, ALU.mult)
    for k in range(8):
        st_ = stg4[si % 2]
        si += 1
        dma(st_, DV(w_mi[k * P:(k + 1) * P, :]))
        cp("act" if k % 2 == 0 else "dve", wmi_b[:, k, :], st_)
    S.barrier()
    A.reset(m4)
    eps_t = A.new(F32, [1])
    memset("dve", eps_t, EPS)
    xt4 = [A.new(F32, [4, D]) for _ in range(2)]
    om_t = [A.new(BF16, [8, TS]) for _ in range(2)]
    xn = A.new(BF16, [4, D])
    junk = A.new(BF16, [D])
    ss4 = A.new(F32, [4])
    sd4 = A.new(F32, [4])
    rstd4 = A.new(F32, [4])
    h2T = [A.new(BF16, [TS]) for _ in range(8)]
    rl = [A.new(F32, [TS]) for _ in range(2)]
    aring = [A.new(BF16, [TS]) for _ in range(4)]
    pj_i[0] = 0

    def load4(t):
        par = t % 2
        dma(xt4[par], DV(xw[t * TS:(t + 1) * TS, :].rearrange("(s p) d -> p s d", p=P)))
        dma(om_t[par], DV(OM[:, :, t * TS:(t + 1) * TS].rearrange("k p t -> p k t")))

    if TW:
        load4(0)
    for t in range(TW):
        par = t % 2
        if t + 1 < TW:
            load4(t + 1)
        x1 = xt4[par]
        for s in range(4):
            for hf in range(2):
                pb = next_bank()
                for k in range(8):
                    mm(pb, om_t[par][:, k, s * P:(s + 1) * P], wo_b[:, k, hf * 512:(hf + 1) * 512],
                       start=(k == 0), stop=(k == 7))
                tt("dve", x1[:, s, hf * 512:(hf + 1) * 512], pb, x1[:, s, hf * 512:(hf + 1) * 512], ALU.add)
        dma(DV(X1[t * TS:(t + 1) * TS, :].rearrange("(s p) d -> p s d", p=P)), x1)
        norm_to_hT(x1, A2, B2, h2T, ss4, sd4, rstd4)
        for f in range(32):
            pb = next_bank()
            for k in range(8):
                mm(pb, wmi_b[:, k, f * P:(f + 1) * P], h2T[k], start=(k == 0), stop=(k == 7))
            r_ = rl[f % 2]
            act(r_, pb, AF.Relu)
            ao = aring[f % 4]
            tt("pool" if f % 2 == 0 else "dve", ao, r_, r_, ALU.mult)
            dma(DV(ACTS[f, :, t * TS:(t + 1) * TS]), ao)
    S.barrier()
    A.reset(mP4)

    wmo_b = A.new(BF16, [32, D])
    A3R = A.new(F32, [D])
    B3R = A.new(F32, [D])
    dma(A3R, DV(MODS[2].partition_broadcast(P), "MODS"))
    dma(B3R, DV(MODS[3].partition_broadcast(P), "MODS"))
    m4 = A.mark()
    gmR = A.new(F32, [D])
    dma(gmR, DV(MODS[1].partition_broadcast(P), "MODS"))
    stg4 = [A.new(F32, [DFF]) for _ in range(2)]
    for k in range(0, 32, 4):
        st_ = stg4[(k // 4) % 2]
        st3 = V(st_.ap.rearrange("p (a c) -> p a c", a=4), st_.key)
        dma(st3, DV(w_mo[k * P:(k + 4) * P, :].rearrange("(a p) c -> p a c", p=P)))
        for a_ in range(4):
            tt("dve" if a_ % 2 == 0 else "pool", wmo_b[:, k + a_, :], st3[:, a_, :], gmR, ALU.mult)
    S.barrier()
    A.reset(m4)
    eps_t = A.new(F32, [1])
    memset("dve", eps_t, EPS)
    xb4 = [A.new(F32, [4, D]) for _ in range(2)]
    at_t = [A.new(BF16, [32, TS]) for _ in range(2)]
    junk = A.new(BF16, [D])
    ss4 = A.new(F32, [4])
    sd4 = A.new(F32, [4])
    rstd4 = A.new(F32, [4])
    pj_i[0] = 0

    def load4b(t):
        par = t % 2
        dma(xb4[par], DV(X1[t * TS:(t + 1) * TS, :].rearrange("(s p) d -> p s d", p=P)))
        dma(at_t[par], DV(ACTS[:, :, t * TS:(t + 1) * TS].rearrange("f p t -> p f t")))

    if TW:
        load4b(0)
    for t in range(TW):
        par = t % 2
        if t + 1 < TW:
            load4b(t + 1)
        x2 = xb4[par]
        for s in range(4):
            for hf in range(2):
                pb = next_bank()
                for f in range(32):
                    mm(pb, at_t[par][:, f, s * P:(s + 1) * P], wmo_b[:, f, hf * 512:(hf + 1) * 512],
                       start=(f == 0), stop=(f == 31))
                tt("dve", x2[:, s, hf * 512:(hf + 1) * 512], pb, x2[:, s, hf * 512:(hf + 1) * 512], ALU.add)
        for s in range(4):
            act(junk, x2[:, s, :], AF.Square, accum=ss4[:, s:s + 1])
        act(sd4, ss4, AF.Sqrt, bias=eps_t, scale=1.0 / D)
        recip(rstd4, sd4)
        for s in range(4):
            stt(x2[:, s, :], x2[:, s, :], rstd4[:, s:s + 1], A3R, ALU.mult, ALU.mult)
            tt("pool", x2[:, s, :], x2[:, s, :], B3R, ALU.add)
        dma(DV(y[t * TS:(t + 1) * TS, :].rearrange("(s p) d -> p s d", p=P)), x2)
    S.barrier()
    S.emit(nc, stack, None if stop_after is None else (S.marks[stop_after] if stop_after < 100 else stop_after))
    stack.close()
    S.arena_log = A.log
    return nc, S


def _arr_pk(v):
    return np.ascontiguousarray(v.reshape(-1, P).T)


def make_consts():
    i = np.arange(P)
    ident = np.eye(P, dtype=np.float32)
    ones = np.ones((P, P), np.float32)
    lcx = (i[:, None] <= i[None, :]).astype(np.float32)
    umx = (i[:, None] > i[None, :]).astype(np.float32)
    lcy = (i[:, None] >= i[None, :]).astype(np.float32)
    umy = (i[:, None] < i[None, :]).astype(np.float32)
    bd = ((i[:, None] // 32) == (i[None, :] // 32)).astype(np.float32)
    return np.ascontiguousarray(np.concatenate([ident, ones, lcx, umx, lcy, umy, bd, 1.0 - bd], axis=1))


def rope_tables(pos):
    inv = (1.0 / (np.float32(10000.0) ** (np.arange(0, 64, 2, dtype=np.float32) / np.float32(64)))).astype(np.float32)
    ang = (pos.astype(np.float32)[None, :] * inv[:, None]).astype(np.float32)
    c = np.cos(ang).astype(np.float32)
    s = np.sin(ang).astype(np.float32)
    cos2 = np.concatenate([c, c, c, c], axis=0)
    sinS = np.concatenate([-s, s, -s, s], axis=0)
    return np.ascontiguousarray(cos2), np.ascontiguousarray(sinS)


def prep_core(W, x_other, x_own, valid_other, pos, cvec, flipped, xdir):
    NO = x_other.shape[0]
    NT = NO + x_own.shape[0]
    m = {}
    m["xo"] = np.ascontiguousarray(x_other, dtype=np.float32)
    m["xw"] = np.ascontiguousarray(x_own, dtype=np.float32)
    m["vmask"] = np.ascontiguousarray(np.broadcast_to(valid_other.astype(np.float32)[None, :], (P, NO)))
    kvalid = np.concatenate([valid_other.astype(np.float32), np.ones(NT - NO, np.float32)])
    kb = np.where(kvalid > 0, 0.0, -30000.0).astype(np.float32)
    m["kbias"] = _arr_pk(kb)
    m["cos2"], m["sinS"] = rope_tables(pos)
    m["cvec"] = _arr_pk(cvec.astype(np.float32))
    m["w_ada"] = W["w_ada"]
    m["w_ada_f"] = W["w_ada_f"]
    m["bada"] = np.ascontiguousarray(np.concatenate([_arr_pk(W["b_ada"]), _arr_pk(W["b_ada_f"])], axis=1))
    m["gains"] = np.ascontiguousarray(np.concatenate([_arr_pk(W["g_mix"]), _arr_pk(W["g_mlp"]), _arr_pk(W["g_final"])], axis=1))
    wi = W["w_in"]
    cq, ckv, kr = wi[:, 0:384], wi[:, 384:640], wi[:, 640:704]
    q, k, v = wi[:, 704:1216], wi[:, 1216:1728], wi[:, 1728:2240]
    z = wi[:, 2240:2752]
    a_f, a_b, b_f, b_b = wi[:, 2752:2756], wi[:, 2756:2760], wi[:, 2760:2764], wi[:, 2764:2768]
    krs = np.concatenate([kr[:, 32:64], kr[:, 0:32]], axis=1)
    if xdir == "f":
        aX, aY, bX, bY = a_f, a_b, b_f, b_b
        alX, alY, dtX, dtY = W["a_log_f"], W["a_log_b"], W["dt_f"], W["dt_b"]
    else:
        aX, aY, bX, bY = a_b, a_f, b_b, b_f
        alX, alY, dtX, dtY = W["a_log_b"], W["a_log_f"], W["dt_b"], W["dt_f"]
    m["w_in"] = np.ascontiguousarray(np.concatenate([cq, ckv, kr, kr, krs, krs, q, k, v, z, aX, aY, bX, bY], axis=1))
    assert m["w_in"].shape[1] == NCOL
    wq = W["w_uq"]
    nope = [wq[:, h * 192:h * 192 + 128] for h in range(4)]
    rope = [wq[:, h * 192 + 128:h * 192 + 192] for h in range(4)]
    ropes = [np.concatenate([r[:, 32:64], r[:, 0:32]], axis=1) for r in rope]
    m["w_uq"] = np.ascontiguousarray(np.concatenate(nope + rope + ropes, axis=1))
    wk = W["w_ukv"]
    kn = [wk[:, h * 256:h * 256 + 128] for h in range(4)]
    vv = [wk[:, h * 256 + 128:h * 256 + 256] for h in range(4)]
    m["w_ukv"] = np.ascontiguousarray(np.concatenate(kn + vv, axis=1))
    cw = W["conv_w"]
    if flipped:
        cw = cw[::-1]
    cwa = np.transpose(cw.reshape(5, 12, P), (2, 1, 0)).reshape(P, 60)
    dtrow = np.concatenate([dtX, dtY])
    alrow = np.concatenate([alX, alY])
    sm = np.zeros((P, 134), np.float32)
    sm[:, 0:3] = _arr_pk(W["g_q"])
    sm[:, 3:5] = _arr_pk(W["g_kv"])
    sm[:, 5] = W["g_gdn"]
    sm[:, 6:38] = np.tile(dtrow, 4)[None, :]
    sm[:, 38:70] = np.tile(alrow, 4)[None, :]
    sm[:, 70:130] = cwa
    m["small"] = sm
    m["consts"] = make_consts()
    m["w_out"] = W["w_out"]
    m["w_mlp_in"] = W["w_mlp_in"]
    m["w_mlp_out"] = W["w_mlp_out"]
    return m


_WKEYS = ["w_ada", "b_ada", "g_mix", "w_in", "g_q", "w_uq", "g_kv", "w_ukv", "conv_w", "a_log_f", "a_log_b",
          "dt_f", "dt_b", "g_gdn", "w_out", "g_mlp", "w_mlp_in", "w_mlp_out"]


def _weights(inputs):
    W = {k: np.ascontiguousarray(np.asarray(inputs[k], dtype=np.float32)[0]) for k in _WKEYS}
    for k in ("w_ada_f", "b_ada_f", "g_final"):
        W[k] = np.ascontiguousarray(np.asarray(inputs[k], dtype=np.float32))
    return W


_CACHE = {}


def kernel(**inputs):
    W = _weights(inputs)
    xp = np.asarray(inputs["x_prompt"], dtype=np.float32)
    xs = np.asarray(inputs["x_sample"], dtype=np.float32)
    cp_ = np.asarray(inputs["c_prompt"], dtype=np.float32)
    cs_ = np.asarray(inputs["c_sample"], dtype=np.float32)
    B, SP_, _ = xp.shape
    BS, SS_, _ = xs.shape
    H = SP_ // 2
    assert SS_ == H and 2 * B + BS == 8
    NO = NW = H
    maps = []
    for b in range(B):
        xf = xp[b, ::-1]
        pos = np.arange(SP_ - 1, -1, -1)
        maps.append(prep_core(W, xf[:H], xf[H:], np.ones(H), pos, cp_[b], True, "b"))
        pos = np.arange(SP_)
        maps.append(prep_core(W, xp[b, :H], xp[b, H:], np.ones(H), pos, cp_[b], False, "f"))
    for b in range(BS):
        pos = np.concatenate([np.zeros(H, np.int64), np.arange(H)])
        maps.append(prep_core(W, np.zeros((H, D), np.float32), xs[b], np.zeros(H), pos, cs_[b], False, "f"))
    key = (NO, NW)
    if key not in _CACHE:
        _CACHE[key] = build(NO, NW)[0]
    nc = _CACHE[key]
    res = run_bass_kernel_spmd(nc, maps, core_ids=list(range(8)))
    yp = np.zeros_like(xp)
    ys = np.zeros_like(xs)
    for b in range(B):
        yp[b, :H] = res.results[2 * b]["y"][::-1]
        yp[b, H:] = res.results[2 * b + 1]["y"]
    for b in range(BS):
        ys[b] = res.results[2 * B + b]["y"]
    return yp, ys
```

```python
import numpy as np
import concourse.bass as bass
import concourse.mybir as mybir
from concourse.bass_utils import run_bass_kernel_spmd

F32 = mybir.dt.float32
BF16 = mybir.dt.bfloat16
AF = mybir.ActivationFunctionType
ALU = mybir.AluOpType
P = 128
TS = 512
EPS = 1e-6
D = 1024
NCOL = 2960
DFF = 4096


class V:
    __slots__ = ("ap", "key")

    def __init__(self, ap, key):
        self.ap = ap
        self.key = key

    def __getitem__(self, idx):
        return V(self.ap[idx], self.key)

    def k(self, key):
        return V(self.ap, key)


class Sched:
    ENGS = ("pe", "act", "dve", "pool", "sp")
    RING = 24

    def __init__(self, same_sync=True):
        self.ops = []
        self.lastw = {}
        self.readers = {}
        self.same_sync = same_sync
        self.last_eng = {}
        self.dma_hist = []
        self.marks = []

    def op(self, eng, fn, r=(), w=(), dma=False):
        i = len(self.ops)
        deps = set()
        for k in r:
            if k is None:
                continue
            lw = self.lastw.get(k)
            if lw is not None:
                deps.add(lw)
        for k in w:
            if k is None:
                continue
            lw = self.lastw.get(k)
            if lw is not None:
                deps.add(lw)
            for j in self.readers.get(k, {}).values():
                deps.add(j)
        import sys as _s
        f = _s._getframe(2)
        self.ops.append(dict(eng=eng, fn=fn, deps=deps, dma=dma, line=(f.f_lineno, f.f_back.f_lineno)))
        for k in r:
            if k is None:
                continue
            self.readers.setdefault(k, {})[("dma", i) if dma else eng] = i
        for k in w:
            if k is None:
                continue
            self.lastw[k] = i
            self.readers[k] = {}
        if dma:
            self.dma_hist.append(i)
        else:
            self.last_eng[eng] = i
        return i

    def barrier(self):
        self.marks.append(len(self.ops) + len(self.ENGS))
        ids = set(self.last_eng.values()) | set(self.dma_hist[-self.RING:])
        for e in self.ENGS:
            i = len(self.ops)
            self.ops.append(dict(eng=e, fn=None, deps=set(ids), dma=False))
            self.last_eng[e] = i
        self.lastw = {}
        self.readers = {}

    def emit(self, nc, stack, limit=None):
        ops = self.ops if limit is None else self.ops[:limit]
        self.ops = ops
        n = len(ops)
        needed = [False] * n
        for i, o in enumerate(ops):
            keep = set()
            for d in o["deps"]:
                od = ops[d]
                if od["fn"] is None:
                    if od["eng"] == o["eng"]:
                        continue
                    keep |= od["deps"]
                    continue
                if (not od["dma"]) and od["eng"] == o["eng"]:
                    if o["dma"]:
                        pass
                    if od["eng"] == "pe" or not self.same_sync or o["dma"] or o["fn"] is None:
                        continue
                keep.add(d)
            o["deps"] = keep
        for o in ops:
            for d in o["deps"]:
                needed[d] = True
        sems = {e: stack.enter_context(nc.semaphore("s_" + e)) for e in self.ENGS}
        ring = [stack.enter_context(nc.semaphore("r%d" % i)) for i in range(self.RING)]
        token = [None] * n
        cnt = {e: 0 for e in self.ENGS}
        ndma = 0
        ring_prev = [None] * self.RING
        for i, o in enumerate(ops):
            if o["fn"] is None:
                continue
            if o["dma"]:
                slot = ndma % self.RING
                val = 16 * (ndma // self.RING + 1)
                token[i] = (ring[slot], val, "r%d" % slot)
                if ring_prev[slot] is not None:
                    o["deps"].add(ring_prev[slot])
                ring_prev[slot] = i
                ndma += 1
            elif needed[i]:
                cnt[o["eng"]] += 1
                token[i] = (sems[o["eng"]], cnt[o["eng"]], "s_" + o["eng"])
        per_eng = {e: [] for e in self.ENGS}
        for i, o in enumerate(ops):
            per_eng[o["eng"]].append(i)
        self.stats = {e: len(per_eng[e]) for e in self.ENGS}
        self.stats["ndma"] = ndma
        block = stack.enter_context(nc.Block())

        def run(eng_name, h):
            waited = {}
            nw = 0
            for i in per_eng[eng_name]:
                o = ops[i]
                for d in sorted(o["deps"]):
                    sem, val, sname = token[d]
                    if waited.get(sname, 0) < val:
                        h.wait_ge(sem, val)
                        waited[sname] = val
                        nw += 1
                if o["fn"] is None:
                    continue
                inst = o["fn"](h)
                if o["dma"]:
                    inst.then_inc(token[i][0], 16)
                elif needed[i]:
                    inst.then_inc(token[i][0], 1)
            self.stats["w_" + eng_name] = nw

        @block.tensor
        def _(h):
            run("pe", h)

        @block.scalar
        def _(h):
            run("act", h)

        @block.vector
        def _(h):
            run("dve", h)

        @block.gpsimd
        def _(h):
            run("pool", h)

        @block.sync
        def _(h):
            run("sp", h)


class Arena:
    def __init__(self, big, nwords, base=0):
        self.big = big
        self.n = nwords
        self.off = base
        self.uid = 0
        self.log = []

    def mark(self):
        return self.off

    def reset(self, m):
        self.off = m

    def new(self, dt, shape, key=None):
        n = int(np.prod(shape))
        words = (n + 1) // 2 if dt == BF16 else n
        words = (words + 1) // 2 * 2
        a = self.off
        self.off += words
        assert self.off <= self.n, ("arena overflow", self.off, self.n)
        v = self.big[:, a:a + words]
        if dt == BF16:
            v = v.bitcast(BF16)
        v = v[:, 0:n]
        if len(shape) == 2:
            v = v.rearrange("p (a b) -> p a b", a=shape[0])
        elif len(shape) == 3:
            v = v.rearrange("p (a b c) -> p a b c", a=shape[0], b=shape[1])
        self.uid += 1
        import sys as _s
        self.log.append((a, words, dt == BF16, tuple(shape), _s._getframe(1).f_lineno, _s._getframe(2).f_lineno))
        return V(v, key if key is not None else ("t", self.uid))


SB_WORDS = 49152 - 2048


def build(NO, NW, stop_after=None):
    NT = NO + NW
    TO, TW = NO // TS, NW // TS
    TT = TO + TW
    NKT = NT // P
    CO, CW = NO // P, NW // P
    nc = bass.Bass("TRN2", target_bir_lowering=False)

    def din(name, shape, dt=F32):
        return nc.dram_tensor(name, list(shape), dt, kind="ExternalInput").ap()

    def dscr(name, shape, dt):
        return nc.dram_tensor(name, list(shape), dt, kind="Internal").ap()

    xo = din("xo", [NO, D])
    xw = din("xw", [NW, D])
    vmask = din("vmask", [P, NO])
    kbias_d = din("kbias", [P, NKT])
    cos_d = din("cos2", [P, NT])
    sin_d = din("sinS", [P, NT])
    cvec = din("cvec", [P, 8])
    w_ada = din("w_ada", [D, 6 * D])
    w_adaf = din("w_ada_f", [D, 2 * D])
    bada = din("bada", [P, 64])
    gains = din("gains", [P, 24])
    w_in = din("w_in", [D, NCOL])
    w_uq = din("w_uq", [384, 1024])
    w_ukv = din("w_ukv", [256, 1024])
    small = din("small", [P, 134])
    consts = din("consts", [P, 1024])
    w_out = din("w_out", [D, D])
    w_mi = din("w_mlp_in", [D, DFF])
    w_mo = din("w_mlp_out", [DFF, D])
    y = nc.dram_tensor("y", [NW, D], F32, kind="ExternalOutput").ap()

    QN = dscr("QN", [4, P, NW], BF16)
    QR = dscr("QR", [2, P, NW], BF16)
    KN = dscr("KN", [4, P, NT], BF16)
    KR = dscr("KR", [P, NT], BF16)
    VV = dscr("VV", [NT, 512], BF16)
    PC = dscr("PC", [12, P, NT + 4], F32)
    ZS = dscr("ZS", [4, P, NW], F32)
    ABt = dscr("ABt", [NT, 16], F32)
    GQ = dscr("GQ", [4, P, NW], BF16)
    GK = dscr("GK", [4, P, NT], BF16)
    GKt = dscr("GKt", [NT, 512], BF16)
    GVt = dscr("GVt", [NT, 512], BF16)
    OX = dscr("OX", [4, P, NW], F32)
    OM = dscr("OM", [8, P, NW], BF16)
    MODS = dscr("MODS", [4, D], F32)
    X1 = dscr("X1", [NW, D], F32)
    ACTS = dscr("ACTS", [32, P, NW], BF16)

    import contextlib
    stack = contextlib.ExitStack()
    big = stack.enter_context(nc.sbuf_tensor("big", [P, SB_WORDS], F32))
    psum = stack.enter_context(nc.psum_tensor("psum", [P, 4096], F32))
    S = Sched(same_sync=True)
    A = Arena(big, SB_WORDS)

    def bank(b, key=None):
        return V(psum[:, b * 512:(b + 1) * 512], key if key is not None else ("bank", b))

    def keys(*vs):
        return [v.key for v in vs if v is not None]

    def mm(out, lhsT, rhs, start=True, stop=True):
        S.op("pe", lambda e: e.matmul(out.ap, lhsT=lhsT.ap, rhs=rhs.ap, start=start, stop=stop),
             r=keys(lhsT, rhs), w=keys(out))

    def tr(out, in_, ident):
        S.op("pe", lambda e: e.transpose(out.ap, in_.ap, ident.ap), r=keys(in_, ident), w=keys(out))

    def act(out, in_, func, bias=None, scale=None, accum=None, extra_r=()):
        kw = {}
        rr = [in_]
        if bias is not None:
            if isinstance(bias, V):
                kw["bias"] = bias.ap
                rr.append(bias)
            else:
                kw["bias"] = float(bias)
        if scale is not None:
            if isinstance(scale, V):
                kw["scale"] = scale.ap
                rr.append(scale)
            else:
                kw["scale"] = float(scale)
        ww = [out]
        if accum is not None:
            kw["accum_out"] = accum.ap
            ww.append(accum)
        S.op("act", lambda e: e.activation(out.ap, in_.ap, func, **kw), r=keys(*rr) + list(extra_r), w=keys(*ww))

    def tt(eng, out, a, b, op):
        S.op(eng, lambda e: e.tensor_tensor(out.ap, a.ap, b.ap, op), r=keys(a, b), w=keys(out))

    def ts(eng, out, a, s1, s2, op0, op1=None):
        rr = [a]
        a1 = s1.ap if isinstance(s1, V) else float(s1)
        if isinstance(s1, V):
            rr.append(s1)
        if s2 is None:
            S.op(eng, lambda e: e.tensor_scalar(out.ap, a.ap, a1, None, op0), r=keys(*rr), w=keys(out))
        else:
            a2 = s2.ap if isinstance(s2, V) else float(s2)
            if isinstance(s2, V):
                rr.append(s2)
            S.op(eng, lambda e: e.tensor_scalar(out.ap, a.ap, a1, a2, op0, op1), r=keys(*rr), w=keys(out))

    def stt(out, a, s, b, op0, op1):
        rr = [a, b]
        a1 = s.ap if isinstance(s, V) else float(s)
        if isinstance(s, V):
            rr.append(s)
        S.op("dve", lambda e: e.scalar_tensor_tensor(out.ap, a.ap, a1, b.ap, op0, op1), r=keys(*rr), w=keys(out))

    def cp(eng, out, in_):
        if eng == "act":
            act(out, in_, AF.Copy)
        else:
            S.op(eng, lambda e: e.tensor_copy(out.ap, in_.ap), r=keys(in_), w=keys(out))

    def recip(out, in_):
        S.op("dve", lambda e: e.reciprocal(out.ap, in_.ap), r=keys(in_), w=keys(out))

    def memset(eng, out, val):
        S.op(eng, lambda e: e.memset(out.ap, val), w=keys(out))

    def dma(out, in_, slow=False):
        if slow:
            S.op("sp", lambda e: e.dma_start(out=out.ap, in_=in_.ap, allow_slow_non_contiguous=True),
                 r=keys(in_), w=keys(out), dma=True)
        else:
            S.op("sp", lambda e: e.dma_start(out=out.ap, in_=in_.ap), r=keys(in_), w=keys(out), dma=True)

    def DV(ap, key=None):
        return V(ap, key)

    cst = A.new(F32, [1024])
    dma(cst, DV(consts))
    ident32 = cst[:, 0:128]
    ones32 = cst[:, 128:256]
    LC = {"X": cst[:, 256:384], "Y": cst[:, 512:640]}
    UM = {"X": cst[:, 384:512], "Y": cst[:, 640:768]}
    BDm = cst[:, 768:896]
    NBDm = cst[:, 896:1024]
    identb = A.new(BF16, [128])
    onesb = A.new(BF16, [128])
    cp("dve", identb, ident32)
    cp("dve", onesb, ones32)
    sm = A.new(F32, [134])
    dma(sm, DV(small))
    gq_t = sm[:, 0:3]
    gkv_t = sm[:, 3:5]
    ggdn_t = sm[:, 5:6]
    dtrow4 = sm[:, 6:38].k(sm.key)
    alrow4 = sm[:, 38:70]
    convw = V(sm.ap[:, 70:130].rearrange("p (c j) -> p c j", c=12), sm.key)
    negA4 = A.new(F32, [32])
    act(negA4, alrow4, AF.Exp)
    ts("dve", negA4, negA4, -1.0, None, ALU.mult)
    gn = A.new(F32, [24])
    dma(gn, DV(gains))
    bd = A.new(F32, [64])
    dma(bd, DV(bada))
    cv = A.new(F32, [8])
    dma(cv, DV(cvec))
    sc = A.new(F32, [8])
    act(sc, cv, AF.Silu)
    modT = A.new(F32, [64])
    A1 = A.new(F32, [8])
    A2 = A.new(F32, [8])
    A3 = A.new(F32, [8])
    m0 = A.mark()
    wst = [A.new(F32, [8, 512]) for _ in range(2)]
    pmod = bank(0)
    for cg in range(16):
        src = w_ada if cg < 12 else w_adaf
        c0 = (cg if cg < 12 else cg - 12) * 512
        wt = wst[cg % 2]
        dma(wt, DV(src[:, c0:c0 + 512].rearrange("(k p) c -> p k c", p=P)))
        for jj in range(4):
            j = cg * 4 + jj
            for k in range(8):
                mm(pmod[:, j:j + 1], wt[:, k, jj * 128:(jj + 1) * 128], sc[:, k:k + 1], start=(k == 0), stop=(k == 7))
    tt("dve", modT, pmod[:, 0:64], bd, ALU.add)
    stt(A1, modT[:, 8:16], 1.0, gn[:, 0:8], ALU.add, ALU.mult)
    stt(A2, modT[:, 32:40], 1.0, gn[:, 8:16], ALU.add, ALU.mult)
    stt(A3, modT[:, 56:64], 1.0, gn[:, 16:24], ALU.add, ALU.mult)
    B1 = modT[:, 0:8]
    B2 = modT[:, 24:32]
    dma(DV(MODS[0].rearrange("(k p) -> p k", p=P), "MODS"), modT[:, 16:24], slow=True)
    dma(DV(MODS[1].rearrange("(k p) -> p k", p=P), "MODS"), modT[:, 40:48], slow=True)
    dma(DV(MODS[2].rearrange("(k p) -> p k", p=P), "MODS"), A3, slow=True)
    dma(DV(MODS[3].rearrange("(k p) -> p k", p=P), "MODS"), modT[:, 48:56], slow=True)
    A.reset(m0)
    S.barrier()
    mP = A.mark()

    win_b = A.new(BF16, [8, NCOL])
    wuq_b = A.new(BF16, [3, 1024])
    wukv_b = A.new(BF16, [2, 1024])
    m1 = A.mark()
    stg = [A.new(F32, [NCOL]) for _ in range(2)]
    for k in range(8):
        st_ = stg[k % 2]
        dma(st_, DV(w_in[k * P:(k + 1) * P, :]))
        cp("act" if k % 2 == 0 else "dve", win_b[:, k, :], st_)
    for k in range(3):
        st_ = stg[k % 2]
        dma(st_[:, 0:1024], DV(w_uq[k * P:(k + 1) * P, :]))
        ts("dve", wuq_b[:, k, :], st_[:, 0:1024], gq_t[:, k:k + 1], 192.0 ** -0.5, ALU.mult, ALU.mult)
    for k in range(2):
        st_ = stg[(k + 1) % 2]
        dma(st_[:, 0:1024], DV(w_ukv[k * P:(k + 1) * P, :]))
        ts("dve", wukv_b[:, k, :], st_[:, 0:1024], gkv_t[:, k:k + 1], None, ALU.mult)
    zt = A.new(F32, [12, 2])
    memset("dve", zt, 0.0)
    dma(DV(PC[:, :, 0:2].rearrange("c p t -> p c t")), zt)
    dma(DV(PC[:, :, NT + 2:NT + 4].rearrange("c p t -> p c t")), zt)
    S.barrier()
    A.reset(m1)

    xt = [A.new(F32, [4, D]) for _ in range(2)]
    cosT = [A.new(F32, [TS]) for _ in range(2)]
    sinT = [A.new(F32, [TS]) for _ in range(2)]
    vm = [A.new(F32, [TS]) for _ in range(2)]
    junk = A.new(BF16, [D])
    ss = [A.new(F32, [4]) for _ in range(2)]
    sd = [A.new(F32, [4]) for _ in range(2)]
    rstd = [A.new(F32, [4]) for _ in range(2)]
    xn = A.new(BF16, [4, D])
    hT = [[A.new(BF16, [TS]) for _ in range(8)] for _ in range(2)]
    pTb = [V(psum[:, 0:256].bitcast(BF16), ("bank", 0)), V(psum[:, 512:768].bitcast(BF16), ("bank", 1))]
    pj_banks = [bank(b) for b in range(2, 8)]
    pj_i = [0]

    def next_bank():
        b = pj_banks[pj_i[0] % len(pj_banks)]
        pj_i[0] += 1
        return b

    cqs = [A.new(F32, [TS]) for _ in range(3)]
    sqb = [A.new(BF16, [TS]) for _ in range(3)]
    cqn = [A.new(BF16, [TS]) for _ in range(3)]
    lnt = A.new(F32, [TS])
    rb = A.new(F32, [TS])
    outb = [A.new(BF16, [TS]) for _ in range(4)]
    outf = [A.new(F32, [TS]) for _ in range(4)]
    t1 = [A.new(F32, [TS]) for _ in range(2)]
    t2 = [A.new(F32, [TS]) for _ in range(2)]
    abt = [A.new(F32, [4, 16]) for _ in range(2)]
    abx = A.new(F32, [4, 8])
    ob_i = [0]
    of_i = [0]

    def nob():
        v = outb[ob_i[0] % 4]
        ob_i[0] += 1
        return v

    def nof():
        v = outf[of_i[0] % 4]
        of_i[0] += 1
        return v

    ev_i = [0]

    def evac_copy(out, in_):
        e = "act" if ev_i[0] % 2 == 0 else "dve"
        ev_i[0] += 1
        cp(e, out, in_)

    def load_tile(t):
        par = t % 2
        own = t >= TO
        src = xw if own else xo
        r0 = (t - TO if own else t) * TS
        dma(xt[par], DV(src[r0:r0 + TS, :].rearrange("(s p) d -> p s d", p=P)))
        dma(cosT[par], DV(cos_d[:, t * TS:(t + 1) * TS]))
        dma(sinT[par], DV(sin_d[:, t * TS:(t + 1) * TS]))
        if not own:
            dma(vm[par], DV(vmask[:, t * TS:(t + 1) * TS]))

    def norm_to_hT(xtile, A_, B_, hT_par, ss_, sd_, rstd_):
        for s in range(4):
            act(junk, xtile[:, s, :], AF.Square, accum=ss_[:, s:s + 1])
        act(sd_, ss_, AF.Sqrt, bias=eps_t, scale=1.0 / D)
        recip(rstd_, sd_)
        for s in range(4):
            ts("dve" if s % 2 == 0 else "pool", xn[:, s, :], xtile[:, s, :], rstd_[:, s:s + 1], None, ALU.mult)
        for k in range(8):
            pt = pTb[k % 2]
            for s in range(4):
                tr(pt[:, s * P:(s + 1) * P], xn[:, s, k * P:(k + 1) * P], identb)
            if k % 2 == 0:
                act(hT_par[k], pt, AF.Identity, bias=B_[:, k:k + 1], scale=A_[:, k:k + 1])
            else:
                ts("dve", hT_par[k], pt, A_[:, k:k + 1], B_[:, k:k + 1], ALU.mult, ALU.add)

    eps_t = A.new(F32, [1])
    memset("dve", eps_t, EPS)
    lnq_t = A.new(F32, [1])
    memset("dve", lnq_t, float(np.log(128.0 ** -0.5)))

    def proj(c, par, width=P):
        pb = next_bank()
        for k in range(8):
            mm(pb[0:width, :], win_b[:, k, c * P:c * P + width], hT[par][k], start=(k == 0), stop=(k == 7))
        return pb

    def rms_bc(chunks, nfeat):
        n = len(chunks)
        for k in range(n):
            tt("pool", sqb[k], chunks[k], chunks[k], ALU.mult)
        pb = next_bank()
        for k in range(n):
            mm(pb, onesb, sqb[k], start=(k == 0), stop=(k == n - 1))
        act(lnt, pb, AF.Ln, bias=eps_t, scale=1.0 / nfeat)
        act(rb, lnt, AF.Exp, scale=-0.5)

    load_tile(0)
    for t in range(TT):
        par = t % 2
        own = t >= TO
        tok0 = t * TS
        ot0 = (t - TO) * TS
        if t + 1 < TT:
            load_tile(t + 1)
        norm_to_hT(xt[par], A1, B1, hT[par], ss[par], sd[par], rstd[par])
        if own:
            for k in range(3):
                pb = proj(k, par)
                evac_copy(cqs[k], pb)
            rms_bc(cqs, 384)
            for k in range(3):
                tt("dve", cqn[k], cqs[k], rb, ALU.mult)
            for g in range(4):
                pb = next_bank()
                for k in range(3):
                    mm(pb, wuq_b[:, k, g * P:(g + 1) * P], cqn[k], start=(k == 0), stop=(k == 2))
                o = nob()
                evac_copy(o, pb)
                dma(DV(QN[g, :, ot0:ot0 + TS]), o)
            for pr in range(2):
                pb = next_bank()
                for k in range(3):
                    mm(pb, wuq_b[:, k, (4 + pr) * P:(5 + pr) * P], cqn[k], start=(k == 0), stop=(k == 2))
                tt("dve", t1[pr], pb, cosT[par], ALU.mult)
                pb2 = next_bank()
                for k in range(3):
                    mm(pb2, wuq_b[:, k, (6 + pr) * P:(7 + pr) * P], cqn[k], start=(k == 0), stop=(k == 2))
                tt("dve", t2[pr], pb2, sinT[par], ALU.mult)
                o = nob()
                tt("pool", o, t1[pr], t2[pr], ALU.add)
                dma(DV(QR[pr, :, ot0:ot0 + TS]), o)
        for k in range(2):
            pb = proj(3 + k, par)
            evac_copy(cqs[k], pb)
        rms_bc(cqs[0:2], 256)
        for k in range(2):
            tt("dve", cqn[k], cqs[k], rb, ALU.mult)
        for h in range(4):
            pb = next_bank()
            for k in range(2):
                mm(pb, wukv_b[:, k, h * P:(h + 1) * P], cqn[k], start=(k == 0), stop=(k == 1))
            o = nob()
            evac_copy(o, pb)
            dma(DV(KN[h, :, tok0:tok0 + TS]), o)
        for s in range(4):
            pb = next_bank()
            for k in range(2):
                mm(pb, cqn[k][:, s * P:(s + 1) * P], wukv_b[:, k, 512:1024], start=(k == 0), stop=(k == 1))
            o = nob()
            evac_copy(o, pb)
            dma(DV(VV[tok0 + s * P:tok0 + (s + 1) * P, :]), o)
        pb = proj(5, par)
        tt("dve", t1[0], pb, cosT[par], ALU.mult)
        pb2 = proj(6, par)
        tt("dve", t2[0], pb2, sinT[par], ALU.mult)
        o = nob()
        tt("pool", o, t1[0], t2[0], ALU.add)
        dma(DV(KR[:, tok0:tok0 + TS]), o)
        if own:
            cts = list(range(12))
        elif t == TO - 1:
            cts = list(range(12))
        else:
            cts = list(range(4, 12))
        for ct in cts:
            pb = proj(7 + ct, par)
            o = nof()
            if own:
                evac_copy(o, pb)
            else:
                tt("dve", o, pb, vm[par], ALU.mult)
            dma(DV(PC[ct, :, 2 + tok0:2 + tok0 + TS]), o)
        if own:
            for zc in range(4):
                pb = proj(19 + zc, par)
                o = nof()
                act(o, pb, AF.Silu)
                dma(DV(ZS[zc, :, ot0:ot0 + TS]), o)
        pb = next_bank()
        for s in range(4):
            for k in range(8):
                mm(pb[:, s * 16:(s + 1) * 16], hT[par][k][:, s * P:(s + 1) * P], win_b[:, k, 2944:2960],
                   start=(k == 0), stop=(k == 7))
        pv = V(pb.ap[:, 0:64].rearrange("p (s c) -> p s c", s=4), pb.key)
        ab_ = abt[par]
        dt4 = V(dtrow4.ap.rearrange("p (s c) -> p s c", s=4), dtrow4.key)
        na4 = V(negA4.ap.rearrange("p (s c) -> p s c", s=4), negA4.key)
        tt("dve", abx, pv[:, :, 0:8], dt4, ALU.add)
        act(abx, abx, AF.Exp)
        act(abx, abx, AF.Ln, bias=1.0)
        tt("dve", ab_[:, :, 0:8], abx, na4, ALU.mult)
        act(ab_[:, :, 8:16], pv[:, :, 8:16], AF.Sigmoid)
        dma(DV(ABt[tok0:tok0 + TS, :].rearrange("(s p) c -> p s c", p=P)), ab_)
    S.barrier()
    A.reset(mP)

    pcx = [A.new(F32, [TS + 4]) for _ in range(2)]
    acc = [A.new(F32, [TS]) for _ in range(2)]
    sl = [A.new(F32, [TS]) for _ in range(2)]
    slb = [A.new(BF16, [TS]) for _ in range(2)]
    sq1 = A.new(BF16, [TS])
    ln1 = A.new(F32, [TS])
    rs1 = A.new(F32, [TS])
    nb = [A.new(BF16, [TS]) for _ in range(2)]
    tok = [A.new(BF16, [4, P]) for _ in range(2)]
    vml = A.new(F32, [TS])
    if TO > 0:
        dma(vml, DV(vmask[:, (TO - 1) * TS:TO * TS]))
    eps_t = A.new(F32, [1])
    memset("dve", eps_t, EPS)
    lnq_t = A.new(F32, [1])
    memset("dve", lnq_t, float(np.log(128.0 ** -0.5)))
    zero_t = A.new(F32, [1])
    memset("dve", zero_t, 0.0)
    items = []
    for t in range(TT):
        own = t >= TO
        for ct in (range(12) if own else range(4, 12)):
            items.append((t, ct))

    def load_pc(i):
        t, ct = items[i]
        dma(pcx[i % 2], DV(PC[ct, :, t * TS:t * TS + TS + 4]))

    if items:
        load_pc(0)
    pj_i[0] = 0
    for i, (t, ct) in enumerate(items):
        par = i % 2
        if i + 1 < len(items):
            load_pc(i + 1)
        tok0 = t * TS
        ot0 = (t - TO) * TS
        px = pcx[par]
        ac = acc[par]
        ts("dve", ac, px[:, 0:TS], convw[:, ct, 0:1], None, ALU.mult)
        for j in range(1, 5):
            stt(ac, px[:, j:j + TS], convw[:, ct, j:j + 1], ac, ALU.mult, ALU.add)
        h = ct % 4
        if ct < 8:
            s_ = sl[par]
            act(s_, ac, AF.Silu)
            if t == TO - 1:
                tt("pool", s_, s_, vml, ALU.mult)
            tt("pool", sq1, s_, s_, ALU.mult)
            pb = next_bank()
            mm(pb, onesb, sq1)
            act(ln1, pb, AF.Ln, bias=eps_t, scale=1.0)
            act(rs1, ln1, AF.Exp, scale=-0.5, bias=(lnq_t if ct < 4 else zero_t))
            n_ = nb[par]
            tt("dve", n_, s_, rs1, ALU.mult)
            if ct < 4:
                dma(DV(GQ[h, :, ot0:ot0 + TS]), n_)
                continue
            dma(DV(GK[h, :, tok0:tok0 + TS]), n_)
            srcb = n_
            dst = GKt
        else:
            s_ = slb[par]
            act(s_, ac, AF.Silu)
            if t == TO - 1:
                tt("pool", s_, s_, vml, ALU.mult)
            srcb = s_
            dst = GVt
        pt = pTb[par]
        for s in range(4):
            tr(pt[:, s * P:(s + 1) * P], srcb[:, s * P:(s + 1) * P], identb)
        tk = tok[par]
        cp("act", tk, V(pt.ap.rearrange("p (s c) -> p s c", s=4), pt.key))
        dma(DV(dst[tok0:tok0 + TS, h * P:(h + 1) * P].rearrange("(s p) c -> p s c", p=P)), tk)
    S.barrier()
    A.reset(mP)

    kb_t = A.new(F32, [NKT])
    dma(kb_t, DV(kbias_d))
    kr_t = A.new(BF16, [NT])
    dma(kr_t, DV(KR))
    kn_t = A.new(BF16, [NT])
    v_t = A.new(BF16, [NKT, P])
    qn_t = [A.new(BF16, [TS]) for _ in range(2)]
    qr_t = [A.new(BF16, [TS]) for _ in range(2)]
    pT_t = [A.new(BF16, [TS]) for _ in range(3)]
    racc = A.new(F32, [TS])
    rec = A.new(F32, [TS])
    oo = [A.new(BF16, [TS]) for _ in range(2)]
    sbanks = [bank(0), bank(1), bank(2)]
    obanks = [bank(3), bank(4)]
    dbank = bank(5)
    it = 0
    for h in range(4):
        dma(kn_t, DV(KN[h]))
        dma(v_t, DV(VV[:, h * P:(h + 1) * P].rearrange("(j p) c -> p j c", p=P)))
        b0 = (h % 2) * 64
        for qt in range(TW):
            qp = it % 2
            it += 1
            dma(qn_t[qp], DV(QN[h, :, qt * TS:(qt + 1) * TS]))
            dma(qr_t[qp], DV(QR[h // 2, :, qt * TS:(qt + 1) * TS]))
            ob = obanks[qp]
            def qk(j, qp=qp, b0=b0):
                sb_ = sbanks[j % 3]
                mm(sb_, kn_t[:, j * P:(j + 1) * P], qn_t[qp], start=True, stop=False)
                mm(sb_, kr_t[b0:b0 + 64, j * P:(j + 1) * P], qr_t[qp][b0:b0 + 64, :], start=False, stop=True)

            qk(0)
            if NKT > 1:
                qk(1)
            for j in range(NKT):
                sb_ = sbanks[j % 3]
                pt = pT_t[j % 3]
                act(pt, sb_, AF.Exp, bias=kb_t[:, j:j + 1])
                if j + 2 < NKT:
                    qk(j + 2)
                mm(ob, v_t[:, j, :], pt, start=(j == 0), stop=(j == NKT - 1))
                if j == 0:
                    cp("dve", racc, pt)
                else:
                    tt("dve", racc, racc, pt, ALU.add)
            mm(dbank, ones32, racc)
            recip(rec, dbank)
            o = oo[qp]
            tt("dve", o, ob, rec, ALU.mult)
            dma(DV(OM[h, :, qt * TS:(qt + 1) * TS]), o)
    S.barrier()
    A.reset(mP)

    def slot(h, i):
        b = 2 * h + i // 4
        o = (i % 4) * P
        return V(psum[:, b * 512 + o:b * 512 + o + P], ("bank", b))

    S32 = {(h, d): A.new(F32, [P]) for h in range(4) for d in "XY"}
    Sb = {(h, d): A.new(BF16, [P]) for h in range(4) for d in "XY"}
    for kk in S32:
        memset("dve", S32[kk], 0.0)
        memset("pool", Sb[kk], 0.0)
    eps_t = A.new(F32, [1])
    memset("dve", eps_t, EPS)
    NPAR = 2
    kT4 = [A.new(BF16, [4, P]) for _ in range(NPAR)]
    qT4 = [A.new(BF16, [4, P]) for _ in range(NPAR)]
    ktok = [A.new(BF16, [512]) for _ in range(NPAR)]
    vtok = [A.new(BF16, [512]) for _ in range(NPAR)]
    ab = [A.new(F32, [16]) for _ in range(NPAR)]
    oxl = [A.new(F32, [4, P]) for _ in range(NPAR)]
    zsl = [A.new(F32, [4, P]) for _ in range(NPAR)]
    egc = [A.new(F32, [4]) for _ in range(NPAR)]
    bege = [A.new(F32, [4]) for _ in range(NPAR)]
    etail = [A.new(F32, [4]) for _ in range(NPAR)]
    negb = [A.new(F32, [4]) for _ in range(NPAR)]

    def per_head(dt, n=NPAR):
        return [[A.new(dt, [P]) for _ in range(4)] for _ in range(n)]

    gUM = per_head(F32)
    gOn = per_head(F32)
    dec = per_head(F32)
    dm = per_head(F32)
    P0f = per_head(F32)
    decT = per_head(F32)
    dTm = per_head(F32)
    Er = per_head(F32)
    XT32 = per_head(F32)
    u32 = per_head(F32)
    Pf = [per_head(F32), per_head(F32)]
    PTf = [per_head(F32), per_head(F32)]
    Lm = per_head(BF16)
    Dm = per_head(BF16)
    Nf = per_head(BF16)
    NTf = per_head(BF16)
    N2f = per_head(BF16)
    Y1 = per_head(BF16)
    XTh = per_head(BF16)
    tA = per_head(F32)
    tB = per_head(F32)
    XTb = per_head(BF16)
    attnT = per_head(BF16)
    qd = per_head(BF16)
    kbg = per_head(BF16)
    vb = per_head(BF16)
    ktl = per_head(BF16)
    wTb = per_head(BF16)
    vnew = per_head(BF16)
    osum = [A.new(F32, [4, P]) for _ in range(NPAR)]
    osq = [A.new(BF16, [4, P]) for _ in range(NPAR)]
    oln = A.new(F32, [4, P])
    ors = A.new(F32, [4, P])
    on_ = A.new(F32, [4, P])
    oout = [A.new(BF16, [4, P]) for _ in range(NPAR)]

    def load_chunk(ci, par, full):
        c0 = ci * P
        dma(kT4[par], DV(GK[:, :, c0:c0 + P].rearrange("h p t -> p h t")))
        dma(ktok[par], DV(GKt[c0:c0 + P, :]))
        dma(vtok[par], DV(GVt[c0:c0 + P, :]))
        dma(ab[par], DV(ABt[c0:c0 + P, :]))
        if full:
            o0 = c0 - NO
            dma(qT4[par], DV(GQ[:, :, o0:o0 + P].rearrange("h p t -> p h t")))

    def gdn_step(ci, d, par, full, last_dir):
        goff = 0 if d == "X" else 4
        boff = 8 if d == "X" else 12
        lc, um = LC[d], UM[d]
        lastcol = P - 1 if d == "X" else 0
        o0 = ci * P - NO
        a_ = ab[par]
        g4 = a_[:, goff:goff + 4]
        b4 = a_[:, boff:boff + 4]
        gc_ps = slot(0, 6)
        gt_ps = slot(0, 7)
        mm(gc_ps[:, 0:4], lc, g4)
        mm(gt_ps[:, 0:4], um, g4)
        act(egc[par], gc_ps[:, 0:4], AF.Exp)
        act(etail[par], gt_ps[:, 0:4], AF.Exp)
        tt("dve", bege[par], b4, egc[par], ALU.mult)
        ts("pool", negb[par], b4, -1.0, None, ALU.mult)
        H = range(4)
        EC = ["act", "dve", "act", "dve"]
        PT = ["pool", "pool", "pool", "dve"]
        for h in H:
            ts("pool", gUM[par][h], um, g4[:, h:h + 1], None, ALU.mult)
            ts("pool", gOn[par][h], ones32, g4[:, h:h + 1], None, ALU.mult)
        for h in H:
            mm(slot(h, 0), lc, gUM[par][h])
            if full:
                mm(slot(h, 1), gUM[par][h], lc)
            mm(slot(h, 2), gOn[par][h], lc)
            mm(slot(h, 3), kT4[par][:, h, :], kT4[par][:, h, :])
            if full:
                mm(slot(h, 4), kT4[par][:, h, :], qT4[par][:, h, :])
        for h in H:
            act(dec[par][h], slot(h, 0), AF.Exp)
            tt(PT[h], dm[par][h], dec[par][h], um, ALU.mult)
            stt(P0f[par][h], slot(h, 3), negb[par][:, h:h + 1], dm[par][h], ALU.mult, ALU.mult)
            tt(PT[h], Pf[0][par][h], P0f[par][h], BDm, ALU.mult)
            tt(PT[h], Lm[par][h], Pf[0][par][h], P0f[par][h], ALU.subtract)
            mm(slot(h, 5), Pf[0][par][h], ident32)
            cp(EC[h], PTf[0][par][h], slot(h, 5))
            tt(PT[h], XT32[par][h], PTf[0][par][h], ident32, ALU.add)
            act(Er[par][h], slot(h, 2), AF.Exp)
            if full:
                act(decT[par][h], slot(h, 1), AF.Exp)
                tt(PT[h], dTm[par][h], decT[par][h], lc, ALU.mult)
                cp(EC[h], tB[par][h], slot(h, 4))
                tt(PT[h], attnT[par][h], tB[par][h], dTm[par][h], ALU.mult)
                tt(PT[h], qd[par][h], qT4[par][:, h, :], Er[par][h], ALU.mult)
            act(kbg[par][h], ktok[par][:, h * P:(h + 1) * P], AF.Identity, scale=bege[par][:, h:h + 1])
            act(vb[par][h], vtok[par][:, h * P:(h + 1) * P], AF.Identity, scale=b4[:, h:h + 1])
            act(ktl[par][h], ktok[par][:, h * P:(h + 1) * P], AF.Identity, scale=etail[par][:, h:h + 1])
        for j in range(1, 5):
            cur, prv = j % 2, (j - 1) % 2
            for h in H:
                mm(slot(h, 0), PTf[prv][par][h], Pf[prv][par][h])
                if j < 4:
                    mm(slot(h, 1), Pf[prv][par][h], PTf[prv][par][h])
            for h in H:
                cp(EC[h], Pf[cur][par][h], slot(h, 0))
                if j < 4:
                    cp(EC[h], PTf[cur][par][h], slot(h, 1))
            for h in H:
                mm(slot(h, 2), Pf[cur][par][h], XT32[par][h])
            for h in H:
                cp(EC[h], tA[par][h], slot(h, 2))
                tt(PT[h], XT32[par][h], XT32[par][h], tA[par][h], ALU.add)
        for h in H:
            cp(PT[h], XTh[par][h], XT32[par][h])
        for h in H:
            mm(slot(h, 5), XTh[par][h], identb)
            mm(slot(h, 0), XTh[par][h], Lm[par][h])
            mm(slot(h, 1), Lm[par][h], XTh[par][h])
        for h in H:
            cp(EC[h], Dm[par][h], slot(h, 5))
            cp(EC[h], Nf[par][h], slot(h, 0))
            cp(EC[h], NTf[par][h], slot(h, 1))
            tt(PT[h], Y1[par][h], ident32, NTf[par][h], ALU.subtract)
        for h in H:
            mm(slot(h, 2), NTf[par][h], Nf[par][h])
        for h in H:
            cp(EC[h], N2f[par][h], slot(h, 2))
        for h in H:
            mm(slot(h, 3), N2f[par][h], Y1[par][h])
        for h in H:
            cp(EC[h], tA[par][h], slot(h, 3))
            tt(PT[h], Y1[par][h], Y1[par][h], tA[par][h], ALU.add)
        for h in H:
            mm(slot(h, 4), Dm[par][h], Y1[par][h])
        for h in H:
            cp(EC[h], XTb[par][h], slot(h, 4))
        for h in H:
            mm(slot(h, 3), kbg[par][h], XTb[par][h])
            mm(slot(h, 4), XTb[par][h], vb[par][h])
        for h in H:
            cp(EC[h], wTb[par][h], slot(h, 3))
            cp(EC[h], u32[par][h], slot(h, 4))
        for h in H:
            mm(slot(h, 5), wTb[par][h], Sb[(h, d)])
        for h in H:
            cp(EC[h], tA[par][h], slot(h, 5))
            tt(PT[h], vnew[par][h], u32[par][h], tA[par][h], ALU.subtract)
        for h in H:
            if full:
                mm(slot(h, 6), Sb[(h, d)], qd[par][h], start=True, stop=False)
                mm(slot(h, 6), vnew[par][h], attnT[par][h], start=False, stop=True)
            mm(slot(h, 7), ktl[par][h], vnew[par][h])
        for h in H:
            cp(EC[h], tB[par][h], slot(h, 7))
            stt(S32[(h, d)], S32[(h, d)], Er[par][h][:, lastcol:lastcol + 1], tB[par][h], ALU.mult, ALU.add)
            cp(EC[h], Sb[(h, d)], S32[(h, d)])
        if full:
            if not last_dir:
                for h in H:
                    cp(EC[h], osum[par][:, h, :], slot(h, 6))
                dma(DV(OX[:, :, o0:o0 + P].rearrange("h p t -> p h t"), ("OX", ci)), osum[par])
            else:
                dma(oxl[par], DV(OX[:, :, o0:o0 + P].rearrange("h p t -> p h t"), ("OX", ci)))
                dma(zsl[par], DV(ZS[:, :, o0:o0 + P].rearrange("h p t -> p h t")))
                for h in H:
                    cp(EC[h], osum[par][:, h, :], slot(h, 6))
                    tt(PT[h], osum[par][:, h, :], osum[par][:, h, :], oxl[par][:, h, :], ALU.add)
                tt("pool", osq[par], osum[par], osum[par], ALU.mult)
                nb_ = V(psum[:, 2 * 512:3 * 512], None)
                S.op("pe", lambda e, o=nb_.ap, l=onesb.ap, r_=osq[par].ap.rearrange("p h t -> p (h t)"):
                     e.matmul(o, lhsT=l, rhs=r_, start=True, stop=True),
                     r=[onesb.key, osq[par].key], w=[("bank", 2)])
                olf = V(oln.ap.rearrange("p h t -> p (h t)"), oln.key)
                S.op("act", lambda e, o=olf.ap, i_=nb_.ap, b_=eps_t.ap: e.activation(o, i_, AF.Ln, bias=b_, scale=1.0 / P),
                     r=[("bank", 2), eps_t.key], w=[oln.key])
                act(ors, oln, AF.Exp, scale=-0.5)
                tt("dve", on_, osum[par], ors, ALU.mult)
                stt(oout[par], on_, ggdn_t[:, 0:1], zsl[par], ALU.mult, ALU.mult)
                dma(DV(OM[4:8, :, o0:o0 + P].rearrange("h p t -> p h t")), oout[par])

    seq = [(ci, "X", ci >= CO, False) for ci in range(CO + CW)]
    seq += [(ci, "Y", True, True) for ci in range(CO + CW - 1, CO - 1, -1)]
    if seq:
        load_chunk(seq[0][0], 0, seq[0][2])
    for i, (ci, d, full, last_dir) in enumerate(seq):
        par = i % 2
        if i + 1 < len(seq):
            load_chunk(seq[i + 1][0], (i + 1) % 2, seq[i + 1][2])
        gdn_step(ci, d, par, full, last_dir)
    S.barrier()
    A.reset(mP)

    mP4 = A.mark()
    wo_b = A.new(BF16, [8, D])
    wmi_b = A.new(BF16, [8, DFF])
    m4 = A.mark()
    gaR = A.new(F32, [D])
    dma(gaR, DV(MODS[0].partition_broadcast(P), "MODS"))
    stg4 = [A.new(F32, [DFF]) for _ in range(2)]
    si = 0
    for k in range(8):
        st_ = stg4[si % 2]
        si += 1
        dma(st_[:, 0:D], DV(w_out[k * P:(k + 1) * P, :]))
        tt("dve", wo_b[:, k, :], st_[:, 0:D], gaR, ALU.mult)
    for k in range(8):
        st_ = stg4[si % 2]
        si += 1
        dma(st_, DV(w_mi[k * P:(k + 1) * P, :]))
        cp("act" if k % 2 == 0 else "dve", wmi_b[:, k, :], st_)
    S.barrier()
    A.reset(m4)
    eps_t = A.new(F32, [1])
    memset("dve", eps_t, EPS)
    xt4 = [A.new(F32, [4, D]) for _ in range(2)]
    om_t = [A.new(BF16, [8, TS]) for _ in range(2)]
    xn = A.new(BF16, [4, D])
    junk = A.new(BF16, [D])
    ss4 = A.new(F32, [4])
    sd4 = A.new(F32, [4])
    rstd4 = A.new(F32, [4])
    h2T = [A.new(BF16, [TS]) for _ in range(8)]
    rl = [A.new(F32, [TS]) for _ in range(2)]
    aring = [A.new(BF16, [TS]) for _ in range(4)]
    pj_i[0] = 0

    def load4(t):
        par = t % 2
        dma(xt4[par], DV(xw[t * TS:(t + 1) * TS, :].rearrange("(s p) d -> p s d", p=P)))
        dma(om_t[par], DV(OM[:, :, t * TS:(t + 1) * TS].rearrange("k p t -> p k t")))

    if TW:
        load4(0)
    for t in range(TW):
        par = t % 2
        if t + 1 < TW:
            load4(t + 1)
        x1 = xt4[par]
        for s in range(4):
            for hf in range(2):
                pb = next_bank()
                for k in range(8):
                    mm(pb, om_t[par][:, k, s * P:(s + 1) * P], wo_b[:, k, hf * 512:(hf + 1) * 512],
                       start=(k == 0), stop=(k == 7))
                tt("dve", x1[:, s, hf * 512:(hf + 1) * 512], pb, x1[:, s, hf * 512:(hf + 1) * 512], ALU.add)
        dma(DV(X1[t * TS:(t + 1) * TS, :].rearrange("(s p) d -> p s d", p=P)), x1)
        norm_to_hT(x1, A2, B2, h2T, ss4, sd4, rstd4)
        for f in range(32):
            pb = next_bank()
            for k in range(8):
                mm(pb, wmi_b[:, k, f * P:(f + 1) * P], h2T[k], start=(k == 0), stop=(k == 7))
            r_ = rl[f % 2]
            act(r_, pb, AF.Relu)
            ao = aring[f % 4]
            tt("pool" if f % 2 == 0 else "dve", ao, r_, r_, ALU.mult)
            dma(DV(ACTS[f, :, t * TS:(t + 1) * TS]), ao)
    S.barrier()
    A.reset(mP4)

    wmo_b = A.new(BF16, [32, D])
    A3R = A.new(F32, [D])
    B3R = A.new(F32, [D])
    dma(A3R, DV(MODS[2].partition_broadcast(P), "MODS"))
    dma(B3R, DV(MODS[3].partition_broadcast(P), "MODS"))
    m4 = A.mark()
    gmR = A.new(F32, [D])
    dma(gmR, DV(MODS[1].partition_broadcast(P), "MODS"))
    stg4 = [A.new(F32, [DFF]) for _ in range(2)]
    for k in range(0, 32, 4):
        st_ = stg4[(k // 4) % 2]
        st3 = V(st_.ap.rearrange("p (a c) -> p a c", a=4), st_.key)
        dma(st3, DV(w_mo[k * P:(k + 4) * P, :].rearrange("(a p) c -> p a c", p=P)))
        for a_ in range(4):
            tt("dve" if a_ % 2 == 0 else "pool", wmo_b[:, k + a_, :], st3[:, a_, :], gmR, ALU.mult)
    S.barrier()
    A.reset(m4)
    eps_t = A.new(F32, [1])
    memset("dve", eps_t, EPS)
    xb4 = [A.new(F32, [4, D]) for _ in range(2)]
    at_t = [A.new(BF16, [32, TS]) for _ in range(2)]
    junk = A.new(BF16, [D])
    ss4 = A.new(F32, [4])
    sd4 = A.new(F32, [4])
    rstd4 = A.new(F32, [4])
    pj_i[0] = 0

    def load4b(t):
        par = t % 2
        dma(xb4[par], DV(X1[t * TS:(t + 1) * TS, :].rearrange("(s p) d -> p s d", p=P)))
        dma(at_t[par], DV(ACTS[:, :, t * TS:(t + 1) * TS].rearrange("f p t -> p f t")))

    if TW:
        load4b(0)
    for t in range(TW):
        par = t % 2
        if t + 1 < TW:
            load4b(t + 1)
        x2 = xb4[par]
        for s in range(4):
            for hf in range(2):
                pb = next_bank()
                for f in range(32):
                    mm(pb, at_t[par][:, f, s * P:(s + 1) * P], wmo_b[:, f, hf * 512:(hf + 1) * 512],
                       start=(f == 0), stop=(f == 31))
                tt("dve", x2[:, s, hf * 512:(hf + 1) * 512], pb, x2[:, s, hf * 512:(hf + 1) * 512], ALU.add)
        for s in range(4):
            act(junk, x2[:, s, :], AF.Square, accum=ss4[:, s:s + 1])
        act(sd4, ss4, AF.Sqrt, bias=eps_t, scale=1.0 / D)
        recip(rstd4, sd4)
        for s in range(4):
            stt(x2[:, s, :], x2[:, s, :], rstd4[:, s:s + 1], A3R, ALU.mult, ALU.mult)
            tt("pool", x2[:, s, :], x2[:, s, :], B3R, ALU.add)
        dma(DV(y[t * TS:(t + 1) * TS, :].rearrange("(s p) d -> p s d", p=P)), x2)
    S.barrier()
    S.emit(nc, stack, None if stop_after is None else (S.marks[stop_after] if stop_after < 100 else stop_after))
    stack.close()
    S.arena_log = A.log
    return nc, S


def _arr_pk(v):
    return np.ascontiguousarray(v.reshape(-1, P).T)


def make_consts():
    i = np.arange(P)
    ident = np.eye(P, dtype=np.float32)
    ones = np.ones((P, P), np.float32)
    lcx = (i[:, None] <= i[None, :]).astype(np.float32)
    umx = (i[:, None] > i[None, :]).astype(np.float32)
    lcy = (i[:, None] >= i[None, :]).astype(np.float32)
    umy = (i[:, None] < i[None, :]).astype(np.float32)
    bd = ((i[:, None] // 32) == (i[None, :] // 32)).astype(np.float32)
    return np.ascontiguousarray(np.concatenate([ident, ones, lcx, umx, lcy, umy, bd, 1.0 - bd], axis=1))


def rope_tables(pos):
    inv = (1.0 / (np.float32(10000.0) ** (np.arange(0, 64, 2, dtype=np.float32) / np.float32(64)))).astype(np.float32)
    ang = (pos.astype(np.float32)[None, :] * inv[:, None]).astype(np.float32)
    c = np.cos(ang).astype(np.float32)
    s = np.sin(ang).astype(np.float32)
    cos2 = np.concatenate([c, c, c, c], axis=0)
    sinS = np.concatenate([-s, s, -s, s], axis=0)
    return np.ascontiguousarray(cos2), np.ascontiguousarray(sinS)


def prep_core(W, x_other, x_own, valid_other, pos, cvec, flipped, xdir):
    NO = x_other.shape[0]
    NT = NO + x_own.shape[0]
    m = {}
    m["xo"] = np.ascontiguousarray(x_other, dtype=np.float32)
    m["xw"] = np.ascontiguousarray(x_own, dtype=np.float32)
    m["vmask"] = np.ascontiguousarray(np.broadcast_to(valid_other.astype(np.float32)[None, :], (P, NO)))
    kvalid = np.concatenate([valid_other.astype(np.float32), np.ones(NT - NO, np.float32)])
    kb = np.where(kvalid > 0, 0.0, -30000.0).astype(np.float32)
    m["kbias"] = _arr_pk(kb)
    m["cos2"], m["sinS"] = rope_tables(pos)
    m["cvec"] = _arr_pk(cvec.astype(np.float32))
    m["w_ada"] = W["w_ada"]
    m["w_ada_f"] = W["w_ada_f"]
    m["bada"] = np.ascontiguousarray(np.concatenate([_arr_pk(W["b_ada"]), _arr_pk(W["b_ada_f"])], axis=1))
    m["gains"] = np.ascontiguousarray(np.concatenate([_arr_pk(W["g_mix"]), _arr_pk(W["g_mlp"]), _arr_pk(W["g_final"])], axis=1))
    wi = W["w_in"]
    cq, ckv, kr = wi[:, 0:384], wi[:, 384:640], wi[:, 640:704]
    q, k, v = wi[:, 704:1216], wi[:, 1216:1728], wi[:, 1728:2240]
    z = wi[:, 2240:2752]
    a_f, a_b, b_f, b_b = wi[:, 2752:2756], wi[:, 2756:2760], wi[:, 2760:2764], wi[:, 2764:2768]
    krs = np.concatenate([kr[:, 32:64], kr[:, 0:32]], axis=1)
    if xdir == "f":
        aX, aY, bX, bY = a_f, a_b, b_f, b_b
        alX, alY, dtX, dtY = W["a_log_f"], W["a_log_b"], W["dt_f"], W["dt_b"]
    else:
        aX, aY, bX, bY = a_b, a_f, b_b, b_f
        alX, alY, dtX, dtY = W["a_log_b"], W["a_log_f"], W["dt_b"], W["dt_f"]
    m["w_in"] = np.ascontiguousarray(np.concatenate([cq, ckv, kr, kr, krs, krs, q, k, v, z, aX, aY, bX, bY], axis=1))
    assert m["w_in"].shape[1] == NCOL
    wq = W["w_uq"]
    nope = [wq[:, h * 192:h * 192 + 128] for h in range(4)]
    rope = [wq[:, h * 192 + 128:h * 192 + 192] for h in range(4)]
    ropes = [np.concatenate([r[:, 32:64], r[:, 0:32]], axis=1) for r in rope]
    m["w_uq"] = np.ascontiguousarray(np.concatenate(nope + rope + ropes, axis=1))
    wk = W["w_ukv"]
    kn = [wk[:, h * 256:h * 256 + 128] for h in range(4)]
    vv = [wk[:, h * 256 + 128:h * 256 + 256] for h in range(4)]
    m["w_ukv"] = np.ascontiguousarray(np.concatenate(kn + vv, axis=1))
    cw = W["conv_w"]
    if flipped:
        cw = cw[::-1]
    cwa = np.transpose(cw.reshape(5, 12, P), (2, 1, 0)).reshape(P, 60)
    dtrow = np.concatenate([dtX, dtY])
    alrow = np.concatenate([alX, alY])
    sm = np.zeros((P, 134), np.float32)
    sm[:, 0:3] = _arr_pk(W["g_q"])
    sm[:, 3:5] = _arr_pk(W["g_kv"])
    sm[:, 5] = W["g_gdn"]
    sm[:, 6:38] = np.tile(dtrow, 4)[None, :]
    sm[:, 38:70] = np.tile(alrow, 4)[None, :]
    sm[:, 70:130] = cwa
    m["small"] = sm
    m["consts"] = make_consts()
    m["w_out"] = W["w_out"]
    m["w_mlp_in"] = W["w_mlp_in"]
    m["w_mlp_out"] = W["w_mlp_out"]
    return m


_WKEYS = ["w_ada", "b_ada", "g_mix", "w_in", "g_q", "w_uq", "g_kv", "w_ukv", "conv_w", "a_log_f", "a_log_b",
          "dt_f", "dt_b", "g_gdn", "w_out", "g_mlp", "w_mlp_in", "w_mlp_out"]


def _weights(inputs):
    W = {k: np.ascontiguousarray(np.asarray(inputs[k], dtype=np.float32)[0]) for k in _WKEYS}
    for k in ("w_ada_f", "b_ada_f", "g_final"):
        W[k] = np.ascontiguousarray(np.asarray(inputs[k], dtype=np.float32))
    return W


_CACHE = {}


def kernel(**inputs):
    W = _weights(inputs)
    xp = np.asarray(inputs["x_prompt"], dtype=np.float32)
    xs = np.asarray(inputs["x_sample"], dtype=np.float32)
    cp_ = np.asarray(inputs["c_prompt"], dtype=np.float32)
    cs_ = np.asarray(inputs["c_sample"], dtype=np.float32)
    B, SP_, _ = xp.shape
    BS, SS_, _ = xs.shape
    H = SP_ // 2
    assert SS_ == H and 2 * B + BS == 8
    NO = NW = H
    maps = []
    for b in range(B):
        xf = xp[b, ::-1]
        pos = np.arange(SP_ - 1, -1, -1)
        maps.append(prep_core(W, xf[:H], xf[H:], np.ones(H), pos, cp_[b], True, "b"))
        pos = np.arange(SP_)
        maps.append(prep_core(W, xp[b, :H], xp[b, H:], np.ones(H), pos, cp_[b], False, "f"))
    for b in range(BS):
        pos = np.concatenate([np.zeros(H, np.int64), np.arange(H)])
        maps.append(prep_core(W, np.zeros((H, D), np.float32), xs[b], np.zeros(H), pos, cs_[b], False, "f"))
    key = (NO, NW)
    if key not in _CACHE:
        _CACHE[key] = build(NO, NW)[0]
    nc = _CACHE[key]
    res = run_bass_kernel_spmd(nc, maps, core_ids=list(range(8)))
    yp = np.zeros_like(xp)
    ys = np.zeros_like(xs)
    for b in range(B):
        yp[b, :H] = res.results[2 * b]["y"][::-1]
        yp[b, H:] = res.results[2 * b + 1]["y"]
    for b in range(BS):
        ys[b] = res.results[2 * B + b]["y"]
    return yp, ys
```

```python
import numpy as np
import concourse.bass as bass
import concourse.mybir as mybir
from concourse.bass_utils import run_bass_kernel_spmd

F32 = mybir.dt.float32
BF16 = mybir.dt.bfloat16
AF = mybir.ActivationFunctionType
ALU = mybir.AluOpType
P = 128
TS = 512
EPS = 1e-6
D = 1024
NCOL = 2960
DFF = 4096


class V:
    __slots__ = ("ap", "key")

    def __init__(self, ap, key):
        self.ap = ap
        self.key = key

    def __getitem__(self, idx):
        return V(self.ap[idx], self.key)

    def k(self, key):
        return V(self.ap, key)


class Sched:
    ENGS = ("pe", "act", "dve", "pool", "sp")
    RING = 24

    def __init__(self, same_sync=True):
        self.ops = []
        self.lastw = {}
        self.readers = {}
        self.same_sync = same_sync
        self.last_eng = {}
        self.dma_hist = []
        self.marks = []

    def op(self, eng, fn, r=(), w=(), dma=False):
        i = len(self.ops)
        deps = set()
        for k in r:
            if k is None:
                continue
            lw = self.lastw.get(k)
            if lw is not None:
                deps.add(lw)
        for k in w:
            if k is None:
                continue
            lw = self.lastw.get(k)
            if lw is not None:
                deps.add(lw)
            for j in self.readers.get(k, {}).values():
                deps.add(j)
        import sys as _s
        f = _s._getframe(2)
        self.ops.append(dict(eng=eng, fn=fn, deps=deps, dma=dma, line=(f.f_lineno, f.f_back.f_lineno)))
        for k in r:
            if k is None:
                continue
            self.readers.setdefault(k, {})[("dma", i) if dma else eng] = i
        for k in w:
            if k is None:
                continue
            self.lastw[k] = i
            self.readers[k] = {}
        if dma:
            self.dma_hist.append(i)
        else:
            self.last_eng[eng] = i
        return i

    def barrier(self):
        self.marks.append(len(self.ops) + len(self.ENGS))
        ids = set(self.last_eng.values()) | set(self.dma_hist[-self.RING:])
        for e in self.ENGS:
            i = len(self.ops)
            self.ops.append(dict(eng=e, fn=None, deps=set(ids), dma=False))
            self.last_eng[e] = i
        self.lastw = {}
        self.readers = {}

    def emit(self, nc, stack, limit=None):
        ops = self.ops if limit is None else self.ops[:limit]
        self.ops = ops
        n = len(ops)
        needed = [False] * n
        for i, o in enumerate(ops):
            keep = set()
            for d in o["deps"]:
                od = ops[d]
                if od["fn"] is None:
                    if od["eng"] == o["eng"]:
                        continue
                    keep |= od["deps"]
                    continue
                if (not od["dma"]) and od["eng"] == o["eng"]:
                    if o["dma"]:
                        pass
                    if od["eng"] == "pe" or not self.same_sync or o["dma"] or o["fn"] is None:
                        continue
                keep.add(d)
            o["deps"] = keep
        for o in ops:
            for d in o["deps"]:
                needed[d] = True
        sems = {e: stack.enter_context(nc.semaphore("s_" + e)) for e in self.ENGS}
        ring = [stack.enter_context(nc.semaphore("r%d" % i)) for i in range(self.RING)]
        token = [None] * n
        cnt = {e: 0 for e in self.ENGS}
        ndma = 0
        ring_prev = [None] * self.RING
        for i, o in enumerate(ops):
            if o["fn"] is None:
                continue
            if o["dma"]:
                slot = ndma % self.RING
                val = 16 * (ndma // self.RING + 1)
                token[i] = (ring[slot], val, "r%d" % slot)
                if ring_prev[slot] is not None:
                    o["deps"].add(ring_prev[slot])
                ring_prev[slot] = i
                ndma += 1
            elif needed[i]:
                cnt[o["eng"]] += 1
                token[i] = (sems[o["eng"]], cnt[o["eng"]], "s_" + o["eng"])
        per_eng = {e: [] for e in self.ENGS}
        for i, o in enumerate(ops):
            per_eng[o["eng"]].append(i)
        self.stats = {e: len(per_eng[e]) for e in self.ENGS}
        self.stats["ndma"] = ndma
        block = stack.enter_context(nc.Block())

        def run(eng_name, h):
            waited = {}
            nw = 0
            for i in per_eng[eng_name]:
                o = ops[i]
                for d in sorted(o["deps"]):
                    sem, val, sname = token[d]
                    if waited.get(sname, 0) < val:
                        h.wait_ge(sem, val)
                        waited[sname] = val
                        nw += 1
                if o["fn"] is None:
                    continue
                inst = o["fn"](h)
                if o["dma"]:
                    inst.then_inc(token[i][0], 16)
                elif needed[i]:
                    inst.then_inc(token[i][0], 1)
            self.stats["w_" + eng_name] = nw

        @block.tensor
        def _(h):
            run("pe", h)

        @block.scalar
        def _(h):
            run("act", h)

        @block.vector
        def _(h):
            run("dve", h)

        @block.gpsimd
        def _(h):
            run("pool", h)

        @block.sync
        def _(h):
            run("sp", h)


class Arena:
    def __init__(self, big, nwords, base=0):
        self.big = big
        self.n = nwords
        self.off = base
        self.uid = 0
        self.log = []

    def mark(self):
        return self.off

    def reset(self, m):
        self.off = m

    def new(self, dt, shape, key=None):
        n = int(np.prod(shape))
        words = (n + 1) // 2 if dt == BF16 else n
        words = (words + 1) // 2 * 2
        a = self.off
        self.off += words
        assert self.off <= self.n, ("arena overflow", self.off, self.n)
        v = self.big[:, a:a + words]
        if dt == BF16:
            v = v.bitcast(BF16)
        v = v[:, 0:n]
        if len(shape) == 2:
            v = v.rearrange("p (a b) -> p a b", a=shape[0])
        elif len(shape) == 3:
            v = v.rearrange("p (a b c) -> p a b c", a=shape[0], b=shape[1])
        self.uid += 1
        import sys as _s
        self.log.append((a, words, dt == BF16, tuple(shape), _s._getframe(1).f_lineno, _s._getframe(2).f_lineno))
        return V(v, key if key is not None else ("t", self.uid))


SB_WORDS = 49152 - 2048


def build(NO, NW, stop_after=None):
    NT = NO + NW
    TO, TW = NO // TS, NW // TS
    TT = TO + TW
    NKT = NT // P
    CO, CW = NO // P, NW // P
    nc = bass.Bass("TRN2", target_bir_lowering=False)

    def din(name, shape, dt=F32):
        return nc.dram_tensor(name, list(shape), dt, kind="ExternalInput").ap()

    def dscr(name, shape, dt):
        return nc.dram_tensor(name, list(shape), dt, kind="Internal").ap()

    xo = din("xo", [NO, D])
    xw = din("xw", [NW, D])
    vmask = din("vmask", [P, NO])
    kbias_d = din("kbias", [P, NKT])
    cos_d = din("cos2", [P, NT])
    sin_d = din("sinS", [P, NT])
    cvec = din("cvec", [P, 8])
    w_ada = din("w_ada", [D, 6 * D])
    w_adaf = din("w_ada_f", [D, 2 * D])
    bada = din("bada", [P, 64])
    gains = din("gains", [P, 24])
    w_in = din("w_in", [D, NCOL])
    w_uq = din("w_uq", [384, 1536])
    w_ukv = din("w_ukv", [256, 1024])
    small = din("small", [P, 134])
    consts = din("consts", [P, 1024])
    w_out = din("w_out", [D, D])
    w_mi = din("w_mlp_in", [D, DFF])
    w_mo = din("w_mlp_out", [DFF, D])
    y = nc.dram_tensor("y", [NW, D], F32, kind="ExternalOutput").ap()

    QN = dscr("QN", [4, P, NW], BF16)
    QR = dscr("QR", [4, P, NW], BF16)
    KN = dscr("KN", [4, P, NT], BF16)
    KR = dscr("KR", [P, NT], BF16)
    VV = dscr("VV", [NT, 512], BF16)
    PC = dscr("PC", [12, P, NT + 4], F32)
    ZS = dscr("ZS", [4, P, NW], F32)
    ABt = dscr("ABt", [NT, 16], F32)
    GQ = dscr("GQ", [4, P, NW], BF16)
    GK = dscr("GK", [4, P, NT], BF16)
    GKt = dscr("GKt", [NT, 512], BF16)
    GVt = dscr("GVt", [NT, 512], BF16)
    OX = dscr("OX", [4, P, NW], F32)
    OM = dscr("OM", [8, P, NW], BF16)
    MODS = dscr("MODS", [4, D], F32)
    X1 = dscr("X1", [NW, D], F32)
    ACTS = dscr("ACTS", [32, P, NW], BF16)

    import contextlib
    stack = contextlib.ExitStack()
    big = stack.enter_context(nc.sbuf_tensor("big", [P, SB_WORDS], F32))
    psum = stack.enter_context(nc.psum_tensor("psum", [P, 4096], F32))
    S = Sched(same_sync=True)
    A = Arena(big, SB_WORDS)

    def bank(b, key=None):
        return V(psum[:, b * 512:(b + 1) * 512], key if key is not None else ("bank", b))

    def keys(*vs):
        return [v.key for v in vs if v is not None]

    def mm(out, lhsT, rhs, start=True, stop=True):
        S.op("pe", lambda e: e.matmul(out.ap, lhsT=lhsT.ap, rhs=rhs.ap, start=start, stop=stop),
             r=keys(lhsT, rhs), w=keys(out))

    def tr(out, in_, ident):
        S.op("pe", lambda e: e.transpose(out.ap, in_.ap, ident.ap), r=keys(in_, ident), w=keys(out))

    def act(out, in_, func, bias=None, scale=None, accum=None, extra_r=()):
        kw = {}
        rr = [in_]
        if bias is not None:
            if isinstance(bias, V):
                kw["bias"] = bias.ap
                rr.append(bias)
            else:
                kw["bias"] = float(bias)
        if scale is not None:
            if isinstance(scale, V):
                kw["scale"] = scale.ap
                rr.append(scale)
            else:
                kw["scale"] = float(scale)
        ww = [out]
        if accum is not None:
            kw["accum_out"] = accum.ap
            ww.append(accum)
        S.op("act", lambda e: e.activation(out.ap, in_.ap, func, **kw), r=keys(*rr) + list(extra_r), w=keys(*ww))

    def tt(eng, out, a, b, op):
        S.op(eng, lambda e: e.tensor_tensor(out.ap, a.ap, b.ap, op), r=keys(a, b), w=keys(out))

    def ts(eng, out, a, s1, s2, op0, op1=None):
        rr = [a]
        a1 = s1.ap if isinstance(s1, V) else float(s1)
        if isinstance(s1, V):
            rr.append(s1)
        if s2 is None:
            S.op(eng, lambda e: e.tensor_scalar(out.ap, a.ap, a1, None, op0), r=keys(*rr), w=keys(out))
        else:
            a2 = s2.ap if isinstance(s2, V) else float(s2)
            if isinstance(s2, V):
                rr.append(s2)
            S.op(eng, lambda e: e.tensor_scalar(out.ap, a.ap, a1, a2, op0, op1), r=keys(*rr), w=keys(out))

    def stt(out, a, s, b, op0, op1):
        rr = [a, b]
        a1 = s.ap if isinstance(s, V) else float(s)
        if isinstance(s, V):
            rr.append(s)
        S.op("dve", lambda e: e.scalar_tensor_tensor(out.ap, a.ap, a1, b.ap, op0, op1), r=keys(*rr), w=keys(out))

    def cp(eng, out, in_):
        if eng == "act":
            act(out, in_, AF.Copy)
        else:
            S.op(eng, lambda e: e.tensor_copy(out.ap, in_.ap), r=keys(in_), w=keys(out))

    def recip(out, in_):
        S.op("dve", lambda e: e.reciprocal(out.ap, in_.ap), r=keys(in_), w=keys(out))

    def memset(eng, out, val):
        S.op(eng, lambda e: e.memset(out.ap, val), w=keys(out))

    def dma(out, in_, slow=False):
        if slow:
            S.op("sp", lambda e: e.dma_start(out=out.ap, in_=in_.ap, allow_slow_non_contiguous=True),
                 r=keys(in_), w=keys(out), dma=True)
        else:
            S.op("sp", lambda e: e.dma_start(out=out.ap, in_=in_.ap), r=keys(in_), w=keys(out), dma=True)

    def DV(ap, key=None):
        return V(ap, key)

    cst = A.new(F32, [1024])
    dma(cst, DV(consts))
    ident32 = cst[:, 0:128]
    ones32 = cst[:, 128:256]
    LC = {"X": cst[:, 256:384], "Y": cst[:, 512:640]}
    UM = {"X": cst[:, 384:512], "Y": cst[:, 640:768]}
    BDm = cst[:, 768:896]
    NBDm = cst[:, 896:1024]
    identb = A.new(BF16, [128])
    onesb = A.new(BF16, [128])
    cp("dve", identb, ident32)
    cp("dve", onesb, ones32)
    sm = A.new(F32, [134])
    dma(sm, DV(small))
    gq_t = sm[:, 0:3]
    gkv_t = sm[:, 3:5]
    ggdn_t = sm[:, 5:6]
    dtrow4 = sm[:, 6:38].k(sm.key)
    alrow4 = sm[:, 38:70]
    convw = V(sm.ap[:, 70:130].rearrange("p (c j) -> p c j", c=12), sm.key)
    negA4 = A.new(F32, [32])
    act(negA4, alrow4, AF.Exp)
    ts("dve", negA4, negA4, -1.0, None, ALU.mult)
    gn = A.new(F32, [24])
    dma(gn, DV(gains))
    bd = A.new(F32, [64])
    dma(bd, DV(bada))
    cv = A.new(F32, [8])
    dma(cv, DV(cvec))
    sc = A.new(F32, [8])
    act(sc, cv, AF.Silu)
    modT = A.new(F32, [64])
    A1 = A.new(F32, [8])
    A2 = A.new(F32, [8])
    A3 = A.new(F32, [8])
    m0 = A.mark()
    wst = [A.new(F32, [8, 512]) for _ in range(2)]
    pmod = bank(0)
    for cg in range(16):
        src = w_ada if cg < 12 else w_adaf
        c0 = (cg if cg < 12 else cg - 12) * 512
        wt = wst[cg % 2]
        dma(wt, DV(src[:, c0:c0 + 512].rearrange("(k p) c -> p k c", p=P)))
        for jj in range(4):
            j = cg * 4 + jj
            for k in range(8):
                mm(pmod[:, j:j + 1], wt[:, k, jj * 128:(jj + 1) * 128], sc[:, k:k + 1], start=(k == 0), stop=(k == 7))
    tt("dve", modT, pmod[:, 0:64], bd, ALU.add)
    stt(A1, modT[:, 8:16], 1.0, gn[:, 0:8], ALU.add, ALU.mult)
    stt(A2, modT[:, 32:40], 1.0, gn[:, 8:16], ALU.add, ALU.mult)
    stt(A3, modT[:, 56:64], 1.0, gn[:, 16:24], ALU.add, ALU.mult)
    B1 = modT[:, 0:8]
    B2 = modT[:, 24:32]
    dma(DV(MODS[0].rearrange("(k p) -> p k", p=P), "MODS"), modT[:, 16:24], slow=True)
    dma(DV(MODS[1].rearrange("(k p) -> p k", p=P), "MODS"), modT[:, 40:48], slow=True)
    dma(DV(MODS[2].rearrange("(k p) -> p k", p=P), "MODS"), A3, slow=True)
    dma(DV(MODS[3].rearrange("(k p) -> p k", p=P), "MODS"), modT[:, 48:56], slow=True)
    A.reset(m0)
    S.barrier()
    mP = A.mark()

    win_b = A.new(BF16, [8, NCOL])
    wuq_b = A.new(BF16, [3, 1536])
    wukv_b = A.new(BF16, [2, 1024])
    m1 = A.mark()
    stg = [A.new(F32, [NCOL]) for _ in range(2)]
    for k in range(8):
        st_ = stg[k % 2]
        dma(st_, DV(w_in[k * P:(k + 1) * P, :]))
        cp("act" if k % 2 == 0 else "dve", win_b[:, k, :], st_)
    for k in range(3):
        st_ = stg[k % 2]
        dma(st_[:, 0:1536], DV(w_uq[k * P:(k + 1) * P, :]))
        ts("dve", wuq_b[:, k, :], st_[:, 0:1536], gq_t[:, k:k + 1], 192.0 ** -0.5, ALU.mult, ALU.mult)
    for k in range(2):
        st_ = stg[(k + 1) % 2]
        dma(st_[:, 0:1024], DV(w_ukv[k * P:(k + 1) * P, :]))
        ts("dve", wukv_b[:, k, :], st_[:, 0:1024], gkv_t[:, k:k + 1], None, ALU.mult)
    zt = A.new(F32, [12, 2])
    memset("dve", zt, 0.0)
    dma(DV(PC[:, :, 0:2].rearrange("c p t -> p c t")), zt)
    dma(DV(PC[:, :, NT + 2:NT + 4].rearrange("c p t -> p c t")), zt)
    S.barrier()
    A.reset(m1)

    xt = [A.new(F32, [4, D]) for _ in range(2)]
    cosT = [A.new(F32, [TS]) for _ in range(2)]
    sinT = [A.new(F32, [TS]) for _ in range(2)]
    vm = [A.new(F32, [TS]) for _ in range(2)]
    junk = A.new(BF16, [D])
    ss = [A.new(F32, [4]) for _ in range(2)]
    sd = [A.new(F32, [4]) for _ in range(2)]
    rstd = [A.new(F32, [4]) for _ in range(2)]
    xn = A.new(BF16, [4, D])
    hT = [[A.new(BF16, [TS]) for _ in range(8)] for _ in range(2)]
    pTb = [V(psum[:, 0:256].bitcast(BF16), ("bank", 0)), V(psum[:, 512:768].bitcast(BF16), ("bank", 1))]
    pj_banks = [bank(b) for b in range(2, 8)]
    pj_i = [0]

    def next_bank():
        b = pj_banks[pj_i[0] % len(pj_banks)]
        pj_i[0] += 1
        return b

    cqs = [A.new(F32, [TS]) for _ in range(3)]
    sqb = [A.new(BF16, [TS]) for _ in range(3)]
    cqn = [A.new(BF16, [TS]) for _ in range(3)]
    lnt = A.new(F32, [TS])
    rb = A.new(F32, [TS])
    outb = [A.new(BF16, [TS]) for _ in range(4)]
    outf = [A.new(F32, [TS]) for _ in range(4)]
    t1 = [A.new(F32, [TS]) for _ in range(2)]
    t2 = [A.new(F32, [TS]) for _ in range(2)]
    abt = [A.new(F32, [4, 16]) for _ in range(2)]
    abx = A.new(F32, [4, 8])
    ob_i = [0]
    of_i = [0]

    def nob():
        v = outb[ob_i[0] % 4]
        ob_i[0] += 1
        return v

    def nof():
        v = outf[of_i[0] % 4]
        of_i[0] += 1
        return v

    ev_i = [0]

    def evac_copy(out, in_):
        e = "act" if ev_i[0] % 2 == 0 else "dve"
        ev_i[0] += 1
        cp(e, out, in_)

    def load_tile(t):
        par = t % 2
        own = t >= TO
        src = xw if own else xo
        r0 = (t - TO if own else t) * TS
        dma(xt[par], DV(src[r0:r0 + TS, :].rearrange("(s p) d -> p s d", p=P)))
        dma(cosT[par], DV(cos_d[:, t * TS:(t + 1) * TS]))
        dma(sinT[par], DV(sin_d[:, t * TS:(t + 1) * TS]))
        if not own:
            dma(vm[par], DV(vmask[:, t * TS:(t + 1) * TS]))

    def norm_to_hT(xtile, A_, B_, hT_par, ss_, sd_, rstd_):
        for s in range(4):
            act(junk, xtile[:, s, :], AF.Square, accum=ss_[:, s:s + 1])
        act(sd_, ss_, AF.Sqrt, bias=eps_t, scale=1.0 / D)
        recip(rstd_, sd_)
        for s in range(4):
            ts("dve" if s % 2 == 0 else "pool", xn[:, s, :], xtile[:, s, :], rstd_[:, s:s + 1], None, ALU.mult)
        for k in range(8):
            pt = pTb[k % 2]
            for s in range(4):
                tr(pt[:, s * P:(s + 1) * P], xn[:, s, k * P:(k + 1) * P], identb)
            if k % 2 == 0:
                act(hT_par[k], pt, AF.Identity, bias=B_[:, k:k + 1], scale=A_[:, k:k + 1])
            else:
                ts("dve", hT_par[k], pt, A_[:, k:k + 1], B_[:, k:k + 1], ALU.mult, ALU.add)

    eps_t = A.new(F32, [1])
    memset("dve", eps_t, EPS)
    lnq_t = A.new(F32, [1])
    memset("dve", lnq_t, float(np.log(128.0 ** -0.5)))

    def proj(c, par, width=P):
        pb = next_bank()
        for k in range(8):
            mm(pb[0:width, :], win_b[:, k, c * P:c * P + width], hT[par][k], start=(k == 0), stop=(k == 7))
        return pb

    def rms_bc(chunks, nfeat):
        n = len(chunks)
        for k in range(n):
            tt("pool", sqb[k], chunks[k], chunks[k], ALU.mult)
        pb = next_bank()
        for k in range(n):
            mm(pb, onesb, sqb[k], start=(k == 0), stop=(k == n - 1))
        act(lnt, pb, AF.Ln, bias=eps_t, scale=1.0 / nfeat)
        act(rb, lnt, AF.Exp, scale=-0.5)

    load_tile(0)
    for t in range(TT):
        par = t % 2
        own = t >= TO
        tok0 = t * TS
        ot0 = (t - TO) * TS
        if t + 1 < TT:
            load_tile(t + 1)
        norm_to_hT(xt[par], A1, B1, hT[par], ss[par], sd[par], rstd[par])
        if own:
            for k in range(3):
                pb = proj(k, par)
                evac_copy(cqs[k], pb)
            rms_bc(cqs, 384)
            for k in range(3):
                tt("dve", cqn[k], cqs[k], rb, ALU.mult)
            for g in range(4):
                pb = next_bank()
                for k in range(3):
                    mm(pb, wuq_b[:, k, g * P:(g + 1) * P], cqn[k], start=(k == 0), stop=(k == 2))
                o = nob()
                evac_copy(o, pb)
                dma(DV(QN[g, :, ot0:ot0 + TS]), o)
            for pr in range(4):
                pb = next_bank()
                for k in range(3):
                    mm(pb, wuq_b[:, k, (4 + pr) * P:(5 + pr) * P], cqn[k], start=(k == 0), stop=(k == 2))
                tt("dve", t1[pr % 2], pb, cosT[par], ALU.mult)
                pb2 = next_bank()
                for k in range(3):
                    mm(pb2, wuq_b[:, k, (8 + pr) * P:(9 + pr) * P], cqn[k], start=(k == 0), stop=(k == 2))
                tt("dve", t2[pr % 2], pb2, sinT[par], ALU.mult)
                o = nob()
                tt("pool", o, t1[pr % 2], t2[pr % 2], ALU.add)
                dma(DV(QR[pr, :, ot0:ot0 + TS]), o)
        for k in range(2):
            pb = proj(3 + k, par)
            evac_copy(cqs[k], pb)
        rms_bc(cqs[0:2], 256)
        for k in range(2):
            tt("dve", cqn[k], cqs[k], rb, ALU.mult)
        for h in range(4):
            pb = next_bank()
            for k in range(2):
                mm(pb, wukv_b[:, k, h * P:(h + 1) * P], cqn[k], start=(k == 0), stop=(k == 1))
            o = nob()
            evac_copy(o, pb)
            dma(DV(KN[h, :, tok0:tok0 + TS]), o)
        for s in range(4):
            pb = next_bank()
            for k in range(2):
                mm(pb, cqn[k][:, s * P:(s + 1) * P], wukv_b[:, k, 512:1024], start=(k == 0), stop=(k == 1))
            o = nob()
            evac_copy(o, pb)
            dma(DV(VV[tok0 + s * P:tok0 + (s + 1) * P, :]), o)
        pb = proj(5, par)
        tt("dve", t1[0], pb, cosT[par], ALU.mult)
        pb2 = proj(6, par)
        tt("dve", t2[0], pb2, sinT[par], ALU.mult)
        o = nob()
        tt("pool", o, t1[0], t2[0], ALU.add)
        dma(DV(KR[:, tok0:tok0 + TS]), o)
        if own:
            cts = list(range(12))
        elif t == TO - 1:
            cts = list(range(12))
        else:
            cts = list(range(4, 12))
        for ct in cts:
            pb = proj(7 + ct, par)
            o = nof()
            if own:
                evac_copy(o, pb)
            else:
                tt("dve", o, pb, vm[par], ALU.mult)
            dma(DV(PC[ct, :, 2 + tok0:2 + tok0 + TS]), o)
        if own:
            for zc in range(4):
                pb = proj(19 + zc, par)
                o = nof()
                act(o, pb, AF.Silu)
                dma(DV(ZS[zc, :, ot0:ot0 + TS]), o)
        pb = next_bank()
        for s in range(4):
            for k in range(8):
                mm(pb[:, s * 16:(s + 1) * 16], hT[par][k][:, s * P:(s + 1) * P], win_b[:, k, 2944:2960],
                   start=(k == 0), stop=(k == 7))
        pv = V(pb.ap[:, 0:64].rearrange("p (s c) -> p s c", s=4), pb.key)
        ab_ = abt[par]
        dt4 = V(dtrow4.ap.rearrange("p (s c) -> p s c", s=4), dtrow4.key)
        na4 = V(negA4.ap.rearrange("p (s c) -> p s c", s=4), negA4.key)
        tt("dve", abx, pv[:, :, 0:8], dt4, ALU.add)
        act(abx, abx, AF.Exp)
        act(abx, abx, AF.Ln, bias=1.0)
        tt("dve", ab_[:, :, 0:8], abx, na4, ALU.mult)
        act(ab_[:, :, 8:16], pv[:, :, 8:16], AF.Sigmoid)
        dma(DV(ABt[tok0:tok0 + TS, :].rearrange("(s p) c -> p s c", p=P)), ab_)
    S.barrier()
    A.reset(mP)

    pcx = [A.new(F32, [TS + 4]) for _ in range(2)]
    acc = [A.new(F32, [TS]) for _ in range(2)]
    sl = [A.new(F32, [TS]) for _ in range(2)]
    slb = [A.new(BF16, [TS]) for _ in range(2)]
    sq1 = A.new(BF16, [TS])
    ln1 = A.new(F32, [TS])
    rs1 = A.new(F32, [TS])
    nb = [A.new(BF16, [TS]) for _ in range(2)]
    tok = [A.new(BF16, [4, P]) for _ in range(2)]
    vml = A.new(F32, [TS])
    if TO > 0:
        dma(vml, DV(vmask[:, (TO - 1) * TS:TO * TS]))
    eps_t = A.new(F32, [1])
    memset("dve", eps_t, EPS)
    lnq_t = A.new(F32, [1])
    memset("dve", lnq_t, float(np.log(128.0 ** -0.5)))
    zero_t = A.new(F32, [1])
    memset("dve", zero_t, 0.0)
    items = []
    for t in range(TT):
        own = t >= TO
        for ct in (range(12) if own else range(4, 12)):
            items.append((t, ct))

    def load_pc(i):
        t, ct = items[i]
        dma(pcx[i % 2], DV(PC[ct, :, t * TS:t * TS + TS + 4]))

    if items:
        load_pc(0)
    pj_i[0] = 0
    for i, (t, ct) in enumerate(items):
        par = i % 2
        if i + 1 < len(items):
            load_pc(i + 1)
        tok0 = t * TS
        ot0 = (t - TO) * TS
        px = pcx[par]
        ac = acc[par]
        ts("dve", ac, px[:, 0:TS], convw[:, ct, 0:1], None, ALU.mult)
        for j in range(1, 5):
            stt(ac, px[:, j:j + TS], convw[:, ct, j:j + 1], ac, ALU.mult, ALU.add)
        h = ct % 4
        if ct < 8:
            s_ = sl[par]
            act(s_, ac, AF.Silu)
            if t == TO - 1:
                tt("pool", s_, s_, vml, ALU.mult)
            tt("pool", sq1, s_, s_, ALU.mult)
            pb = next_bank()
            mm(pb, onesb, sq1)
            act(ln1, pb, AF.Ln, bias=eps_t, scale=1.0)
            act(rs1, ln1, AF.Exp, scale=-0.5, bias=(lnq_t if ct < 4 else zero_t))
            n_ = nb[par]
            tt("dve", n_, s_, rs1, ALU.mult)
            if ct < 4:
                dma(DV(GQ[h, :, ot0:ot0 + TS]), n_)
                continue
            dma(DV(GK[h, :, tok0:tok0 + TS]), n_)
            srcb = n_
            dst = GKt
        else:
            s_ = slb[par]
            act(s_, ac, AF.Silu)
            if t == TO - 1:
                tt("pool", s_, s_, vml, ALU.mult)
            srcb = s_
            dst = GVt
        pt = pTb[par]
        for s in range(4):
            tr(pt[:, s * P:(s + 1) * P], srcb[:, s * P:(s + 1) * P], identb)
        tk = tok[par]
        cp("act", tk, V(pt.ap.rearrange("p (s c) -> p s c", s=4), pt.key))
        dma(DV(dst[tok0:tok0 + TS, h * P:(h + 1) * P].rearrange("(s p) c -> p s c", p=P)), tk)
    S.barrier()
    A.reset(mP)

    kb_t = A.new(F32, [NKT])
    dma(kb_t, DV(kbias_d))
    kr_t = A.new(BF16, [NT])
    dma(kr_t, DV(KR))
    kn_t = A.new(BF16, [NT])
    v_t = A.new(BF16, [NKT, P])
    qn_t = [A.new(BF16, [TS]) for _ in range(2)]
    qr_t = [A.new(BF16, [TS]) for _ in range(2)]
    pT_t = [A.new(BF16, [TS]) for _ in range(3)]
    racc = A.new(F32, [TS])
    rec = A.new(F32, [TS])
    oo = [A.new(BF16, [TS]) for _ in range(2)]
    sbanks = [bank(0), bank(1), bank(2)]
    obanks = [bank(3), bank(4)]
    dbank = bank(5)
    it = 0
    for h in range(4):
        dma(kn_t, DV(KN[h]))
        dma(v_t, DV(VV[:, h * P:(h + 1) * P].rearrange("(j p) c -> p j c", p=P)))
        b0 = (h % 2) * 64
        for qt in range(TW):
            qp = it % 2
            it += 1
            dma(qn_t[qp], DV(QN[h, :, qt * TS:(qt + 1) * TS]))
            dma(qr_t[qp], DV(QR[h, :, qt * TS:(qt + 1) * TS]))
            ob = obanks[qp]
            def qk(j, qp=qp, b0=b0):
                sb_ = sbanks[j % 3]
                mm(sb_, kn_t[:, j * P:(j + 1) * P], qn_t[qp], start=True, stop=False)
                mm(sb_, kr_t[:, j * P:(j + 1) * P], qr_t[qp], start=False, stop=True)

            qk(0)
            if NKT > 1:
                qk(1)
            for j in range(NKT):
                sb_ = sbanks[j % 3]
                pt = pT_t[j % 3]
                act(pt, sb_, AF.Exp, bias=kb_t[:, j:j + 1])
                if j + 2 < NKT:
                    qk(j + 2)
                mm(ob, v_t[:, j, :], pt, start=(j == 0), stop=(j == NKT - 1))
                if j == 0:
                    cp("dve", racc, pt)
                else:
                    tt("dve", racc, racc, pt, ALU.add)
            mm(dbank, ones32, racc)
            recip(rec, dbank)
            o = oo[qp]
            tt("dve", o, ob, rec, ALU.mult)
            dma(DV(OM[h, :, qt * TS:(qt + 1) * TS]), o)
    S.barrier()
    A.reset(mP)

    def slot(h, i):
        b = 2 * h + i // 4
        o = (i % 4) * P
        return V(psum[:, b * 512 + o:b * 512 + o + P], ("bank", b))

    S32 = {(h, d): A.new(F32, [P]) for h in range(4) for d in "XY"}
    Sb = {(h, d): A.new(BF16, [P]) for h in range(4) for d in "XY"}
    for kk in S32:
        memset("dve", S32[kk], 0.0)
        memset("pool", Sb[kk], 0.0)
    eps_t = A.new(F32, [1])
    memset("dve", eps_t, EPS)
    NPAR = 2
    kT4 = [A.new(BF16, [4, P]) for _ in range(NPAR)]
    qT4 = [A.new(BF16, [4, P]) for _ in range(NPAR)]
    ktok = [A.new(BF16, [512]) for _ in range(NPAR)]
    vtok = [A.new(BF16, [512]) for _ in range(NPAR)]
    ab = [A.new(F32, [16]) for _ in range(NPAR)]
    oxl = [A.new(F32, [4, P]) for _ in range(NPAR)]
    zsl = [A.new(F32, [4, P]) for _ in range(NPAR)]
    egc = [A.new(F32, [4]) for _ in range(NPAR)]
    bege = [A.new(F32, [4]) for _ in range(NPAR)]
    etail = [A.new(F32, [4]) for _ in range(NPAR)]
    negb = [A.new(F32, [4]) for _ in range(NPAR)]

    def per_head(dt, n=NPAR):
        return [[A.new(dt, [P]) for _ in range(4)] for _ in range(n)]

    gUM = per_head(F32)
    gOn = per_head(F32)
    dec = per_head(F32)
    dm = per_head(F32)
    P0f = per_head(F32)
    decT = per_head(F32)
    dTm = per_head(F32)
    Er = per_head(F32)
    XT32 = per_head(F32)
    u32 = per_head(F32)
    Pf = [per_head(F32), per_head(F32)]
    PTf = [per_head(F32), per_head(F32)]
    Lm = per_head(BF16)
    Dm = per_head(BF16)
    Nf = per_head(BF16)
    NTf = per_head(BF16)
    N2f = per_head(BF16)
    Y1 = per_head(BF16)
    XTh = per_head(BF16)
    tA = per_head(F32)
    tB = per_head(F32)
    XTb = per_head(BF16)
    attnT = per_head(BF16)
    qd = per_head(BF16)
    kbg = per_head(BF16)
    vb = per_head(BF16)
    ktl = per_head(BF16)
    wTb = per_head(BF16)
    vnew = per_head(BF16)
    osum = [A.new(F32, [4, P]) for _ in range(NPAR)]
    osq = [A.new(BF16, [4, P]) for _ in range(NPAR)]
    oln = A.new(F32, [4, P])
    ors = A.new(F32, [4, P])
    on_ = A.new(F32, [4, P])
    oout = [A.new(BF16, [4, P]) for _ in range(NPAR)]

    def load_chunk(ci, par, full):
        c0 = ci * P
        dma(kT4[par], DV(GK[:, :, c0:c0 + P].rearrange("h p t -> p h t")))
        dma(ktok[par], DV(GKt[c0:c0 + P, :]))
        dma(vtok[par], DV(GVt[c0:c0 + P, :]))
        dma(ab[par], DV(ABt[c0:c0 + P, :]))
        if full:
            o0 = c0 - NO
            dma(qT4[par], DV(GQ[:, :, o0:o0 + P].rearrange("h p t -> p h t")))

    def gdn_step(ci, d, par, full, last_dir):
        goff = 0 if d == "X" else 4
        boff = 8 if d == "X" else 12
        lc, um = LC[d], UM[d]
        lastcol = P - 1 if d == "X" else 0
        o0 = ci * P - NO
        a_ = ab[par]
        g4 = a_[:, goff:goff + 4]
        b4 = a_[:, boff:boff + 4]
        gc_ps = slot(0, 6)
        gt_ps = slot(0, 7)
        mm(gc_ps[:, 0:4], lc, g4)
        mm(gt_ps[:, 0:4], um, g4)
        act(egc[par], gc_ps[:, 0:4], AF.Exp)
        act(etail[par], gt_ps[:, 0:4], AF.Exp)
        tt("dve", bege[par], b4, egc[par], ALU.mult)
        ts("pool", negb[par], b4, -1.0, None, ALU.mult)
        H = range(4)
        EC = ["act", "dve", "act", "dve"]
        PT = ["pool", "pool", "pool", "dve"]
        for h in H:
            ts("pool", gUM[par][h], um, g4[:, h:h + 1], None, ALU.mult)
            ts("pool", gOn[par][h], ones32, g4[:, h:h + 1], None, ALU.mult)
        for h in H:
            mm(slot(h, 0), lc, gUM[par][h])
            if full:
                mm(slot(h, 1), gUM[par][h], lc)
            mm(slot(h, 2), gOn[par][h], lc)
            mm(slot(h, 3), kT4[par][:, h, :], kT4[par][:, h, :])
            if full:
                mm(slot(h, 4), kT4[par][:, h, :], qT4[par][:, h, :])
        for h in H:
            act(dec[par][h], slot(h, 0), AF.Exp)
            tt(PT[h], dm[par][h], dec[par][h], um, ALU.mult)
            stt(P0f[par][h], slot(h, 3), negb[par][:, h:h + 1], dm[par][h], ALU.mult, ALU.mult)
            tt(PT[h], Pf[0][par][h], P0f[par][h], BDm, ALU.mult)
            tt(PT[h], Lm[par][h], Pf[0][par][h], P0f[par][h], ALU.subtract)
            mm(slot(h, 5), Pf[0][par][h], ident32)
            cp(EC[h], PTf[0][par][h], slot(h, 5))
            tt(PT[h], XT32[par][h], PTf[0][par][h], ident32, ALU.add)
            act(Er[par][h], slot(h, 2), AF.Exp)
            if full:
                act(decT[par][h], slot(h, 1), AF.Exp)
                tt(PT[h], dTm[par][h], decT[par][h], lc, ALU.mult)
                cp(EC[h], tB[par][h], slot(h, 4))
                tt(PT[h], attnT[par][h], tB[par][h], dTm[par][h], ALU.mult)
                tt(PT[h], qd[par][h], qT4[par][:, h, :], Er[par][h], ALU.mult)
            act(kbg[par][h], ktok[par][:, h * P:(h + 1) * P], AF.Identity, scale=bege[par][:, h:h + 1])
            act(vb[par][h], vtok[par][:, h * P:(h + 1) * P], AF.Identity, scale=b4[:, h:h + 1])
            act(ktl[par][h], ktok[par][:, h * P:(h + 1) * P], AF.Identity, scale=etail[par][:, h:h + 1])
        for j in range(1, 5):
            cur, prv = j % 2, (j - 1) % 2
            for h in H:
                mm(slot(h, 0), PTf[prv][par][h], Pf[prv][par][h])
                if j < 4:
                    mm(slot(h, 1), Pf[prv][par][h], PTf[prv][par][h])
            for h in H:
                cp(EC[h], Pf[cur][par][h], slot(h, 0))
                if j < 4:
                    cp(EC[h], PTf[cur][par][h], slot(h, 1))
            for h in H:
                mm(slot(h, 2), Pf[cur][par][h], XT32[par][h])
            for h in H:
                cp(EC[h], tA[par][h], slot(h, 2))
                tt(PT[h], XT32[par][h], XT32[par][h], tA[par][h], ALU.add)
        for h in H:
            cp(PT[h], XTh[par][h], XT32[par][h])
        for h in H:
            mm(slot(h, 5), XTh[par][h], identb)
            mm(slot(h, 0), XTh[par][h], Lm[par][h])
            mm(slot(h, 1), Lm[par][h], XTh[par][h])
        for h in H:
            cp(EC[h], Dm[par][h], slot(h, 5))
            cp(EC[h], Nf[par][h], slot(h, 0))
            cp(EC[h], NTf[par][h], slot(h, 1))
            tt(PT[h], Y1[par][h], ident32, NTf[par][h], ALU.subtract)
        for h in H:
            mm(slot(h, 2), NTf[par][h], Nf[par][h])
        for h in H:
            cp(EC[h], N2f[par][h], slot(h, 2))
        for h in H:
            mm(slot(h, 3), N2f[par][h], Y1[par][h])
        for h in H:
            cp(EC[h], tA[par][h], slot(h, 3))
            tt(PT[h], Y1[par][h], Y1[par][h], tA[par][h], ALU.add)
        for h in H:
            mm(slot(h, 4), Dm[par][h], Y1[par][h])
        for h in H:
            cp(EC[h], XTb[par][h], slot(h, 4))
        for h in H:
            mm(slot(h, 3), kbg[par][h], XTb[par][h])
            mm(slot(h, 4), XTb[par][h], vb[par][h])
        for h in H:
            cp(EC[h], wTb[par][h], slot(h, 3))
            cp(EC[h], u32[par][h], slot(h, 4))
        for h in H:
            mm(slot(h, 5), wTb[par][h], Sb[(h, d)])
        for h in H:
            cp(EC[h], tA[par][h], slot(h, 5))
            tt(PT[h], vnew[par][h], u32[par][h], tA[par][h], ALU.subtract)
        for h in H:
            if full:
                mm(slot(h, 6), Sb[(h, d)], qd[par][h], start=True, stop=False)
                mm(slot(h, 6), vnew[par][h], attnT[par][h], start=False, stop=True)
            mm(slot(h, 7), ktl[par][h], vnew[par][h])
        for h in H:
            cp(EC[h], tB[par][h], slot(h, 7))
            stt(S32[(h, d)], S32[(h, d)], Er[par][h][:, lastcol:lastcol + 1], tB[par][h], ALU.mult, ALU.add)
            cp(EC[h], Sb[(h, d)], S32[(h, d)])
        if full:
            if not last_dir:
                for h in H:
                    cp(EC[h], osum[par][:, h, :], slot(h, 6))
                dma(DV(OX[:, :, o0:o0 + P].rearrange("h p t -> p h t"), ("OX", ci)), osum[par])
            else:
                dma(oxl[par], DV(OX[:, :, o0:o0 + P].rearrange("h p t -> p h t"), ("OX", ci)))
                dma(zsl[par], DV(ZS[:, :, o0:o0 + P].rearrange("h p t -> p h t")))
                for h in H:
                    cp(EC[h], osum[par][:, h, :], slot(h, 6))
                    tt(PT[h], osum[par][:, h, :], osum[par][:, h, :], oxl[par][:, h, :], ALU.add)
                tt("pool", osq[par], osum[par], osum[par], ALU.mult)
                nb_ = V(psum[:, 2 * 512:3 * 512], None)
                S.op("pe", lambda e, o=nb_.ap, l=onesb.ap, r_=osq[par].ap.rearrange("p h t -> p (h t)"):
                     e.matmul(o, lhsT=l, rhs=r_, start=True, stop=True),
                     r=[onesb.key, osq[par].key], w=[("bank", 2)])
                olf = V(oln.ap.rearrange("p h t -> p (h t)"), oln.key)
                S.op("act", lambda e, o=olf.ap, i_=nb_.ap, b_=eps_t.ap: e.activation(o, i_, AF.Ln, bias=b_, scale=1.0 / P),
                     r=[("bank", 2), eps_t.key], w=[oln.key])
                act(ors, oln, AF.Exp, scale=-0.5)
                tt("dve", on_, osum[par], ors, ALU.mult)
                stt(oout[par], on_, ggdn_t[:, 0:1], zsl[par], ALU.mult, ALU.mult)
                dma(DV(OM[4:8, :, o0:o0 + P].rearrange("h p t -> p h t")), oout[par])

    seq = [(ci, "X", ci >= CO, False) for ci in range(CO + CW)]
    seq += [(ci, "Y", True, True) for ci in range(CO + CW - 1, CO - 1, -1)]
    if seq:
        load_chunk(seq[0][0], 0, seq[0][2])
    for i, (ci, d, full, last_dir) in enumerate(seq):
        par = i % 2
        if i + 1 < len(seq):
            load_chunk(seq[i + 1][0], (i + 1) % 2, seq[i + 1][2])
        gdn_step(ci, d, par, full, last_dir)
    S.barrier()
    A.reset(mP)

    mP4 = A.mark()
    wo_b = A.new(BF16, [8, D])
    wmi_b = A.new(BF16, [8, DFF])
    m4 = A.mark()
    gaR = A.new(F32, [D])
    dma(gaR, DV(MODS[0].partition_broadcast(P), "MODS"))
    stg4 = [A.new(F32, [DFF]) for _ in range(2)]
    si = 0
    for k in range(8):
        st_ = stg4[si % 2]
        si += 1
        dma(st_[:, 0:D], DV(w_out[k * P:(k + 1) * P, :]))
        tt("dve", wo_b[:, k, :], st_[:, 0:D], gaR, ALU.mult)
    for k in range(8):
        st_ = stg4[si % 2]
        si += 1
        dma(st_, DV(w_mi[k * P:(k + 1) * P, :]))
        cp("act" if k % 2 == 0 else "dve", wmi_b[:, k, :], st_)
    S.barrier()
    A.reset(m4)
    eps_t = A.new(F32, [1])
    memset("dve", eps_t, EPS)
    xt4 = [A.new(F32, [4, D]) for _ in range(2)]
    om_t = [A.new(BF16, [8, TS]) for _ in range(2)]
    xn = A.new(BF16, [4, D])
    junk = A.new(BF16, [D])
    ss4 = A.new(F32, [4])
    sd4 = A.new(F32, [4])
    rstd4 = A.new(F32, [4])
    h2T = [A.new(BF16, [TS]) for _ in range(8)]
    rl = [A.new(F32, [TS]) for _ in range(2)]
    aring = [A.new(BF16, [TS]) for _ in range(4)]
    pj_i[0] = 0

    def load4(t):
        par = t % 2
        dma(xt4[par], DV(xw[t * TS:(t + 1) * TS, :].rearrange("(s p) d -> p s d", p=P)))
        dma(om_t[par], DV(OM[:, :, t * TS:(t + 1) * TS].rearrange("k p t -> p k t")))

    if TW:
        load4(0)
    for t in range(TW):
        par = t % 2
        if t + 1 < TW:
            load4(t + 1)
        x1 = xt4[par]
        for s in range(4):
            for hf in range(2):
                pb = next_bank()
                for k in range(8):
                    mm(pb, om_t[par][:, k, s * P:(s + 1) * P], wo_b[:, k, hf * 512:(hf + 1) * 512],
                       start=(k == 0), stop=(k == 7))
                tt("dve", x1[:, s, hf * 512:(hf + 1) * 512], pb, x1[:, s, hf * 512:(hf + 1) * 512], ALU.add)
        dma(DV(X1[t * TS:(t + 1) * TS, :].rearrange("(s p) d -> p s d", p=P)), x1)
        norm_to_hT(x1, A2, B2, h2T, ss4, sd4, rstd4)
        for f in range(32):
            pb = next_bank()
            for k in range(8):
                mm(pb, wmi_b[:, k, f * P:(f + 1) * P], h2T[k], start=(k == 0), stop=(k == 7))
            r_ = rl[f % 2]
            act(r_, pb, AF.Relu)
            ao = aring[f % 4]
            tt("pool" if f % 2 == 0 else "dve", ao, r_, r_, ALU.mult)
            dma(DV(ACTS[f, :, t * TS:(t + 1) * TS]), ao)
    S.barrier()
    A.reset(mP4)

    wmo_b = A.new(BF16, [32, D])
    A3R = A.new(F32, [D])
    B3R = A.new(F32, [D])
    dma(A3R, DV(MODS[2].partition_broadcast(P), "MODS"))
    dma(B3R, DV(MODS[3].partition_broadcast(P), "MODS"))
    m4 = A.mark()
    gmR = A.new(F32, [D])
    dma(gmR, DV(MODS[1].partition_broadcast(P), "MODS"))
    stg4 = [A.new(F32, [DFF]) for _ in range(2)]
    for k in range(0, 32, 4):
        st_ = stg4[(k // 4) % 2]
        st3 = V(st_.ap.rearrange("p (a c) -> p a c", a=4), st_.key)
        dma(st3, DV(w_mo[k * P:(k + 4) * P, :].rearrange("(a p) c -> p a c", p=P)))
        for a_ in range(4):
            tt("dve" if a_ % 2 == 0 else "pool", wmo_b[:, k + a_, :], st3[:, a_, :], gmR, ALU.mult)
    S.barrier()
    A.reset(m4)
    eps_t = A.new(F32, [1])
    memset("dve", eps_t, EPS)
    xb4 = [A.new(F32, [4, D]) for _ in range(2)]
    at_t = [A.new(BF16, [32, TS]) for _ in range(2)]
    junk = A.new(BF16, [D])
    ss4 = A.new(F32, [4])
    sd4 = A.new(F32, [4])
    rstd4 = A.new(F32, [4])
    pj_i[0] = 0

    def load4b(t):
        par = t % 2
        dma(xb4[par], DV(X1[t * TS:(t + 1) * TS, :].rearrange("(s p) d -> p s d", p=P)))
        dma(at_t[par], DV(ACTS[:, :, t * TS:(t + 1) * TS].rearrange("f p t -> p f t")))

    if TW:
        load4b(0)
    for t in range(TW):
        par = t % 2
        if t + 1 < TW:
            load4b(t + 1)
        x2 = xb4[par]
        for s in range(4):
            for hf in range(2):
                pb = next_bank()
                for f in range(32):
                    mm(pb, at_t[par][:, f, s * P:(s + 1) * P], wmo_b[:, f, hf * 512:(hf + 1) * 512],
                       start=(f == 0), stop=(f == 31))
                tt("dve", x2[:, s, hf * 512:(hf + 1) * 512], pb, x2[:, s, hf * 512:(hf + 1) * 512], ALU.add)
        for s in range(4):
            act(junk, x2[:, s, :], AF.Square, accum=ss4[:, s:s + 1])
        act(sd4, ss4, AF.Sqrt, bias=eps_t, scale=1.0 / D)
        recip(rstd4, sd4)
        for s in range(4):
            stt(x2[:, s, :], x2[:, s, :], rstd4[:, s:s + 1], A3R, ALU.mult, ALU.mult)
            tt("pool", x2[:, s, :], x2[:, s, :], B3R, ALU.add)
        dma(DV(y[t * TS:(t + 1) * TS, :].rearrange("(s p) d -> p s d", p=P)), x2)
    S.barrier()
    S.emit(nc, stack, None if stop_after is None else (S.marks[stop_after] if stop_after < 100 else stop_after))
    stack.close()
    S.arena_log = A.log
    return nc, S


def _arr_pk(v):
    return np.ascontiguousarray(v.reshape(-1, P).T)


def make_consts():
    i = np.arange(P)
    ident = np.eye(P, dtype=np.float32)
    ones = np.ones((P, P), np.float32)
    lcx = (i[:, None] <= i[None, :]).astype(np.float32)
    umx = (i[:, None] > i[None, :]).astype(np.float32)
    lcy = (i[:, None] >= i[None, :]).astype(np.float32)
    umy = (i[:, None] < i[None, :]).astype(np.float32)
    bd = ((i[:, None] // 32) == (i[None, :] // 32)).astype(np.float32)
    return np.ascontiguousarray(np.concatenate([ident, ones, lcx, umx, lcy, umy, bd, 1.0 - bd], axis=1))


def rope_tables(pos):
    inv = (1.0 / (np.float32(10000.0) ** (np.arange(0, 64, 2, dtype=np.float32) / np.float32(64)))).astype(np.float32)
    ang = (pos.astype(np.float32)[None, :] * inv[:, None]).astype(np.float32)
    c = np.cos(ang).astype(np.float32)
    s = np.sin(ang).astype(np.float32)
    cos2 = np.concatenate([c, c, c, c], axis=0)
    sinS = np.concatenate([-s, s, -s, s], axis=0)
    return np.ascontiguousarray(cos2), np.ascontiguousarray(sinS)


def prep_core(W, x_other, x_own, valid_other, pos, cvec, flipped, xdir):
    NO = x_other.shape[0]
    NT = NO + x_own.shape[0]
    m = {}
    m["xo"] = np.ascontiguousarray(x_other, dtype=np.float32)
    m["xw"] = np.ascontiguousarray(x_own, dtype=np.float32)
    m["vmask"] = np.ascontiguousarray(np.broadcast_to(valid_other.astype(np.float32)[None, :], (P, NO)))
    kvalid = np.concatenate([valid_other.astype(np.float32), np.ones(NT - NO, np.float32)])
    kb = np.where(kvalid > 0, 0.0, -30000.0).astype(np.float32)
    m["kbias"] = _arr_pk(kb)
    m["cos2"], m["sinS"] = rope_tables(pos)
    m["cvec"] = _arr_pk(cvec.astype(np.float32))
    m["w_ada"] = W["w_ada"]
    m["w_ada_f"] = W["w_ada_f"]
    m["bada"] = np.ascontiguousarray(np.concatenate([_arr_pk(W["b_ada"]), _arr_pk(W["b_ada_f"])], axis=1))
    m["gains"] = np.ascontiguousarray(np.concatenate([_arr_pk(W["g_mix"]), _arr_pk(W["g_mlp"]), _arr_pk(W["g_final"])], axis=1))
    wi = W["w_in"]
    cq, ckv, kr = wi[:, 0:384], wi[:, 384:640], wi[:, 640:704]
    q, k, v = wi[:, 704:1216], wi[:, 1216:1728], wi[:, 1728:2240]
    z = wi[:, 2240:2752]
    a_f, a_b, b_f, b_b = wi[:, 2752:2756], wi[:, 2756:2760], wi[:, 2760:2764], wi[:, 2764:2768]
    krs = np.concatenate([kr[:, 32:64], kr[:, 0:32]], axis=1)
    if xdir == "f":
        aX, aY, bX, bY = a_f, a_b, b_f, b_b
        alX, alY, dtX, dtY = W["a_log_f"], W["a_log_b"], W["dt_f"], W["dt_b"]
    else:
        aX, aY, bX, bY = a_b, a_f, b_b, b_f
        alX, alY, dtX, dtY = W["a_log_b"], W["a_log_f"], W["dt_b"], W["dt_f"]
    m["w_in"] = np.ascontiguousarray(np.concatenate([cq, ckv, kr, kr, krs, krs, q, k, v, z, aX, aY, bX, bY], axis=1))
    assert m["w_in"].shape[1] == NCOL
    wq = W["w_uq"]
    nope = [wq[:, h * 192:h * 192 + 128] for h in range(4)]
    rope = [wq[:, h * 192 + 128:h * 192 + 192] for h in range(4)]
    ropes = [np.concatenate([r[:, 32:64], r[:, 0:32]], axis=1) for r in rope]
    zz = np.zeros((wq.shape[0], 64), np.float32)
    rope_p = [np.concatenate([r, zz], axis=1) for r in rope]
    ropes_p = [np.concatenate([r, zz], axis=1) for r in ropes]
    m["w_uq"] = np.ascontiguousarray(np.concatenate(nope + rope_p + ropes_p, axis=1))
    wk = W["w_ukv"]
    kn = [wk[:, h * 256:h * 256 + 128] for h in range(4)]
    vv = [wk[:, h * 256 + 128:h * 256 + 256] for h in range(4)]
    m["w_ukv"] = np.ascontiguousarray(np.concatenate(kn + vv, axis=1))
    cw = W["conv_w"]
    if flipped:
        cw = cw[::-1]
    cwa = np.transpose(cw.reshape(5, 12, P), (2, 1, 0)).reshape(P, 60)
    dtrow = np.concatenate([dtX, dtY])
    alrow = np.concatenate([alX, alY])
    sm = np.zeros((P, 134), np.float32)
    sm[:, 0:3] = _arr_pk(W["g_q"])
    sm[:, 3:5] = _arr_pk(W["g_kv"])
    sm[:, 5] = W["g_gdn"]
    sm[:, 6:38] = np.tile(dtrow, 4)[None, :]
    sm[:, 38:70] = np.tile(alrow, 4)[None, :]
    sm[:, 70:130] = cwa
    m["small"] = sm
    m["consts"] = make_consts()
    m["w_out"] = W["w_out"]
    m["w_mlp_in"] = W["w_mlp_in"]
    m["w_mlp_out"] = W["w_mlp_out"]
    return m


_WKEYS = ["w_ada", "b_ada", "g_mix", "w_in", "g_q", "w_uq", "g_kv", "w_ukv", "conv_w", "a_log_f", "a_log_b",
          "dt_f", "dt_b", "g_gdn", "w_out", "g_mlp", "w_mlp_in", "w_mlp_out"]


def _weights(inputs):
    W = {k: np.ascontiguousarray(np.asarray(inputs[k], dtype=np.float32)[0]) for k in _WKEYS}
    for k in ("w_ada_f", "b_ada_f", "g_final"):
        W[k] = np.ascontiguousarray(np.asarray(inputs[k], dtype=np.float32))
    return W


_CACHE = {}


def kernel(**inputs):
    W = _weights(inputs)
    xp = np.asarray(inputs["x_prompt"], dtype=np.float32)
    xs = np.asarray(inputs["x_sample"], dtype=np.float32)
    cp_ = np.asarray(inputs["c_prompt"], dtype=np.float32)
    cs_ = np.asarray(inputs["c_sample"], dtype=np.float32)
    B, SP_, _ = xp.shape
    BS, SS_, _ = xs.shape
    H = SP_ // 2
    assert SS_ == H and 2 * B + BS == 8
    NO = NW = H
    maps = []
    for b in range(B):
        xf = xp[b, ::-1]
        pos = np.arange(SP_ - 1, -1, -1)
        maps.append(prep_core(W, xf[:H], xf[H:], np.ones(H), pos, cp_[b], True, "b"))
        pos = np.arange(SP_)
        maps.append(prep_core(W, xp[b, :H], xp[b, H:], np.ones(H), pos, cp_[b], False, "f"))
    for b in range(BS):
        pos = np.concatenate([np.zeros(H, np.int64), np.arange(H)])
        maps.append(prep_core(W, np.zeros((H, D), np.float32), xs[b], np.zeros(H), pos, cs_[b], False, "f"))
    key = (NO, NW)
    if key not in _CACHE:
        _CACHE[key] = build(NO, NW)[0]
    nc = _CACHE[key]
    res = run_bass_kernel_spmd(nc, maps, core_ids=list(range(8)))
    yp = np.zeros_like(xp)
    ys = np.zeros_like(xs)
    for b in range(B):
        yp[b, :H] = res.results[2 * b]["y"][::-1]
        yp[b, H:] = res.results[2 * b + 1]["y"]
    for b in range(BS):
        ys[b] = res.results[2 * B + b]["y"]
    return yp, ys
```
